# Optimizing a Trainium2 kernel written in Bass

```python
import math
import jax, jax.numpy as jnp
from jax import lax
import numpy as np

D_MODEL = 2048
BATCH = 8
SEQ = 2048
DEPTH = 1

HEAD_DIM = 64
MIX_WIDTH = D_MODEL
DIFF_WIDTH = MIX_WIDTH // 2
DIFF_HEADS = DIFF_WIDTH // (2 * HEAD_DIM)
SWA_WIDTH = MIX_WIDTH - DIFF_WIDTH
SWA_Q_HEADS = SWA_WIDTH // HEAD_DIM
SWA_KV_HEADS = SWA_Q_HEADS // 4
SWA_GROUP = SWA_Q_HEADS // SWA_KV_HEADS
WINDOW = 128
Q_BLOCK = 128
D_FF = -(-8 * D_MODEL // (3 * 256)) * 256
RMS_EPS = 1e-5

DIFF_Q_COLS = DIFF_HEADS * 2 * HEAD_DIM
DIFF_K_COLS = DIFF_HEADS * 2 * HEAD_DIM
DIFF_V_COLS = DIFF_HEADS * 2 * HEAD_DIM
SWA_Q_COLS = SWA_Q_HEADS * HEAD_DIM
SWA_K_COLS = SWA_KV_HEADS * HEAD_DIM
SWA_V_COLS = SWA_KV_HEADS * HEAD_DIM
IN_COLS = DIFF_Q_COLS + DIFF_K_COLS + DIFF_V_COLS + SWA_Q_COLS + SWA_K_COLS + SWA_V_COLS

kernel_name = "hymba_diffattn_swa_sink_alibi_swiglu"


def alibi_slopes(n_heads):
    return np.array([2.0 ** (-8.0 * (h + 1) / n_heads) for h in range(n_heads)], dtype=np.float32)


def lambda_init_fn(layer_idx):
    return 0.8 - 0.6 * math.exp(-0.3 * layer_idx)


def rmsnorm(x, w):
    xf = x.astype(jnp.float32)
    y = xf * lax.rsqrt(jnp.mean(xf * xf, axis=-1, keepdims=True) + RMS_EPS)
    return (y * w.astype(jnp.float32)).astype(x.dtype)


def diff_attention(q, k, v, lq1, lk1, lq2, lk2, subln_w, lambda_init):
    B, S = q.shape[0], q.shape[1]
    nblk = S // Q_BLOCK
    lam = (jnp.exp(jnp.sum(lq1.astype(jnp.float32) * lk1.astype(jnp.float32)))
           - jnp.exp(jnp.sum(lq2.astype(jnp.float32) * lk2.astype(jnp.float32)))
           + lambda_init)
    slopes = jnp.asarray(alibi_slopes(DIFF_HEADS))
    scale = HEAD_DIM ** -0.5
    kpos = jnp.arange(S)
    qb = q.reshape(B, nblk, Q_BLOCK, DIFF_HEADS, 2, HEAD_DIM).transpose(1, 0, 2, 3, 4, 5)

    def one_block(args):
        qblk, n = args
        qpos = n * Q_BLOCK + jnp.arange(Q_BLOCK)
        s = jnp.einsum('bqhcd,bkhcd->bhcqk', qblk, k).astype(jnp.float32) * scale
        dist = (qpos[:, None] - kpos[None, :]).astype(jnp.float32)
        s = s - slopes[None, :, None, None, None] * dist
        s = jnp.where(dist >= 0, s, -jnp.inf)
        p = jax.nn.softmax(s, axis=-1)
        a = p[:, :, 0] - lam * p[:, :, 1]
        return jnp.einsum('bhqk,bkhe->bqhe', a.astype(v.dtype), v)

    o = lax.map(one_block, (qb, jnp.arange(nblk)))
    o = o.transpose(1, 0, 2, 3, 4).reshape(B, S, DIFF_HEADS, 2 * HEAD_DIM)
    o = rmsnorm(o, subln_w) * (1.0 - lambda_init)
    return o.reshape(B, S, DIFF_HEADS * 2 * HEAD_DIM)


def swa_sink_attention(q, k, v, sinks):
    B, S = q.shape[0], q.shape[1]
    nblk = S // WINDOW
    scale = HEAD_DIM ** -0.5
    qb = q.reshape(B, nblk, WINDOW, SWA_KV_HEADS, SWA_GROUP, HEAD_DIM)

    def band(t):
        tp = jnp.pad(t, ((0, 0), (WINDOW, 0), (0, 0), (0, 0)))
        tb = tp.reshape(B, nblk + 1, WINDOW, SWA_KV_HEADS, HEAD_DIM)
        return jnp.concatenate([tb[:, :-1], tb[:, 1:]], axis=2)

    kb, vb = band(k), band(v)
    s = jnp.einsum('bnqhgd,bnkhd->bnhgqk', qb, kb).astype(jnp.float32) * scale
    i = jnp.arange(WINDOW)[:, None]
    j = jnp.arange(2 * WINDOW)[None, :]
    dist = WINDOW + i - j
    blk = jnp.arange(nblk)[:, None, None]
    valid = (dist >= 0) & (dist < WINDOW) & (blk * WINDOW - WINDOW + j >= 0)
    slopes = jnp.asarray(alibi_slopes(SWA_Q_HEADS)).reshape(SWA_KV_HEADS, SWA_GROUP)
    s = s - slopes[:, :, None, None] * dist.astype(jnp.float32)
    s = jnp.where(valid[None, :, None, None], s, -jnp.inf)
    sink = jnp.broadcast_to(
        sinks.astype(jnp.float32).reshape(SWA_KV_HEADS, SWA_GROUP)[None, None, :, :, None, None],
        s.shape[:-1] + (1,))
    p = jax.nn.softmax(jnp.concatenate([s, sink], axis=-1), axis=-1)[..., :-1]
    o = jnp.einsum('bnhgqk,bnkhd->bnqhgd', p.astype(v.dtype), vb)
    return o.reshape(B, S, SWA_Q_HEADS * HEAD_DIM)


def setup_inputs(seed: int = 0) -> dict:
    key = jax.random.key(seed)
    ks = jax.random.split(key, 16)
    f32 = jnp.float32
    nrm = lambda k, shape, s: jax.random.normal(k, shape, f32) * s
    return {
        "x": nrm(ks[0], (BATCH, SEQ, D_MODEL), 1.0),
        "attn_norm_w": 1.0 + nrm(ks[1], (DEPTH, D_MODEL), 0.02),
        "w_in": nrm(ks[2], (DEPTH, D_MODEL, IN_COLS), D_MODEL ** -0.5),
        "lambda_q1": nrm(ks[3], (DEPTH, HEAD_DIM), 0.1),
        "lambda_k1": nrm(ks[4], (DEPTH, HEAD_DIM), 0.1),
        "lambda_q2": nrm(ks[5], (DEPTH, HEAD_DIM), 0.1),
        "lambda_k2": nrm(ks[6], (DEPTH, HEAD_DIM), 0.1),
        "subln_w": 1.0 + nrm(ks[7], (DEPTH, 2 * HEAD_DIM), 0.02),
        "sinks": nrm(ks[8], (DEPTH, SWA_Q_HEADS), 0.5),
        "w_out": nrm(ks[9], (DEPTH, MIX_WIDTH, D_MODEL), MIX_WIDTH ** -0.5),
        "ffn_norm_w": 1.0 + nrm(ks[10], (DEPTH, D_MODEL), 0.02),
        "w_gate": nrm(ks[11], (DEPTH, D_MODEL, D_FF), D_MODEL ** -0.5),
        "w_up": nrm(ks[12], (DEPTH, D_MODEL, D_FF), D_MODEL ** -0.5),
        "w_down": nrm(ks[13], (DEPTH, D_FF, D_MODEL), D_FF ** -0.5),
        "final_norm_w": 1.0 + nrm(ks[14], (D_MODEL,), 0.02),
    }


def reference(x, attn_norm_w, w_in, lambda_q1, lambda_k1, lambda_q2, lambda_k2, subln_w,
              sinks, w_out, ffn_norm_w, w_gate, w_up, w_down, final_norm_w):
    B, S = x.shape[0], x.shape[1]
    splits = np.cumsum([DIFF_Q_COLS, DIFF_K_COLS, DIFF_V_COLS, SWA_Q_COLS, SWA_K_COLS]).tolist()
    for l in range(DEPTH):
        lambda_init = lambda_init_fn(l)
        h = rmsnorm(x, attn_norm_w[l])
        proj = jnp.einsum('bsd,dc->bsc', h, w_in[l])
        qa, ka, va, qs, ksw, vs = jnp.split(proj, splits, axis=-1)
        oa = diff_attention(
            qa.reshape(B, S, DIFF_HEADS, 2, HEAD_DIM),
            ka.reshape(B, S, DIFF_HEADS, 2, HEAD_DIM),
            va.reshape(B, S, DIFF_HEADS, 2 * HEAD_DIM),
            lambda_q1[l], lambda_k1[l], lambda_q2[l], lambda_k2[l], subln_w[l], lambda_init)
        ob = swa_sink_attention(
            qs.reshape(B, S, SWA_Q_HEADS, HEAD_DIM),
            ksw.reshape(B, S, SWA_KV_HEADS, HEAD_DIM),
            vs.reshape(B, S, SWA_KV_HEADS, HEAD_DIM),
            sinks[l])
        mixed = jnp.concatenate([oa, ob], axis=-1)
        x = x + jnp.einsum('bsc,cd->bsd', mixed, w_out[l])
        h = rmsnorm(x, ffn_norm_w[l])
        g = jnp.einsum('bsd,df->bsf', h, w_gate[l])
        u = jnp.einsum('bsd,df->bsf', h, w_up[l])
        x = x + jnp.einsum('bsf,fd->bsd', jax.nn.silu(g) * u, w_down[l])
    return rmsnorm(x, final_norm_w)
```

```python
import math
import numpy as np
import concourse.bass as bass
import concourse.mybir as mybir
from concourse.bass_utils import run_bass_kernel_spmd

F32 = mybir.dt.float32
BF16 = mybir.dt.bfloat16
AF = mybir.ActivationFunctionType
ALU = mybir.AluOpType
AX = mybir.AxisListType

S = 2048
D = 2048
DFF = 5632
NG = 4
GT = 512
EPS = 1e-5
LAMBDA_INIT = 0.8 - 0.6 * math.exp(-0.3 * 0)
SLOPE_D = [2.0 ** (-8.0 * (h + 1) / 8) for h in range(8)]
SLOPE_S = [2.0 ** (-8.0 * (h + 1) / 16) for h in range(16)]
NEG = -30000.0
LOOKAHEAD = 4
DEBUG_STOP = None
DEBUG_DUMP = []
DEBUG_MAXBLK = None


class SemC:
    def __init__(self, nc, name):
        self.h = nc.alloc_semaphore(name)
        self.cnt = 0


class Eng:
    def __init__(self, nc, e, name, inorder_safe=False):
        self.e = e
        self.name = name
        self.sem = SemC(nc, "prog_" + name)
        self.seen = {}
        self.inorder_safe = inorder_safe


class Buf:
    __slots__ = ("name", "w", "r")

    def __init__(self, name):
        self.name = name
        self.w = {}
        self.r = {}


def _merge(d, tok):
    s, v = tok
    if d.get(s, 0) < v:
        d[s] = v


class K:
    def __init__(self, nc):
        self.nc = nc
        self.PE = Eng(nc, nc.tensor, "pe", inorder_safe=True)
        self.ACT = Eng(nc, nc.scalar, "act")
        self.DVE = Eng(nc, nc.vector, "dve")
        self.POOL = Eng(nc, nc.gpsimd, "pool")
        self.SP = Eng(nc, nc.sync, "sp")
        self.nwait = 0

    def wait(self, E, sem, val):
        if sem is E.sem and E.inorder_safe:
            return
        if E.seen.get(sem, 0) >= val:
            return
        E.e.wait_ge(sem.h, val)
        E.seen[sem] = val
        self.nwait += 1

    def op(self, E, fn, reads=(), writes=(), signal=True, dsem=None, waw=True):
        deps = {}
        for b in reads:
            for s, v in b.w.items():
                _merge(deps, (s, v))
        for b in writes:
            for s, v in b.r.items():
                _merge(deps, (s, v))
            if waw or b.r:
                for s, v in b.w.items():
                    _merge(deps, (s, v))
        for s, v in deps.items():
            self.wait(E, s, v)
        ins = fn()
        if dsem is not None:
            dsem.cnt += 16
            ins.then_inc(dsem.h, 16)
            tok = (dsem, dsem.cnt)
        elif signal:
            E.sem.cnt += 1
            ins.then_inc(E.sem.h, 1)
            tok = (E.sem, E.sem.cnt)
        else:
            tok = (E.sem, E.sem.cnt + 1)
        for b in writes:
            if b.r:
                b.r = {}
                b.w = {}
            _merge(b.w, tok)
        for b in reads:
            _merge(b.r, tok)
        return tok


def build_nc():
    nc = bass.Bass("TRN2", target_bir_lowering=False)
    k = K(nc)
    PE, ACT, DVE, POOL, SP = k.PE, k.ACT, k.DVE, k.POOL, k.SP

    def din(name, shape):
        return nc.dram_tensor(name, shape, F32, kind="ExternalInput").ap()

    x = din("x", [S, D])
    w_in = din("w_in", [D, 4608])
    w_out = din("w_out", [D, D])
    w_gate = din("w_gate", [D, DFF])
    w_up = din("w_up", [D, DFF])
    w_down = din("w_down", [DFF, D])
    attn_norm_w = din("attn_norm_w", [D])
    ffn_norm_w = din("ffn_norm_w", [D])
    final_norm_w = din("final_norm_w", [D])
    lq1 = din("lambda_q1", [64])
    lk1 = din("lambda_k1", [64])
    lq2 = din("lambda_q2", [64])
    lk2 = din("lambda_k2", [64])
    subln_w = din("subln_w", [128])
    sinks = din("sinks", [16])
    out = nc.dram_tensor("out", [S, D], F32, kind="ExternalOutput").ap()

    w_in_v = w_in.rearrange("(k p) n -> p k n", p=128)
    w_out_v = w_out.rearrange("(k p) n -> p k n", p=128)
    w_gate_v = w_gate.rearrange("(k p) n -> p k n", p=128)
    w_up_v = w_up.rearrange("(k p) n -> p k n", p=128)
    w_down_v = w_down.rearrange("(k p) n -> p k n", p=128)

    def sb(name, shape, dt):
        return nc.alloc_sbuf_tensor(name, shape, dt)

    xg = sb("xg", [128, 4, 2048], F32)
    hT = sb("hT", [128, 16, 512], BF16)
    big = sb("big", [128, 24, 512], BF16)
    KTd = sb("KTd", [128, 8, 2048], BF16)
    Vd = sb("Vd", [128, 16, 8, 130], BF16)
    KTs = sb("KTs", [128, 4, 640], BF16)
    Vs = sb("Vs", [128, 5, 4, 66], BF16)
    wr = sb("wr", [128, 3, 4096], BF16)
    wbc = sb("wbc", [128, 2048], F32)
    xb = sb("xb", [128, 2048], BF16)
    sg = sb("sg", [128, 4, 512], F32)
    pTs = sb("pTs", [128, 2, 2, 4, 128], BF16)
    pT = pTs[:].rearrange("p a b c q -> p (a b c) q")
    ost = sb("ost", [128, 8, 128], F32)
    obf = sb("obf", [128, 1024], BF16)
    ident = sb("ident", [128, 128], BF16)
    maskC4 = sb("maskC4", [128, 4, 128], BF16)
    maskP4 = sb("maskP4", [128, 4, 128], BF16)
    Tt = sb("Tt", [128, 16], F32)
    Tu = sb("Tu", [128, 16], F32)
    varg = sb("varg", [128, 5, 16], F32)
    qbias = sb("qbias", [128, 5, 16], F32)
    vfac = sb("vfac", [128, 5, 16], F32)
    biasd = sb("biasd", [128, 8, 16], F32)
    biass = sb("biass", [128, 16, 2], F32)
    sinkbc = sb("sinkbc", [128, 16], F32)
    sinkp = sb("sinkp", [128, 16], F32)
    sublnbc = sb("sublnbc", [128, 128], F32)
    lam4 = sb("lam4", [128, 4, 64], F32)
    lsm = sb("lsm", [128, 8], F32)
    nlam = sb("nlam", [128, 1], F32)
    nh = sb("nh", [128, 8], F32)
    epsc = sb("epsc", [128, 1], F32)
    ss = sb("ss", [128, 4], F32)
    vv = sb("vv", [128, 4], F32)
    rstd = sb("rstd", [128, 4], F32)
    ssq = sb("ssq", [128, 8], F32)
    vq = sb("vq", [128, 8], F32)
    rq = sb("rq", [128, 8], F32)
    rr = sb("rr", [128, 4], F32)
    den = sb("den", [128, 2, 4], F32)
    rden = sb("rden", [128, 2, 4], F32)

    ps = nc.alloc_psum_tensor("ps", [128, 8, 512], F32)

    B_xg = [Buf(f"xg{t}") for t in range(4)]
    B_hT = [Buf(f"hT{t}") for t in range(4)]
    B_big = [Buf(f"big{c}") for c in range(24)]
    B_KTd = [[Buf(f"KTd{h}_{g}") for g in range(NG)] for h in range(8)]
    B_Vd = [Buf(f"Vd{t}") for t in range(16)]
    B_KTs = [Buf(f"KTs{t}") for t in range(5)]
    B_Vs = [Buf(f"Vs{t}") for t in range(5)]
    B_wr = [Buf(f"wr{i}") for i in range(3)]
    S_wr = [SemC(nc, f"s_wr{i}") for i in range(3)]
    B_wbc = Buf("wbc")
    S_wbc = SemC(nc, "s_wbc")
    B_xb = Buf("xb")
    B_sg = [Buf(f"sg{i}") for i in range(4)]
    B_pT = [Buf(f"pT{i}") for i in range(16)]
    B_pTs = [Buf("pTs0"), Buf("pTs1")]
    B_ost = Buf("ost")
    B_obf = Buf("obf")
    B_junk = Buf("junk")
    B_const = Buf("const")
    B_stat = Buf("stat")
    B_qstat = Buf("qstat")
    B_den = [Buf("den0"), Buf("den1")]
    B_rden = [Buf("rden0"), Buf("rden1")]
    B_vv, B_rstd = Buf("vv"), Buf("rstd")
    B_bk = [Buf(f"bank{b}") for b in range(8)]
    B_ps = B_bk[0:4]
    S_x = [SemC(nc, f"s_x{t}") for t in range(4)]
    S_o = [SemC(nc, f"s_o{t}") for t in range(4)]
    S_set = SemC(nc, "s_set")
    S_set2 = SemC(nc, "s_set2")
    S_set3 = SemC(nc, "s_set3")

    def setup():
        Bi, Bm, Bt, Bn, Bl, Bsk, Bsu = Buf("i"), Buf("m"), Buf("t"), Buf("n"), Buf("l"), Buf("sk"), Buf("su")
        k.op(POOL, lambda: nc.gpsimd.memset(ident[:], 0.0), writes=[Bi])
        k.op(POOL, lambda: nc.gpsimd.affine_select(out=ident[:], in_=ident[:], compare_op=ALU.not_equal,
                                                   fill=1.0, base=0, pattern=[[-1, 128]], channel_multiplier=1),
             reads=[Bi], writes=[Bi])
        k.op(POOL, lambda: nc.gpsimd.memset(maskC4[:], 1.0), writes=[Bm])
        k.op(POOL, lambda: nc.gpsimd.affine_select(out=maskC4[:], in_=maskC4[:], compare_op=ALU.is_ge,
                                                   fill=0.0, base=0, pattern=[[0, 4], [1, 128]],
                                                   channel_multiplier=-1), reads=[Bm], writes=[Bm])
        k.op(POOL, lambda: nc.gpsimd.memset(maskP4[:], 1.0), writes=[Bn])
        k.op(POOL, lambda: nc.gpsimd.affine_select(out=maskP4[:], in_=maskP4[:], compare_op=ALU.is_ge,
                                                   fill=0.0, base=-1, pattern=[[0, 4], [-1, 128]],
                                                   channel_multiplier=1), reads=[Bn], writes=[Bn])
        k.op(POOL, lambda: nc.gpsimd.iota(Tt[:], pattern=[[-128, 16]], base=-64, channel_multiplier=1, allow_small_or_imprecise_dtypes=True),
             writes=[Bt])
        k.op(POOL, lambda: nc.gpsimd.memset(nh[:], -0.5), writes=[Buf("x")])
        k.op(POOL, lambda: nc.gpsimd.memset(epsc[:], EPS), writes=[Buf("x")])
        Bvones = Buf("vones")
        k.op(POOL, lambda: nc.gpsimd.memset(Vd[:, :, :, 128:130], 1.0), writes=[Bvones])
        k.op(POOL, lambda: nc.gpsimd.memset(Vs[:, :, :, 64:66], 1.0), writes=[Buf("x")])
        for i, v in enumerate([lq1, lk1, lq2, lk2]):
            k.op(SP, lambda: nc.sync.dma_start(out=lam4[:, i, :], in_=v.partition_broadcast(128)),
                 writes=[Bl], dsem=S_set, waw=False)
        k.op(SP, lambda: nc.sync.dma_start(out=sinkbc[:], in_=sinks.partition_broadcast(128)),
             writes=[Bsk], dsem=S_set2, waw=False)
        k.op(SP, lambda: nc.sync.dma_start(out=sublnbc[:], in_=subln_w.partition_broadcast(128)),
             writes=[Bsu], dsem=S_set3, waw=False)
        Bu, Bva, Bvf = Buf("u"), Buf("va"), Buf("vf")
        k.op(POOL, lambda: nc.gpsimd.iota(Tu[:], pattern=[[128, 16]], base=-1024, channel_multiplier=1,
                                          allow_small_or_imprecise_dtypes=True), writes=[Bu])
        for hh in range(5):
            k.op(DVE, lambda: nc.vector.tensor_scalar(out=varg[:, hh, :], in0=Tu[:], scalar1=SLOPE_D[3 + hh], scalar2=None,
                                                      op0=ALU.mult), reads=[Bu], writes=[Bva], waw=False)
        k.op(ACT, lambda: nc.scalar.activation(out=vfac[:], in_=varg[:], func=AF.Exp), reads=[Bva], writes=[Bvf])
        Bq = Buf("q")
        k.op(POOL, lambda: nc.gpsimd.iota(qbias[:, 0, :], pattern=[[128, 16]], base=64 - 1024, channel_multiplier=0,
                                          allow_small_or_imprecise_dtypes=True), writes=[Bq])
        for hh in range(1, 5):
            k.op(POOL, lambda: nc.gpsimd.tensor_scalar(out=qbias[:, hh, :], in0=qbias[:, 0, :], scalar1=-SLOPE_D[3 + hh],
                                                       scalar2=None, op0=ALU.mult), reads=[Bq], writes=[Buf("x")])
        k.op(POOL, lambda: nc.gpsimd.tensor_scalar(out=qbias[:, 0, :], in0=qbias[:, 0, :], scalar1=-SLOPE_D[3],
                                                   scalar2=None, op0=ALU.mult), reads=[Bq], writes=[Bq])
        for hh in range(5):
            k.op(DVE, lambda: nc.vector.tensor_copy(out=Vd[:, :, 3 + hh, 128:130],
                                                    in_=vfac[:, hh, :].unsqueeze(2).broadcast_to([128, 16, 2])),
                 reads=[Bvf, Bvones], writes=[Buf("x")])
        for h in range(8):
            k.op(DVE, lambda: nc.vector.tensor_scalar(out=biasd[:, h, :], in0=Tt[:], scalar1=SLOPE_D[h],
                                                      scalar2=None, op0=ALU.mult), reads=[Bt], writes=[Buf("x")])
        for h in range(16):
            k.op(DVE, lambda: nc.vector.tensor_scalar(out=biass[:, h, :], in0=Tt[:, 0:2], scalar1=SLOPE_S[h],
                                                      scalar2=None, op0=ALU.mult), reads=[Bt], writes=[Buf("x")])
        B_l2, B_l3, B_l4, B_l5 = Buf("l2"), Buf("l3"), Buf("l4"), Buf("l5")
        k.op(DVE, lambda: nc.vector.tensor_tensor(out=lam4[:, 0, :], in0=lam4[:, 0, :], in1=lam4[:, 1, :], op=ALU.mult),
             reads=[Bl], writes=[B_l2], waw=False)
        k.op(DVE, lambda: nc.vector.tensor_tensor(out=lam4[:, 2, :], in0=lam4[:, 2, :], in1=lam4[:, 3, :], op=ALU.mult),
             reads=[Bl], writes=[B_l2], waw=False)
        k.op(DVE, lambda: nc.vector.tensor_reduce(out=lsm[:, 0:1], in_=lam4[:, 0, :], axis=AX.X, op=ALU.add),
             reads=[B_l2], writes=[B_l3], waw=False)
        k.op(DVE, lambda: nc.vector.tensor_reduce(out=lsm[:, 1:2], in_=lam4[:, 2, :], axis=AX.X, op=ALU.add),
             reads=[B_l2], writes=[B_l3], waw=False)
        k.op(ACT, lambda: nc.scalar.activation(out=lsm[:, 2:4], in_=lsm[:, 0:2], func=AF.Exp),
             reads=[B_l3], writes=[B_l4])
        k.op(DVE, lambda: nc.vector.tensor_tensor(out=lsm[:, 4:5], in0=lsm[:, 3:4], in1=lsm[:, 2:3], op=ALU.subtract),
             reads=[B_l4], writes=[B_l5])
        k.op(DVE, lambda: nc.vector.tensor_scalar(out=nlam[:], in0=lsm[:, 4:5], scalar1=-LAMBDA_INIT, scalar2=None,
                                                  op0=ALU.add), reads=[B_l5], writes=[Buf("x")])
        k.op(DVE, lambda: nc.vector.tensor_scalar(out=sublnbc[:], in0=sublnbc[:], scalar1=1.0 - LAMBDA_INIT,
                                                  scalar2=None, op0=ALU.mult), reads=[Bsu], writes=[Bsu])
        for h in range(16):
            k.op(ACT, lambda: nc.scalar.activation(out=sinkp[:, h:h + 1], in_=Tt[:, 0:1], func=AF.Exp,
                                                   scale=SLOPE_S[h], bias=sinkbc[:, h:h + 1]),
                 reads=[Bt, Bsk], writes=[Buf("x")])
        engs = [PE, ACT, DVE, POOL, SP]
        for E in engs:
            for Fe in engs:
                if Fe is not E and Fe.sem.cnt > 0:
                    k.wait(E, Fe.sem, Fe.sem.cnt)
            for ss_ in (S_set, S_set2, S_set3):
                k.wait(E, ss_, ss_.cnt)

    ring_state = {"i": 0, "g": 0, "pid": 0}
    NPIECE = 95
    wcache = nc.dram_tensor("wcache", [NPIECE, 128, 4096], BF16, kind="Internal").ap()
    B_wc = [Buf(f"wc{i}") for i in range(NPIECE)]
    S_ws = [SemC(nc, f"s_ws{i}") for i in range(3)]
    S_wrh = [SemC(nc, f"s_wrh{i}") for i in range(3)]

    def cache_group(pid):
        return 0 if pid % 3 == 0 else 1

    def ring_next():
        i = ring_state["i"] % 3
        ring_state["i"] += 1
        pid = ring_state["pid"]
        ring_state["pid"] += 1
        return i, pid

    def cache_store(i, pid, nel):
        k.op(SP, lambda: nc.sync.dma_start(out=wcache[pid, :, 0:nel], in_=wr[:, i, 0:nel]),
             reads=[B_wr[i]], writes=[B_wc[pid]], dsem=S_ws[i])

    def cache_load(i, pid, nel):
        k.op(SP, lambda: nc.sync.dma_start(out=wr[:, i, 0:nel], in_=wcache[pid, :, 0:nel]),
             reads=[B_wc[pid]], writes=[B_wr[i]], dsem=S_wrh[i])

    def load_piece(src_view, k0, k1, c0, c1):
        i, pid = ring_next()
        kk = k1 - k0
        n = c1 - c0
        assert kk * n <= 4096
        dst = wr[:, i, 0:kk * n].rearrange("p (k n) -> p k n", k=kk)
        gfill = cache_group(pid)
        if ring_state["g"] <= gfill:
            k.op(POOL, lambda: nc.gpsimd.dma_start(out=dst, in_=src_view[:, k0:k1, c0:c1]),
                 writes=[B_wr[i]], dsem=S_wr[i])
            if ring_state["g"] == gfill:
                cache_store(i, pid, kk * n)
        else:
            cache_load(i, pid, kk * n)
        return dst, B_wr[i]

    bank_state = {"i": 0}

    def next_bank():
        b = bank_state["i"] % 4
        bank_state["i"] += 1
        return b

    evac_state = {"i": 0}

    def evac_copy(out_ap, in_ap, reads, writes, waw=True, eng=None):
        evac_state["i"] += 1
        if (evac_state["i"] % 2 == 0 and eng is None) or eng == "act":
            return k.op(ACT, lambda: nc.scalar.activation(out=out_ap, in_=in_ap, func=AF.Copy),
                        reads=reads, writes=writes, waw=waw)
        return k.op(DVE, lambda: nc.vector.tensor_copy(out=out_ap, in_=in_ap), reads=reads, writes=writes, waw=waw)

    def load_wbc(vec):
        k.op(SP, lambda: nc.sync.dma_start(out=wbc[:], in_=vec.partition_broadcast(128)), writes=[B_wbc], dsem=S_wbc)

    B_statt = [Buf(f"stat{t}") for t in range(4)]
    B_vvt = [Buf(f"vv{t}") for t in range(4)]
    B_rstdt = [Buf(f"rstd{t}") for t in range(4)]
    B_xbh = [Buf("xbh0"), Buf("xbh1")]

    def stats_tile(t):
        k.op(ACT, lambda: nc.scalar.activation(out=xb[:], in_=xg[:, t, :], func=AF.Square, accum_out=ss[:, t:t + 1]),
             reads=[B_xg[t]], writes=B_xbh + [B_statt[t]])
        k.op(ACT, lambda: nc.scalar.activation(out=vv[:, t:t + 1], in_=ss[:, t:t + 1], func=AF.Sqrt, scale=1.0 / D,
                                               bias=epsc[:]), reads=[B_statt[t]], writes=[B_vvt[t]])
        k.op(DVE, lambda: nc.vector.reciprocal(out=rstd[:, t:t + 1], in_=vv[:, t:t + 1]),
             reads=[B_vvt[t]], writes=[B_rstdt[t]])

    def norm_to_hT():
        for t in range(4):
            stats_tile(t)
        for t in range(4):
            for r in range(2):
                cs = slice(r * 1024, (r + 1) * 1024)
                k.op(DVE, lambda: nc.vector.scalar_tensor_tensor(out=xb[:, cs], in0=xg[:, t, cs], scalar=rstd[:, t:t + 1],
                                                                 in1=wbc[:, cs], op0=ALU.mult, op1=ALU.mult),
                     reads=[B_xg[t], B_rstdt[t], B_wbc], writes=[B_xbh[r]])
                b = next_bank()
                tp = ps[:, b, :].bitcast(BF16).rearrange("p (c n) -> p c n", c=8)
                for c in range(8):
                    dc = r * 8 + c
                    k.op(PE, lambda: nc.tensor.transpose(out=tp[:, c, :], in_=xb[:, dc * 128:(dc + 1) * 128],
                                                         identity=ident[:]),
                         reads=[B_xbh[r]], writes=[B_ps[b]], signal=(c == 7))
                evac_copy(hT[:, r * 8:(r + 1) * 8, t * 128:(t + 1) * 128], tp, reads=[B_ps[b]], writes=[B_hT[t]], waw=False)

    ss2 = sb("ss2", [128, 4], F32)
    vv2 = sb("vv2", [128, 4], F32)
    rstd2 = sb("rstd2", [128, 4], F32)
    B_e1 = [Buf(f"e1_{t}") for t in range(4)]
    B_e2 = [Buf(f"e2_{t}") for t in range(4)]
    B_e3 = [Buf(f"e3_{t}") for t in range(4)]
    S_xe = SemC(nc, "s_xe")
    en_state = {"i": 0}

    def early_norm_chain(gn, t):
        sgx = sg[:].rearrange("p a n -> p (a n)")
        junk16 = ost[:].rearrange("p a e -> p (a e)").bitcast(BF16)
        row0 = (gn * 4 + t) * 128
        k.op(ACT, lambda: nc.scalar.dma_start(out=sgx, in_=x[row0:row0 + 128, :]), writes=B_sg, dsem=S_xe)
        k.op(ACT, lambda: nc.scalar.activation(out=junk16, in_=sgx, func=AF.Square, accum_out=ss2[:, t:t + 1]),
             reads=B_sg, writes=B_osth[0] + [B_e1[t]])
        k.op(ACT, lambda: nc.scalar.activation(out=vv2[:, t:t + 1], in_=ss2[:, t:t + 1], func=AF.Sqrt, scale=1.0 / D,
                                               bias=epsc[:]), reads=[B_e1[t]], writes=[B_e2[t]])
        k.op(DVE, lambda: nc.vector.reciprocal(out=rstd2[:, t:t + 1], in_=vv2[:, t:t + 1]),
             reads=[B_e2[t]], writes=[B_e3[t]])
        for r in range(2):
            cs = slice(r * 1024, (r + 1) * 1024)
            k.op(DVE, lambda: nc.vector.scalar_tensor_tensor(out=xb[:, cs], in0=sgx[:, cs], scalar=rstd2[:, t:t + 1],
                                                             in1=wbc[:, cs], op0=ALU.mult, op1=ALU.mult),
                 reads=B_sg + [B_e3[t], B_wbc], writes=[B_xbh[r]])

    def early_norm_transposes(t):
        for r in range(2):
            b = 4 + en_state["i"] % 2
            en_state["i"] += 1
            tp = ps[:, b, :].bitcast(BF16).rearrange("p (c n) -> p c n", c=8)
            for c in range(8):
                dc = r * 8 + c
                k.op(PE, lambda: nc.tensor.transpose(out=tp[:, c, :], in_=xb[:, dc * 128:(dc + 1) * 128], identity=ident[:]),
                     reads=[B_xbh[r]], writes=[B_bk[b]], signal=(c == 7))
            evac_copy(hT[:, r * 8:(r + 1) * 8, t * 128:(t + 1) * 128], tp, reads=[B_bk[b]], writes=[B_hT[t]], waw=False)

    def proj_feature_major(src_view, col0, nchunks, dest_fn, dest_bufs_fn):
        for s0 in range(0, nchunks, 4):
            nch = min(4, nchunks - s0)
            ncols = nch * 128
            kper = 4096 // ncols
            for k0 in range(0, 16, kper):
                piece, bw = load_piece(src_view, k0, k0 + kper, col0 + s0 * 128, col0 + s0 * 128 + ncols)
                for c in range(nch):
                    for kk in range(kper):
                        dc = k0 + kk
                        k.op(PE, lambda: nc.tensor.matmul(
                            out=ps[:, c, :], lhsT=piece[:, kk, c * 128:(c + 1) * 128], rhs=hT[:, dc, :],
                            start=(dc == 0), stop=(dc == 15)),
                             reads=[bw] + B_hT, writes=[B_ps[c]], signal=(kk == kper - 1))
            for c in range(nch):
                evac_copy(dest_fn(s0 + c), ps[:, c, :], reads=[B_ps[c]], writes=dest_bufs_fn(s0 + c))

    def proj_token_major(src_view, kchunks, col0, ncols, lhs_fn, lhs_bufs_fn, evac_fn):
        kper = 4096 // ncols
        pieces = [(a, min(a + kper, kchunks)) for a in range(0, kchunks, kper)]
        for (k0, k1) in pieces:
            piece, bw = load_piece(src_view, k0, k1, col0, col0 + ncols)
            for t in range(4):
                for kk in range(k0, k1):
                    last = (kk == kchunks - 1)
                    k.op(PE, lambda t=t, kk=kk, k0=k0, piece=piece: nc.tensor.matmul(
                        out=ps[:, t, 0:ncols], lhsT=lhs_fn(kk, t), rhs=piece[:, kk - k0, :],
                        start=(kk == 0), stop=(kk == kchunks - 1)),
                         reads=[bw] + lhs_bufs_fn(kk, t), writes=[B_ps[t]], signal=(kk == k1 - 1))
        for t in range(4):
            evac_fn(t)

    pending_io = []

    def group(g):
        T0 = g * 4
        ring_state["g"] = g
        ring_state["pid"] = 0
        def load_x():
            for t in range(4):
                k.op(SP, lambda t=t: nc.sync.dma_start(out=xg[:, t, :], in_=x[(T0 + t) * 128:(T0 + t + 1) * 128, :]),
                     writes=[B_xg[t]], dsem=S_x[t])

        if g == 0:
            load_x()
            load_wbc(attn_norm_w)
            norm_to_hT()
            load_wbc(ffn_norm_w)

        proj_feature_major(w_in_v, 0, 8, lambda c: big[:, c, :], lambda c: [B_big[c]])
        proj_feature_major(w_in_v, 1024, 8, lambda c: KTd[:, c, g * GT:(g + 1) * GT], lambda c: [B_KTd[c][g]])
        def evac_v(t, vb):
            tile_ = T0 + t
            pv = ps[:, t, :].rearrange("p (h e) -> p h e", h=4)
            if vb == 0:
                k.op(DVE, lambda: nc.vector.tensor_copy(out=Vd[:, tile_, 0:3, 0:128], in_=pv[:, 0:3, :]),
                     reads=[B_ps[t]], writes=[B_Vd[tile_]], waw=False)
                k.op(DVE, lambda: nc.vector.tensor_scalar(out=Vd[:, tile_, 3, 0:128], in0=pv[:, 3, :],
                                                          scalar1=vfac[:, 0, tile_:tile_ + 1], scalar2=None, op0=ALU.mult),
                     reads=[B_ps[t]], writes=[B_Vd[tile_]], waw=False)
            else:
                k.op(DVE, lambda: nc.vector.tensor_tensor(
                    out=Vd[:, tile_, 4:8, 0:128], in0=pv,
                    in1=vfac[:, 1:5, tile_].unsqueeze(2).broadcast_to([128, 4, 128]), op=ALU.mult),
                     reads=[B_ps[t]], writes=[B_Vd[tile_]], waw=False)

        for vb in range(2):
            proj_token_major(
                w_in_v, 16, 2048 + vb * 512, 512,
                lambda kk, t: hT[:, kk, t * 128:(t + 1) * 128], lambda kk, t: [B_hT[t]],
                lambda t, vb=vb: evac_v(t, vb))
        if DEBUG_STOP == "qkv":
            return
        if g > 0:
            for f_ in pending_io:
                f_()
            del pending_io[:]
            load_x()
            load_wbc(ffn_norm_w)
        diff_attention(g)
        if DEBUG_STOP == "dattn":
            return
        if g > 0:
            k.op(DVE, lambda: nc.vector.tensor_copy(out=KTs[:, :, 0:128], in_=KTs[:, :, 512:640]),
                 reads=[B_KTs[4]], writes=[B_KTs[0]])
            k.op(DVE, lambda: nc.vector.tensor_copy(out=Vs[:, 0, :, :], in_=Vs[:, 4, :, :]),
                 reads=[B_Vs[4]], writes=[B_Vs[0]])
        proj_feature_major(w_in_v, 3072, 8, lambda c: big[:, c, :], lambda c: [B_big[c]])
        for kh in range(2):
            i, pid = ring_next()
            dst = wr[:, i, :].rearrange("p (k h u e) -> p k h u e", k=8, h=4, u=2)
            srcv = w_in_v[:, kh * 8:kh * 8 + 8, 4096:4352].rearrange("p k (h e) -> p k h e", h=4)
            if g <= cache_group(pid):
                for u in range(2):
                    for hh in range(4):
                        k.op(POOL, lambda: nc.gpsimd.dma_start(out=dst[:, :, hh, u, :], in_=srcv[:, :, hh, :]),
                             writes=[B_wr[i]], dsem=S_wr[i], waw=(u == 0 and hh == 0))
                if g == cache_group(pid):
                    cache_store(i, pid, 4096)
            else:
                cache_load(i, pid, 4096)
            piece = wr[:, i, :].rearrange("p (k n) -> p k n", k=8)
            for kvh in range(4):
                for kk in range(8):
                    dc = kh * 8 + kk
                    k.op(PE, lambda: nc.tensor.matmul(out=ps[:, kvh, :], lhsT=piece[:, kk, kvh * 128:(kvh + 1) * 128],
                                                      rhs=hT[:, dc, :], start=(dc == 0), stop=(dc == 15)),
                         reads=[B_wr[i]] + B_hT, writes=[B_ps[kvh]], signal=(kk == 7))
        for kvh in range(4):
            evac_copy(KTs[:, kvh, 128:640], ps[:, kvh, :], reads=[B_ps[kvh]], writes=B_KTs[1:5], waw=False)
        proj_token_major(
            w_in_v, 16, 4352, 256,
            lambda kk, t: hT[:, kk, t * 128:(t + 1) * 128], lambda kk, t: [B_hT[t]],
            lambda t: evac_copy(Vs[:, 1 + t, :, 0:64], ps[:, t, 0:256].rearrange("p (h e) -> p h e", h=4),
                                reads=[B_ps[t]], writes=[B_Vs[1 + t]], waw=False))
        if DEBUG_STOP == "sproj":
            return
        swa_attention(g)
        if DEBUG_STOP == "attn":
            return
        for db in range(4):
            proj_token_major(
                w_out_v, 16, db * 512, 512,
                lambda kk, t: big[:, 8 + kk, t * 128:(t + 1) * 128], lambda kk, t: [B_big[8 + kk]],
                lambda t, db=db: k.op(DVE, lambda: nc.vector.tensor_tensor(
                    out=xg[:, t, db * 512:(db + 1) * 512], in0=ps[:, t, :], in1=xg[:, t, db * 512:(db + 1) * 512],
                    op=ALU.add), reads=[B_ps[t], B_xg[t]], writes=[B_xg[t]]))
        if DEBUG_STOP == "oproj":
            return
        norm_to_hT()
        for fh in range(2):
            f0 = fh * 22
            for s0 in range(0, 22, 4):
                nch = min(4, 22 - s0)
                ncols = nch * 128
                kper = 4096 // ncols
                c0 = (f0 + s0) * 128
                for (wv, bank0) in ((w_gate_v, 0), (w_up_v, 4)):
                    for k0 in range(0, 16, kper):
                        piece, bw = load_piece(wv, k0, k0 + kper, c0, c0 + ncols)
                        for c in range(nch):
                            for kk in range(kper):
                                dc = k0 + kk
                                k.op(PE, lambda: nc.tensor.matmul(
                                    out=ps[:, bank0 + c, :], lhsT=piece[:, kk, c * 128:(c + 1) * 128], rhs=hT[:, dc, :],
                                    start=(dc == 0), stop=(dc == 15)),
                                     reads=[bw] + B_hT, writes=[B_bk[bank0 + c]], signal=(kk == kper - 1))
                    if bank0 == 0:
                        for c in range(nch):
                            k.op(ACT, lambda: nc.scalar.activation(out=sg[:, c, :], in_=ps[:, c, :], func=AF.Silu),
                                 reads=[B_bk[c]], writes=[B_sg[c]])
                for c in range(nch):
                    k.op(DVE, lambda: nc.vector.tensor_tensor(out=big[:, s0 + c, :], in0=sg[:, c, :], in1=ps[:, 4 + c, :],
                                                              op=ALU.mult),
                         reads=[B_sg[c], B_bk[4 + c]], writes=[B_big[s0 + c]])
            early = (fh == 1 and g + 1 < NG)
            if early:
                load_wbc(attn_norm_w)
            for db in range(4):
                if early:
                    if db > 0:
                        early_norm_transposes(db - 1)
                    early_norm_chain(g + 1, db)
                proj_token_major(
                    w_down_v[:, f0:f0 + 22, :], 22, db * 512, 512,
                    lambda kk, t: big[:, kk, t * 128:(t + 1) * 128], lambda kk, t: [B_big[kk]],
                    lambda t, db=db: k.op(DVE, lambda: nc.vector.tensor_tensor(
                        out=xg[:, t, db * 512:(db + 1) * 512], in0=ps[:, t, :], in1=xg[:, t, db * 512:(db + 1) * 512],
                        op=ALU.add), reads=[B_ps[t], B_xg[t]], writes=[B_xg[t]]))
        if g + 1 < NG:
            early_norm_transposes(3)
        load_wbc(final_norm_w)
        for t in range(4):
            stats_tile(t)
        for t in range(4):
            k.op(DVE, lambda t=t: nc.vector.scalar_tensor_tensor(out=xg[:, t, :], in0=xg[:, t, :], scalar=rstd[:, t:t + 1],
                                                                 in1=wbc[:], op0=ALU.mult, op1=ALU.mult),
                 reads=[B_rstdt[t], B_wbc], writes=[B_xg[t]])
            pending_io.append(lambda t=t: k.op(
                SP, lambda: nc.sync.dma_start(out=out[(T0 + t) * 128:(T0 + t + 1) * 128, :], in_=xg[:, t, :]),
                reads=[B_xg[t]], dsem=S_o[t]))
        if g == NG - 1:
            for f_ in pending_io:
                f_()
            del pending_io[:]

    B_rr, B_rr2 = Buf("rr"), Buf("rr2")
    B_osth = [[Buf(f"ost{q}_{h}") for h in range(8)] for q in range(2)]
    B_ssq, B_vq, B_rq = Buf("ssq"), Buf("vq"), Buf("rq")
    tp_state = {"i": 0}

    def transpose_out(qt, chunk0):
        b = 2 + tp_state["i"] % 2
        tp_state["i"] += 1
        tp = ps[:, b, :].bitcast(BF16).rearrange("p (c n) -> p c n", c=8)
        for c in range(8):
            k.op(PE, lambda: nc.tensor.transpose(out=tp[:, c, :], in_=obf[:, c * 128:(c + 1) * 128], identity=ident[:]),
                 reads=[B_obf], writes=[B_ps[b]], signal=(c == 7))
        evac_copy(big[:, chunk0:chunk0 + 8, qt * 128:(qt + 1) * 128], tp, reads=[B_ps[b]],
                  writes=B_big[chunk0:chunk0 + 8], waw=False, eng="dve")

    def diff_attention(g):
        T0 = g * 4
        items = []
        for qt in range(4):
            for h in range(8):
                T = T0 + qt
                kmin = 0
                while kmin < T and SLOPE_D[h] * (128 * (T - kmin - 1) + 1) > 124.0:
                    kmin += 1
                for k0 in range(kmin, T + 1, 4):
                    items.append((qt, h, k0, min(k0 + 4, T + 1), kmin))
        if DEBUG_MAXBLK is not None:
            items = items[:DEBUG_MAXBLK]
        st = {"p": 0, "u": -1}
        ostv = [ost[:], sg[:, 2:4, :].rearrange("p a (b e) -> p (a b) e", e=128)]
        for i_ in range(16):
            for pb_ in B_pTs:
                for s_, v_ in list(pb_.r.items()) + list(pb_.w.items()):
                    _merge(B_pT[i_].r, (s_, v_))
        for h_ in range(8):
            for sgb in B_sg[2:4]:
                for s_, v_ in list(sgb.r.items()) + list(sgb.w.items()):
                    _merge(B_osth[1][h_].r, (s_, v_))

        def emit_scores(idx):
            qt, h, k0, k1, kmin = items[idx]
            p = idx % 2
            for kt in range(k0, k1):
                for m in range(2):
                    bk = 4 + 2 * p + m
                    j = kt - k0
                    k.op(PE, lambda: nc.tensor.matmul(out=ps[:, bk, j * 128:(j + 1) * 128],
                                                      lhsT=KTd[m * 64:(m + 1) * 64, h, kt * 128:(kt + 1) * 128],
                                                      rhs=big[m * 64:(m + 1) * 64, h, qt * 128:(qt + 1) * 128],
                                                      start=True, stop=True),
                         reads=[B_KTd[h][kt // 4], B_big[h]], writes=[B_bk[bk]], signal=True)

        def finish_unit(qt, h, ob):
            o1 = ps[:, ob, 0:130]
            o2 = ps[:, ob, 130:260]
            ov = ostv[qt % 2][:, h, :]
            bo = B_osth[qt % 2][h]
            k.op(DVE, lambda: nc.vector.reciprocal(out=rr[:, 0:1], in_=o1[:, 128:129]), reads=[B_bk[ob]], writes=[B_rr])
            k.op(DVE, lambda: nc.vector.reciprocal(out=rr[:, 1:2], in_=o2[:, 128:129]), reads=[B_bk[ob]], writes=[B_rr],
                 waw=False)
            k.op(DVE, lambda: nc.vector.tensor_tensor(out=rr[:, 2:3], in0=rr[:, 1:2], in1=nlam[:], op=ALU.mult),
                 reads=[B_rr], writes=[B_rr2])
            k.op(DVE, lambda: nc.vector.tensor_scalar(out=ov, in0=o1[:, 0:128], scalar1=rr[:, 0:1], scalar2=None,
                                                      op0=ALU.mult), reads=[B_bk[ob], B_rr], writes=[bo])
            k.op(DVE, lambda: nc.vector.scalar_tensor_tensor(out=ov, in0=o2[:, 0:128], scalar=rr[:, 2:3],
                                                             in1=ov, op0=ALU.mult, op1=ALU.add),
                 reads=[B_bk[ob], B_rr2, bo], writes=[bo])

        def finish_qtile(qt):
            ov = ostv[qt % 2]
            bos = B_osth[qt % 2]
            sqv = sg[:, 0:2, :].rearrange("p a (b e) -> p (a b) e", e=128)
            k.op(DVE, lambda: nc.vector.tensor_tensor(out=sqv, in0=ov, in1=ov, op=ALU.mult),
                 reads=bos, writes=B_sg[0:2])
            k.op(DVE, lambda: nc.vector.tensor_reduce(out=ssq[:], in_=sqv, axis=AX.X, op=ALU.add),
                 reads=B_sg[0:2], writes=[B_ssq])
            k.op(DVE, lambda: nc.vector.tensor_scalar(out=vq[:], in0=ssq[:], scalar1=1.0 / 128, scalar2=EPS,
                                                      op0=ALU.mult, op1=ALU.add), reads=[B_ssq], writes=[B_vq])
            k.op(POOL, lambda: nc.gpsimd.tensor_tensor(out=rq[:], in0=vq[:], in1=nh[:], op=ALU.pow),
                 reads=[B_vq], writes=[B_rq])
            k.op(POOL, lambda: nc.gpsimd.tensor_tensor(out=ov, in0=ov,
                                                       in1=rq[:].unsqueeze(2).broadcast_to([128, 8, 128]), op=ALU.mult),
                 reads=[B_rq] + bos, writes=bos)
            k.op(POOL, lambda: nc.gpsimd.tensor_tensor(out=obf[:].rearrange("p (h e) -> p h e", h=8), in0=ov,
                                                       in1=sublnbc[:].unsqueeze(1).broadcast_to([128, 8, 128]),
                                                       op=ALU.mult),
                 reads=bos, writes=[B_obf])
            deferred.append([6, lambda: transpose_out(qt, 8)])

        def emit_pv(idx):
            qt, h, k0, k1, kmin = items[idx]
            T = T0 + qt
            p = idx % 2
            if k0 == kmin:
                st["u"] += 1
            ob = st["u"] % 2
            wide = h >= 3
            wbase = {}
            if wide:
                nkt = k1 - k0
                for m in range(2):
                    bk = 4 + 2 * p + m
                    if st["p"] % 4:
                        st["p"] += 4 - st["p"] % 4
                    base = st["p"] % 16
                    st["p"] += 4
                    wbase[m] = base
                    k.op(ACT, lambda: nc.scalar.activation(
                        out=pT[:, base:base + nkt, :], in_=ps[:, bk, 0:nkt * 128].rearrange("p (j q) -> p j q", q=128),
                        func=AF.Exp, scale=0.125, bias=qbias[:, h - 3, T:T + 1]),
                         reads=[B_bk[bk]], writes=B_pT[base:base + nkt])
            for kt in range(k0, k1):
                for m in range(2):
                    bk = 4 + 2 * p + m
                    j = kt - k0
                    if wide:
                        pi = wbase[m] + j
                    else:
                        pi = st["p"] % 16
                        st["p"] += 1
                        k.op(ACT, lambda: nc.scalar.activation(out=pT[:, pi, :], in_=ps[:, bk, j * 128:(j + 1) * 128],
                                                               func=AF.Exp, scale=0.125,
                                                               bias=biasd[:, h, T - kt:T - kt + 1]),
                             reads=[B_bk[bk]], writes=[B_pT[pi]])
                    if kt == T:
                        k.op(DVE, lambda: nc.vector.tensor_tensor(out=pT[:, pi, :], in0=pT[:, pi, :], in1=maskC4[:, 0, :],
                                                                  op=ALU.mult), reads=[B_pT[pi]], writes=[B_pT[pi]])
                    k.op(PE, lambda: nc.tensor.matmul(out=ps[:, ob, m * 130:(m + 1) * 130], lhsT=pT[:, pi, :],
                                                      rhs=Vd[:, kt, h, :], start=(kt == kmin and m == 0), stop=(kt == T),
                                                      skip_group_check=True),
                         reads=[B_pT[pi], B_Vd[kt]], writes=[B_bk[ob]], signal=True)
            if k1 == T + 1:
                finish_unit(qt, h, ob)
                if h == 7:
                    finish_qtile(qt)

        n = len(items)
        deferred = []
        if n:
            emit_scores(0)
        for i in range(n):
            if i + 1 < n:
                emit_scores(i + 1)
            emit_pv(i)
            for d_ in list(deferred):
                d_[0] -= 1
                if d_[0] <= 0:
                    deferred.remove(d_)
                    d_[1]()
        for d_ in deferred:
            d_[1]()
        for i_ in range(16):
            for s_, v_ in list(B_pT[i_].r.items()) + list(B_pT[i_].w.items()):
                for pb_ in B_pTs:
                    _merge(pb_.r, (s_, v_))
        for h_ in range(8):
            for s_, v_ in list(B_osth[1][h_].r.items()) + list(B_osth[1][h_].w.items()):
                for sgb in B_sg[2:4]:
                    _merge(sgb.r, (s_, v_))

    HMAP = [0, 2, 1, 3]

    def swa_attention(g):
        T0 = g * 4
        items = [(qt, kvh) for qt in range(4) for kvh in range(4)]

        def sreg(p, cur, half):
            return ps[:, 4 + 2 * p + half, cur * 256:(cur + 1) * 256]

        def emit_scores(idx):
            qt, kvh = items[idx]
            T = T0 + qt
            p = idx % 2
            for cur in ([0, 1] if T >= 1 else [1]):
                kcols = (qt + cur) * 128
                for half in range(2):
                    r0 = half * 64
                    k.op(PE, lambda: nc.tensor.matmul(
                        out=sreg(p, cur, half), lhsT=KTs[r0:r0 + 64, kvh, kcols:kcols + 128],
                        rhs=big[r0:r0 + 64, 2 * kvh:2 * kvh + 2, qt * 128:(qt + 1) * 128], start=True, stop=True),
                         reads=[B_KTs[qt + cur]] + B_big[2 * kvh:2 * kvh + 2], writes=[B_bk[4 + 2 * p + half]], signal=True)

        def emit_pv(idx):
            qt, kvh = items[idx]
            T = T0 + qt
            has_prev = T >= 1
            p = idx % 2
            pb = idx % 2
            ob = idx % 2
            osw = ps[:, ob, 0:264].rearrange("p (h e) -> p h e", h=4)
            for s_ in range(4):
                head = kvh * 4 + HMAP[s_]
                hf, jj = s_ // 2, s_ % 2
                for cur in ([0, 1] if has_prev else [1]):
                    k.op(ACT, lambda: nc.scalar.activation(
                        out=pTs[:, pb, cur, s_, :], in_=sreg(p, cur, hf)[:, jj * 128:(jj + 1) * 128], func=AF.Exp,
                        scale=0.125, bias=biass[:, head, 1 - cur:2 - cur]),
                         reads=[B_bk[4 + 2 * p + hf]], writes=[B_pTs[pb]], waw=False)
            if has_prev:
                k.op(POOL, lambda: nc.gpsimd.tensor_tensor(out=pTs[:, pb, 0, :, :], in0=pTs[:, pb, 0, :, :], in1=maskP4[:],
                                                           op=ALU.mult), reads=[B_pTs[pb]], writes=[B_pTs[pb]])
            k.op(POOL, lambda: nc.gpsimd.tensor_tensor(out=pTs[:, pb, 1, :, :], in0=pTs[:, pb, 1, :, :], in1=maskC4[:],
                                                       op=ALU.mult), reads=[B_pTs[pb]], writes=[B_pTs[pb]])
            first = True
            for s_ in range(4):
                i = HMAP[s_]
                if has_prev:
                    k.op(PE, lambda: nc.tensor.matmul(out=osw[:, i, :], lhsT=pTs[:, pb, 0, s_, :],
                                                      rhs=Vs[:, qt, kvh, :], start=first, stop=False,
                                                      skip_group_check=True),
                         reads=[B_pTs[pb], B_Vs[qt]], writes=[B_bk[ob]], signal=False)
                    first = False
                k.op(PE, lambda: nc.tensor.matmul(out=osw[:, i, :], lhsT=pTs[:, pb, 1, s_, :],
                                                  rhs=Vs[:, qt + 1, kvh, :], start=first, stop=True,
                                                  skip_group_check=True),
                     reads=[B_pTs[pb], B_Vs[qt + 1]], writes=[B_bk[ob]], signal=(s_ == 3))
                first = False
            dd = den[:, ob, :]
            rd = rden[:, ob, :]
            k.op(DVE, lambda: nc.vector.tensor_tensor(out=dd, in0=osw[:, :, 64], in1=sinkp[:, kvh * 4:kvh * 4 + 4],
                                                      op=ALU.add), reads=[B_bk[ob]], writes=[B_den[ob]])
            k.op(DVE, lambda: nc.vector.reciprocal(out=rd, in_=dd), reads=[B_den[ob]], writes=[B_rden[ob]])
            k.op(DVE, lambda: nc.vector.tensor_tensor(
                out=obf[:, kvh * 256:(kvh + 1) * 256].rearrange("p (h e) -> p h e", h=4), in0=osw[:, :, 0:64],
                in1=rd.unsqueeze(2).broadcast_to([128, 4, 64]), op=ALU.mult),
                 reads=[B_bk[ob], B_rden[ob]], writes=[B_obf], waw=(kvh == 0))
            if kvh == 3:
                transpose_out(qt, 16)

        n = len(items)
        deferred = []
        emit_scores(0)
        for i in range(n):
            if i + 1 < n:
                emit_scores(i + 1)
            pend = list(deferred)
            del deferred[:]
            emit_pv(i)
            for f_ in pend:
                f_()
        for f_ in deferred:
            f_()

    setup()
    for g in range(NG):
        group(g)
    for t in range(4):
        k.wait(SP, S_o[t], S_o[t].cnt)
    if DEBUG_DUMP:
        tens = {"hT": hT, "big": big, "KTd": KTd, "Vd": Vd, "KTs": KTs, "Vs": Vs, "xg": xg, "biasd": biasd,
                "sinkp": sinkp, "nlam": nlam, "maskC4": maskC4, "maskP4": maskP4, "ident": ident, "rstd": rstd,
                "xb": xb, "ost": ost, "obf": obf}
        for E in [PE, ACT, DVE, POOL]:
            k.wait(SP, E.sem, E.sem.cnt)
        for i_ in range(3):
            k.wait(SP, S_wr[i_], S_wr[i_].cnt)
        S_d = SemC(nc, "s_dbg")
        for name in DEBUG_DUMP:
            tt = tens[name]
            shp = list(tt.shape)
            flat = 1
            for d_ in shp[1:]:
                flat *= d_
            dd_ = nc.dram_tensor("dbg_" + name, [128, flat], tt.dtype, kind="ExternalOutput").ap()
            letters = "abcde"[:len(shp) - 1]
            src = tt[:] if len(shp) == 2 else tt[:].rearrange("p " + " ".join(letters) + " -> p (" + " ".join(letters) + ")")
            k.op(SP, lambda: nc.sync.dma_start(out=dd_, in_=src), dsem=S_d)
        k.wait(SP, S_d, S_d.cnt)
    return nc


_NC_CACHE = {}


def kernel(x, attn_norm_w, w_in, lambda_q1, lambda_k1, lambda_q2, lambda_k2, subln_w, sinks, w_out,
           ffn_norm_w, w_gate, w_up, w_down, final_norm_w):
    f = lambda a: np.ascontiguousarray(np.asarray(a, dtype=np.float32))
    x = f(x)
    shared = {
        "w_in": f(w_in)[0], "w_out": f(w_out)[0], "w_gate": f(w_gate)[0], "w_up": f(w_up)[0], "w_down": f(w_down)[0],
        "attn_norm_w": f(attn_norm_w)[0], "ffn_norm_w": f(ffn_norm_w)[0], "final_norm_w": f(final_norm_w),
        "lambda_q1": f(lambda_q1)[0], "lambda_k1": f(lambda_k1)[0], "lambda_q2": f(lambda_q2)[0],
        "lambda_k2": f(lambda_k2)[0], "subln_w": f(subln_w)[0], "sinks": f(sinks)[0],
    }
    if "nc" not in _NC_CACHE:
        _NC_CACHE["nc"] = build_nc()
    nc = _NC_CACHE["nc"]
    in_maps = [dict(shared, x=x[b]) for b in range(8)]
    res = run_bass_kernel_spmd(nc, in_maps, core_ids=list(range(8)))
    return np.stack([np.asarray(res.results[b]["out"], dtype=np.float32) for b in range(8)], axis=0)
```

```python
import math
import numpy as np
import concourse.bass as bass
import concourse.mybir as mybir
from concourse.bass_utils import run_bass_kernel_spmd

F32 = mybir.dt.float32
BF16 = mybir.dt.bfloat16
AF = mybir.ActivationFunctionType
ALU = mybir.AluOpType
AX = mybir.AxisListType

S = 2048
D = 2048
DFF = 5632
NG = 4
GT = 512
EPS = 1e-5
LAMBDA_INIT = 0.8 - 0.6 * math.exp(-0.3 * 0)
SLOPE_D = [2.0 ** (-8.0 * (h + 1) / 8) for h in range(8)]
SLOPE_S = [2.0 ** (-8.0 * (h + 1) / 16) for h in range(16)]
NEG = -30000.0
LOOKAHEAD = 4
DEBUG_STOP = None
DEBUG_DUMP = []
DEBUG_MAXBLK = None


class SemC:
    def __init__(self, nc, name):
        self.h = nc.alloc_semaphore(name)
        self.cnt = 0


class Eng:
    def __init__(self, nc, e, name, inorder_safe=False):
        self.e = e
        self.name = name
        self.sem = SemC(nc, "prog_" + name)
        self.seen = {}
        self.inorder_safe = inorder_safe


class Buf:
    __slots__ = ("name", "w", "r")

    def __init__(self, name):
        self.name = name
        self.w = {}
        self.r = {}


def _merge(d, tok):
    s, v = tok
    if d.get(s, 0) < v:
        d[s] = v


class K:
    def __init__(self, nc):
        self.nc = nc
        self.PE = Eng(nc, nc.tensor, "pe", inorder_safe=True)
        self.ACT = Eng(nc, nc.scalar, "act")
        self.DVE = Eng(nc, nc.vector, "dve")
        self.POOL = Eng(nc, nc.gpsimd, "pool")
        self.SP = Eng(nc, nc.sync, "sp")
        self.nwait = 0

    def wait(self, E, sem, val):
        if sem is E.sem and E.inorder_safe:
            return
        if E.seen.get(sem, 0) >= val:
            return
        E.e.wait_ge(sem.h, val)
        E.seen[sem] = val
        self.nwait += 1

    def op(self, E, fn, reads=(), writes=(), signal=True, dsem=None, waw=True):
        deps = {}
        for b in reads:
            for s, v in b.w.items():
                _merge(deps, (s, v))
        for b in writes:
            for s, v in b.r.items():
                _merge(deps, (s, v))
            if waw or b.r:
                for s, v in b.w.items():
                    _merge(deps, (s, v))
        for s, v in deps.items():
            self.wait(E, s, v)
        ins = fn()
        if dsem is not None:
            dsem.cnt += 16
            ins.then_inc(dsem.h, 16)
            tok = (dsem, dsem.cnt)
        elif signal:
            E.sem.cnt += 1
            ins.then_inc(E.sem.h, 1)
            tok = (E.sem, E.sem.cnt)
        else:
            tok = (E.sem, E.sem.cnt + 1)
        for b in writes:
            if b.r:
                b.r = {}
                b.w = {}
            _merge(b.w, tok)
        for b in reads:
            _merge(b.r, tok)
        return tok


def build_nc():
    nc = bass.Bass("TRN2", target_bir_lowering=False)
    k = K(nc)
    PE, ACT, DVE, POOL, SP = k.PE, k.ACT, k.DVE, k.POOL, k.SP

    def din(name, shape):
        return nc.dram_tensor(name, shape, F32, kind="ExternalInput").ap()

    x = din("x", [S, D])
    w_in = din("w_in", [D, 4608])
    w_out = din("w_out", [D, D])
    w_gate = din("w_gate", [D, DFF])
    w_up = din("w_up", [D, DFF])
    w_down = din("w_down", [DFF, D])
    attn_norm_w = din("attn_norm_w", [D])
    ffn_norm_w = din("ffn_norm_w", [D])
    final_norm_w = din("final_norm_w", [D])
    lq1 = din("lambda_q1", [64])
    lk1 = din("lambda_k1", [64])
    lq2 = din("lambda_q2", [64])
    lk2 = din("lambda_k2", [64])
    subln_w = din("subln_w", [128])
    sinks = din("sinks", [16])
    out = nc.dram_tensor("out", [S, D], F32, kind="ExternalOutput").ap()

    w_in_v = w_in.rearrange("(k p) n -> p k n", p=128)
    w_out_v = w_out.rearrange("(k p) n -> p k n", p=128)
    w_gate_v = w_gate.rearrange("(k p) n -> p k n", p=128)
    w_up_v = w_up.rearrange("(k p) n -> p k n", p=128)
    w_down_v = w_down.rearrange("(k p) n -> p k n", p=128)

    def sb(name, shape, dt):
        return nc.alloc_sbuf_tensor(name, shape, dt)

    xg = sb("xg", [128, 4, 2048], F32)
    hT = sb("hT", [128, 16, 512], BF16)
    big = sb("big", [128, 24, 512], BF16)
    KTd = sb("KTd", [128, 8, 2048], BF16)
    Vd = sb("Vd", [128, 16, 8, 130], BF16)
    KTs = sb("KTs", [128, 4, 640], BF16)
    Vs = sb("Vs", [128, 5, 4, 66], BF16)
    wr = sb("wr", [128, 3, 4096], BF16)
    wbc = sb("wbc", [128, 2048], F32)
    xb = sb("xb", [128, 2048], BF16)
    sg = sb("sg", [128, 4, 512], F32)
    pTs = sb("pTs", [128, 2, 2, 4, 128], BF16)
    pT = pTs[:].rearrange("p a b c q -> p (a b c) q")
    ost = sb("ost", [128, 8, 128], F32)
    obf = sb("obf", [128, 1024], BF16)
    ident = sb("ident", [128, 128], BF16)
    maskC4 = sb("maskC4", [128, 4, 128], BF16)
    maskP4 = sb("maskP4", [128, 4, 128], BF16)
    Tt = sb("Tt", [128, 16], F32)
    Tu = sb("Tu", [128, 16], F32)
    varg = sb("varg", [128, 5, 16], F32)
    qbias = sb("qbias", [128, 5, 16], F32)
    vfac = sb("vfac", [128, 5, 16], F32)
    biasd = sb("biasd", [128, 8, 16], F32)
    biass = sb("biass", [128, 16, 2], F32)
    sinkbc = sb("sinkbc", [128, 16], F32)
    sinkp = sb("sinkp", [128, 16], F32)
    sublnbc = sb("sublnbc", [128, 128], F32)
    lam4 = sb("lam4", [128, 4, 64], F32)
    lsm = sb("lsm", [128, 8], F32)
    nlam = sb("nlam", [128, 1], F32)
    nh = sb("nh", [128, 8], F32)
    epsc = sb("epsc", [128, 1], F32)
    ss = sb("ss", [128, 4], F32)
    vv = sb("vv", [128, 4], F32)
    rstd = sb("rstd", [128, 4], F32)
    ssq = sb("ssq", [128, 8], F32)
    vq = sb("vq", [128, 8], F32)
    rq = sb("rq", [128, 8], F32)
    rr = sb("rr", [128, 4], F32)
    den = sb("den", [128, 2, 4], F32)
    rden = sb("rden", [128, 2, 4], F32)

    ps = nc.alloc_psum_tensor("ps", [128, 8, 512], F32)

    B_xg = [Buf(f"xg{t}") for t in range(4)]
    B_hT = [Buf(f"hT{t}") for t in range(4)]
    B_big = [Buf(f"big{c}") for c in range(24)]
    B_KTd = [[Buf(f"KTd{h}_{g}") for g in range(NG)] for h in range(8)]
    B_Vd = [Buf(f"Vd{t}") for t in range(16)]
    B_KTs = [Buf(f"KTs{t}") for t in range(5)]
    B_Vs = [Buf(f"Vs{t}") for t in range(5)]
    B_wr = [Buf(f"wr{i}") for i in range(3)]
    S_wr = [SemC(nc, f"s_wr{i}") for i in range(3)]
    B_wbc = Buf("wbc")
    S_wbc = SemC(nc, "s_wbc")
    B_xb = Buf("xb")
    B_sg = [Buf(f"sg{i}") for i in range(4)]
    B_pT = [Buf(f"pT{i}") for i in range(16)]
    B_pTs = [Buf("pTs0"), Buf("pTs1")]
    B_ost = Buf("ost")
    B_obf = Buf("obf")
    B_junk = Buf("junk")
    B_const = Buf("const")
    B_stat = Buf("stat")
    B_qstat = Buf("qstat")
    B_den = [Buf("den0"), Buf("den1")]
    B_rden = [Buf("rden0"), Buf("rden1")]
    B_vv, B_rstd = Buf("vv"), Buf("rstd")
    B_bk = [Buf(f"bank{b}") for b in range(8)]
    B_ps = B_bk[0:4]
    S_x = [SemC(nc, f"s_x{t}") for t in range(4)]
    S_o = [SemC(nc, f"s_o{t}") for t in range(4)]
    S_set = SemC(nc, "s_set")
    S_set2 = SemC(nc, "s_set2")
    S_set3 = SemC(nc, "s_set3")

    def setup():
        Bi, Bm, Bt, Bn, Bl, Bsk, Bsu = Buf("i"), Buf("m"), Buf("t"), Buf("n"), Buf("l"), Buf("sk"), Buf("su")
        k.op(POOL, lambda: nc.gpsimd.memset(ident[:], 0.0), writes=[Bi])
        k.op(POOL, lambda: nc.gpsimd.affine_select(out=ident[:], in_=ident[:], compare_op=ALU.not_equal,
                                                   fill=1.0, base=0, pattern=[[-1, 128]], channel_multiplier=1),
             reads=[Bi], writes=[Bi])
        k.op(POOL, lambda: nc.gpsimd.memset(maskC4[:], 1.0), writes=[Bm])
        k.op(POOL, lambda: nc.gpsimd.affine_select(out=maskC4[:], in_=maskC4[:], compare_op=ALU.is_ge,
                                                   fill=0.0, base=0, pattern=[[0, 4], [1, 128]],
                                                   channel_multiplier=-1), reads=[Bm], writes=[Bm])
        k.op(POOL, lambda: nc.gpsimd.memset(maskP4[:], 1.0), writes=[Bn])
        k.op(POOL, lambda: nc.gpsimd.affine_select(out=maskP4[:], in_=maskP4[:], compare_op=ALU.is_ge,
                                                   fill=0.0, base=-1, pattern=[[0, 4], [-1, 128]],
                                                   channel_multiplier=1), reads=[Bn], writes=[Bn])
        k.op(POOL, lambda: nc.gpsimd.iota(Tt[:], pattern=[[-128, 16]], base=-64, channel_multiplier=1, allow_small_or_imprecise_dtypes=True),
             writes=[Bt])
        k.op(POOL, lambda: nc.gpsimd.memset(nh[:], -0.5), writes=[Buf("x")])
        k.op(POOL, lambda: nc.gpsimd.memset(epsc[:], EPS), writes=[Buf("x")])
        Bvones = Buf("vones")
        k.op(POOL, lambda: nc.gpsimd.memset(Vd[:, :, :, 128:130], 1.0), writes=[Bvones])
        k.op(POOL, lambda: nc.gpsimd.memset(Vs[:, :, :, 64:66], 1.0), writes=[Buf("x")])
        for i, v in enumerate([lq1, lk1, lq2, lk2]):
            k.op(SP, lambda: nc.sync.dma_start(out=lam4[:, i, :], in_=v.partition_broadcast(128)),
                 writes=[Bl], dsem=S_set, waw=False)
        k.op(SP, lambda: nc.sync.dma_start(out=sinkbc[:], in_=sinks.partition_broadcast(128)),
             writes=[Bsk], dsem=S_set2, waw=False)
        k.op(SP, lambda: nc.sync.dma_start(out=sublnbc[:], in_=subln_w.partition_broadcast(128)),
             writes=[Bsu], dsem=S_set3, waw=False)
        Bu, Bva, Bvf = Buf("u"), Buf("va"), Buf("vf")
        k.op(POOL, lambda: nc.gpsimd.iota(Tu[:], pattern=[[128, 16]], base=-1024, channel_multiplier=1,
                                          allow_small_or_imprecise_dtypes=True), writes=[Bu])
        for hh in range(5):
            k.op(DVE, lambda: nc.vector.tensor_scalar(out=varg[:, hh, :], in0=Tu[:], scalar1=SLOPE_D[3 + hh], scalar2=None,
                                                      op0=ALU.mult), reads=[Bu], writes=[Bva], waw=False)
        k.op(ACT, lambda: nc.scalar.activation(out=vfac[:], in_=varg[:], func=AF.Exp), reads=[Bva], writes=[Bvf])
        Bq = Buf("q")
        k.op(POOL, lambda: nc.gpsimd.iota(qbias[:, 0, :], pattern=[[128, 16]], base=64 - 1024, channel_multiplier=0,
                                          allow_small_or_imprecise_dtypes=True), writes=[Bq])
        for hh in range(1, 5):
            k.op(POOL, lambda: nc.gpsimd.tensor_scalar(out=qbias[:, hh, :], in0=qbias[:, 0, :], scalar1=-SLOPE_D[3 + hh],
                                                       scalar2=None, op0=ALU.mult), reads=[Bq], writes=[Buf("x")])
        k.op(POOL, lambda: nc.gpsimd.tensor_scalar(out=qbias[:, 0, :], in0=qbias[:, 0, :], scalar1=-SLOPE_D[3],
                                                   scalar2=None, op0=ALU.mult), reads=[Bq], writes=[Bq])
        for hh in range(5):
            k.op(DVE, lambda: nc.vector.tensor_copy(out=Vd[:, :, 3 + hh, 128:130],
                                                    in_=vfac[:, hh, :].unsqueeze(2).broadcast_to([128, 16, 2])),
                 reads=[Bvf, Bvones], writes=[Buf("x")])
        for h in range(8):
            k.op(DVE, lambda: nc.vector.tensor_scalar(out=biasd[:, h, :], in0=Tt[:], scalar1=SLOPE_D[h],
                                                      scalar2=None, op0=ALU.mult), reads=[Bt], writes=[Buf("x")])
        for h in range(16):
            k.op(DVE, lambda: nc.vector.tensor_scalar(out=biass[:, h, :], in0=Tt[:, 0:2], scalar1=SLOPE_S[h],
                                                      scalar2=None, op0=ALU.mult), reads=[Bt], writes=[Buf("x")])
        B_l2, B_l3, B_l4, B_l5 = Buf("l2"), Buf("l3"), Buf("l4"), Buf("l5")
        k.op(DVE, lambda: nc.vector.tensor_tensor(out=lam4[:, 0, :], in0=lam4[:, 0, :], in1=lam4[:, 1, :], op=ALU.mult),
             reads=[Bl], writes=[B_l2], waw=False)
        k.op(DVE, lambda: nc.vector.tensor_tensor(out=lam4[:, 2, :], in0=lam4[:, 2, :], in1=lam4[:, 3, :], op=ALU.mult),
             reads=[Bl], writes=[B_l2], waw=False)
        k.op(DVE, lambda: nc.vector.tensor_reduce(out=lsm[:, 0:1], in_=lam4[:, 0, :], axis=AX.X, op=ALU.add),
             reads=[B_l2], writes=[B_l3], waw=False)
        k.op(DVE, lambda: nc.vector.tensor_reduce(out=lsm[:, 1:2], in_=lam4[:, 2, :], axis=AX.X, op=ALU.add),
             reads=[B_l2], writes=[B_l3], waw=False)
        k.op(ACT, lambda: nc.scalar.activation(out=lsm[:, 2:4], in_=lsm[:, 0:2], func=AF.Exp),
             reads=[B_l3], writes=[B_l4])
        k.op(DVE, lambda: nc.vector.tensor_tensor(out=lsm[:, 4:5], in0=lsm[:, 3:4], in1=lsm[:, 2:3], op=ALU.subtract),
             reads=[B_l4], writes=[B_l5])
        k.op(DVE, lambda: nc.vector.tensor_scalar(out=nlam[:], in0=lsm[:, 4:5], scalar1=-LAMBDA_INIT, scalar2=None,
                                                  op0=ALU.add), reads=[B_l5], writes=[Buf("x")])
        k.op(DVE, lambda: nc.vector.tensor_scalar(out=sublnbc[:], in0=sublnbc[:], scalar1=1.0 - LAMBDA_INIT,
                                                  scalar2=None, op0=ALU.mult), reads=[Bsu], writes=[Bsu])
        for h in range(16):
            k.op(ACT, lambda: nc.scalar.activation(out=sinkp[:, h:h + 1], in_=Tt[:, 0:1], func=AF.Exp,
                                                   scale=SLOPE_S[h], bias=sinkbc[:, h:h + 1]),
                 reads=[Bt, Bsk], writes=[Buf("x")])
        engs = [PE, ACT, DVE, POOL, SP]
        for E in engs:
            for Fe in engs:
                if Fe is not E and Fe.sem.cnt > 0:
                    k.wait(E, Fe.sem, Fe.sem.cnt)
            for ss_ in (S_set, S_set2, S_set3):
                k.wait(E, ss_, ss_.cnt)

    ring_state = {"i": 0, "g": 0, "pid": 0}
    NPIECE = 95
    wcache = nc.dram_tensor("wcache", [NPIECE, 128, 4096], BF16, kind="Internal").ap()
    B_wc = [Buf(f"wc{i}") for i in range(NPIECE)]
    S_ws = [SemC(nc, f"s_ws{i}") for i in range(3)]
    S_wrh = [SemC(nc, f"s_wrh{i}") for i in range(3)]

    def cache_group(pid):
        return 0 if pid % 3 == 0 else 1

    def ring_next():
        i = ring_state["i"] % 3
        ring_state["i"] += 1
        pid = ring_state["pid"]
        ring_state["pid"] += 1
        return i, pid

    def cache_store(i, pid, nel):
        k.op(SP, lambda: nc.sync.dma_start(out=wcache[pid, :, 0:nel], in_=wr[:, i, 0:nel]),
             reads=[B_wr[i]], writes=[B_wc[pid]], dsem=S_ws[i])

    def cache_load(i, pid, nel):
        k.op(SP, lambda: nc.sync.dma_start(out=wr[:, i, 0:nel], in_=wcache[pid, :, 0:nel]),
             reads=[B_wc[pid]], writes=[B_wr[i]], dsem=S_wrh[i])

    def load_piece(src_view, k0, k1, c0, c1):
        i, pid = ring_next()
        kk = k1 - k0
        n = c1 - c0
        assert kk * n <= 4096
        dst = wr[:, i, 0:kk * n].rearrange("p (k n) -> p k n", k=kk)
        gfill = cache_group(pid)
        if ring_state["g"] <= gfill:
            k.op(POOL, lambda: nc.gpsimd.dma_start(out=dst, in_=src_view[:, k0:k1, c0:c1]),
                 writes=[B_wr[i]], dsem=S_wr[i])
            if ring_state["g"] == gfill:
                cache_store(i, pid, kk * n)
        else:
            cache_load(i, pid, kk * n)
        return dst, B_wr[i]

    bank_state = {"i": 0}

    def next_bank():
        b = bank_state["i"] % 4
        bank_state["i"] += 1
        return b

    evac_state = {"i": 0}

    def evac_copy(out_ap, in_ap, reads, writes, waw=True, eng=None):
        evac_state["i"] += 1
        if (evac_state["i"] % 2 == 0 and eng is None) or eng == "act":
            return k.op(ACT, lambda: nc.scalar.activation(out=out_ap, in_=in_ap, func=AF.Copy),
                        reads=reads, writes=writes, waw=waw)
        return k.op(DVE, lambda: nc.vector.tensor_copy(out=out_ap, in_=in_ap), reads=reads, writes=writes, waw=waw)

    def load_wbc(vec):
        k.op(SP, lambda: nc.sync.dma_start(out=wbc[:], in_=vec.partition_broadcast(128)), writes=[B_wbc], dsem=S_wbc)

    B_statt = [Buf(f"stat{t}") for t in range(4)]
    B_vvt = [Buf(f"vv{t}") for t in range(4)]
    B_rstdt = [Buf(f"rstd{t}") for t in range(4)]
    B_xbh = [Buf("xbh0"), Buf("xbh1")]

    def stats_tile(t):
        k.op(ACT, lambda: nc.scalar.activation(out=xb[:], in_=xg[:, t, :], func=AF.Square, accum_out=ss[:, t:t + 1]),
             reads=[B_xg[t]], writes=B_xbh + [B_statt[t]])
        k.op(ACT, lambda: nc.scalar.activation(out=vv[:, t:t + 1], in_=ss[:, t:t + 1], func=AF.Sqrt, scale=1.0 / D,
                                               bias=epsc[:]), reads=[B_statt[t]], writes=[B_vvt[t]])
        k.op(DVE, lambda: nc.vector.reciprocal(out=rstd[:, t:t + 1], in_=vv[:, t:t + 1]),
             reads=[B_vvt[t]], writes=[B_rstdt[t]])

    def norm_to_hT():
        for t in range(4):
            stats_tile(t)
        for t in range(4):
            for r in range(2):
                cs = slice(r * 1024, (r + 1) * 1024)
                k.op(DVE, lambda: nc.vector.scalar_tensor_tensor(out=xb[:, cs], in0=xg[:, t, cs], scalar=rstd[:, t:t + 1],
                                                                 in1=wbc[:, cs], op0=ALU.mult, op1=ALU.mult),
                     reads=[B_xg[t], B_rstdt[t], B_wbc], writes=[B_xbh[r]])
                b = next_bank()
                tp = ps[:, b, :].bitcast(BF16).rearrange("p (c n) -> p c n", c=8)
                for c in range(8):
                    dc = r * 8 + c
                    k.op(PE, lambda: nc.tensor.transpose(out=tp[:, c, :], in_=xb[:, dc * 128:(dc + 1) * 128],
                                                         identity=ident[:]),
                         reads=[B_xbh[r]], writes=[B_ps[b]], signal=(c == 7))
                evac_copy(hT[:, r * 8:(r + 1) * 8, t * 128:(t + 1) * 128], tp, reads=[B_ps[b]], writes=[B_hT[t]], waw=False)

    ss2 = sb("ss2", [128, 4], F32)
    vv2 = sb("vv2", [128, 4], F32)
    rstd2 = sb("rstd2", [128, 4], F32)
    B_e1 = [Buf(f"e1_{t}") for t in range(4)]
    B_e2 = [Buf(f"e2_{t}") for t in range(4)]
    B_e3 = [Buf(f"e3_{t}") for t in range(4)]
    S_xe = SemC(nc, "s_xe")
    en_state = {"i": 0}

    def early_norm_chain(gn, t):
        sgx = sg[:].rearrange("p a n -> p (a n)")
        junk16 = ost[:].rearrange("p a e -> p (a e)").bitcast(BF16)
        row0 = (gn * 4 + t) * 128
        k.op(ACT, lambda: nc.scalar.dma_start(out=sgx, in_=x[row0:row0 + 128, :]), writes=B_sg, dsem=S_xe)
        k.op(ACT, lambda: nc.scalar.activation(out=junk16, in_=sgx, func=AF.Square, accum_out=ss2[:, t:t + 1]),
             reads=B_sg, writes=B_osth[0] + [B_e1[t]])
        k.op(ACT, lambda: nc.scalar.activation(out=vv2[:, t:t + 1], in_=ss2[:, t:t + 1], func=AF.Sqrt, scale=1.0 / D,
                                               bias=epsc[:]), reads=[B_e1[t]], writes=[B_e2[t]])
        k.op(DVE, lambda: nc.vector.reciprocal(out=rstd2[:, t:t + 1], in_=vv2[:, t:t + 1]),
             reads=[B_e2[t]], writes=[B_e3[t]])
        for r in range(2):
            cs = slice(r * 1024, (r + 1) * 1024)
            k.op(DVE, lambda: nc.vector.scalar_tensor_tensor(out=xb[:, cs], in0=sgx[:, cs], scalar=rstd2[:, t:t + 1],
                                                             in1=wbc[:, cs], op0=ALU.mult, op1=ALU.mult),
                 reads=B_sg + [B_e3[t], B_wbc], writes=[B_xbh[r]])

    def early_norm_transposes(t):
        for r in range(2):
            b = 4 + en_state["i"] % 2
            en_state["i"] += 1
            tp = ps[:, b, :].bitcast(BF16).rearrange("p (c n) -> p c n", c=8)
            for c in range(8):
                dc = r * 8 + c
                k.op(PE, lambda: nc.tensor.transpose(out=tp[:, c, :], in_=xb[:, dc * 128:(dc + 1) * 128], identity=ident[:]),
                     reads=[B_xbh[r]], writes=[B_bk[b]], signal=(c == 7))
            evac_copy(hT[:, r * 8:(r + 1) * 8, t * 128:(t + 1) * 128], tp, reads=[B_bk[b]], writes=[B_hT[t]], waw=False)

    def proj_feature_major(src_view, col0, nchunks, dest_fn, dest_bufs_fn):
        for s0 in range(0, nchunks, 4):
            nch = min(4, nchunks - s0)
            ncols = nch * 128
            kper = 4096 // ncols
            for k0 in range(0, 16, kper):
                piece, bw = load_piece(src_view, k0, k0 + kper, col0 + s0 * 128, col0 + s0 * 128 + ncols)
                for c in range(nch):
                    for kk in range(kper):
                        dc = k0 + kk
                        k.op(PE, lambda: nc.tensor.matmul(
                            out=ps[:, c, :], lhsT=piece[:, kk, c * 128:(c + 1) * 128], rhs=hT[:, dc, :],
                            start=(dc == 0), stop=(dc == 15)),
                             reads=[bw] + B_hT, writes=[B_ps[c]], signal=(kk == kper - 1))
            for c in range(nch):
                evac_copy(dest_fn(s0 + c), ps[:, c, :], reads=[B_ps[c]], writes=dest_bufs_fn(s0 + c))

    def proj_token_major(src_view, kchunks, col0, ncols, lhs_fn, lhs_bufs_fn, evac_fn):
        kper = 4096 // ncols
        pieces = [(a, min(a + kper, kchunks)) for a in range(0, kchunks, kper)]
        for (k0, k1) in pieces:
            piece, bw = load_piece(src_view, k0, k1, col0, col0 + ncols)
            for t in range(4):
                for kk in range(k0, k1):
                    last = (kk == kchunks - 1)
                    k.op(PE, lambda t=t, kk=kk, k0=k0, piece=piece: nc.tensor.matmul(
                        out=ps[:, t, 0:ncols], lhsT=lhs_fn(kk, t), rhs=piece[:, kk - k0, :],
                        start=(kk == 0), stop=(kk == kchunks - 1)),
                         reads=[bw] + lhs_bufs_fn(kk, t), writes=[B_ps[t]], signal=(kk == k1 - 1))
        for t in range(4):
            evac_fn(t)

    pending_io = []

    def group(g):
        T0 = g * 4
        ring_state["g"] = g
        ring_state["pid"] = 0
        def load_x():
            for t in range(4):
                k.op(SP, lambda t=t: nc.sync.dma_start(out=xg[:, t, :], in_=x[(T0 + t) * 128:(T0 + t + 1) * 128, :]),
                     writes=[B_xg[t]], dsem=S_x[t])

        if g == 0:
            load_x()
            load_wbc(attn_norm_w)
            norm_to_hT()
            load_wbc(ffn_norm_w)

        proj_feature_major(w_in_v, 0, 8, lambda c: big[:, c, :], lambda c: [B_big[c]])
        proj_feature_major(w_in_v, 1024, 8, lambda c: KTd[:, c, g * GT:(g + 1) * GT], lambda c: [B_KTd[c][g]])
        def evac_v(t, vb):
            tile_ = T0 + t
            pv = ps[:, t, :].rearrange("p (h e) -> p h e", h=4)
            if vb == 0:
                k.op(DVE, lambda: nc.vector.tensor_copy(out=Vd[:, tile_, 0:3, 0:128], in_=pv[:, 0:3, :]),
                     reads=[B_ps[t]], writes=[B_Vd[tile_]], waw=False)
                k.op(DVE, lambda: nc.vector.tensor_scalar(out=Vd[:, tile_, 3, 0:128], in0=pv[:, 3, :],
                                                          scalar1=vfac[:, 0, tile_:tile_ + 1], scalar2=None, op0=ALU.mult),
                     reads=[B_ps[t]], writes=[B_Vd[tile_]], waw=False)
            else:
                k.op(DVE, lambda: nc.vector.tensor_tensor(
                    out=Vd[:, tile_, 4:8, 0:128], in0=pv,
                    in1=vfac[:, 1:5, tile_].unsqueeze(2).broadcast_to([128, 4, 128]), op=ALU.mult),
                     reads=[B_ps[t]], writes=[B_Vd[tile_]], waw=False)

        for vb in range(2):
            proj_token_major(
                w_in_v, 16, 2048 + vb * 512, 512,
                lambda kk, t: hT[:, kk, t * 128:(t + 1) * 128], lambda kk, t: [B_hT[t]],
                lambda t, vb=vb: evac_v(t, vb))
        if DEBUG_STOP == "qkv":
            return
        if g > 0:
            for f_ in pending_io:
                f_()
            del pending_io[:]
            load_x()
            load_wbc(ffn_norm_w)
        diff_attention(g)
        if DEBUG_STOP == "dattn":
            return
        if g > 0:
            k.op(DVE, lambda: nc.vector.tensor_copy(out=KTs[:, :, 0:128], in_=KTs[:, :, 512:640]),
                 reads=[B_KTs[4]], writes=[B_KTs[0]])
            k.op(DVE, lambda: nc.vector.tensor_copy(out=Vs[:, 0, :, :], in_=Vs[:, 4, :, :]),
                 reads=[B_Vs[4]], writes=[B_Vs[0]])
        proj_feature_major(w_in_v, 3072, 8, lambda c: big[:, c, :], lambda c: [B_big[c]])
        for kh in range(2):
            i, pid = ring_next()
            dst = wr[:, i, :].rearrange("p (k h u e) -> p k h u e", k=8, h=4, u=2)
            srcv = w_in_v[:, kh * 8:kh * 8 + 8, 4096:4352].rearrange("p k (h e) -> p k h e", h=4)
            if g <= cache_group(pid):
                for u in range(2):
                    for hh in range(4):
                        k.op(POOL, lambda: nc.gpsimd.dma_start(out=dst[:, :, hh, u, :], in_=srcv[:, :, hh, :]),
                             writes=[B_wr[i]], dsem=S_wr[i], waw=(u == 0 and hh == 0))
                if g == cache_group(pid):
                    cache_store(i, pid, 4096)
            else:
                cache_load(i, pid, 4096)
            piece = wr[:, i, :].rearrange("p (k n) -> p k n", k=8)
            for kvh in range(4):
                for kk in range(8):
                    dc = kh * 8 + kk
                    k.op(PE, lambda: nc.tensor.matmul(out=ps[:, kvh, :], lhsT=piece[:, kk, kvh * 128:(kvh + 1) * 128],
                                                      rhs=hT[:, dc, :], start=(dc == 0), stop=(dc == 15)),
                         reads=[B_wr[i]] + B_hT, writes=[B_ps[kvh]], signal=(kk == 7))
        for kvh in range(4):
            evac_copy(KTs[:, kvh, 128:640], ps[:, kvh, :], reads=[B_ps[kvh]], writes=B_KTs[1:5], waw=False)
        proj_token_major(
            w_in_v, 16, 4352, 256,
            lambda kk, t: hT[:, kk, t * 128:(t + 1) * 128], lambda kk, t: [B_hT[t]],
            lambda t: evac_copy(Vs[:, 1 + t, :, 0:64], ps[:, t, 0:256].rearrange("p (h e) -> p h e", h=4),
                                reads=[B_ps[t]], writes=[B_Vs[1 + t]], waw=False))
        if DEBUG_STOP == "sproj":
            return
        swa_attention(g)
        if DEBUG_STOP == "attn":
            return
        for db in range(4):
            proj_token_major(
                w_out_v, 16, db * 512, 512,
                lambda kk, t: big[:, 8 + kk, t * 128:(t + 1) * 128], lambda kk, t: [B_big[8 + kk]],
                lambda t, db=db: k.op(DVE, lambda: nc.vector.tensor_tensor(
                    out=xg[:, t, db * 512:(db + 1) * 512], in0=ps[:, t, :], in1=xg[:, t, db * 512:(db + 1) * 512],
                    op=ALU.add), reads=[B_ps[t], B_xg[t]], writes=[B_xg[t]]))
        if DEBUG_STOP == "oproj":
            return
        norm_to_hT()
        for fh in range(2):
            f0 = fh * 22
            for s0 in range(0, 22, 4):
                nch = min(4, 22 - s0)
                ncols = nch * 128
                kper = 4096 // ncols
                c0 = (f0 + s0) * 128
                for (wv, bank0) in ((w_gate_v, 0), (w_up_v, 4)):
                    for k0 in range(0, 16, kper):
                        piece, bw = load_piece(wv, k0, k0 + kper, c0, c0 + ncols)
                        for c in range(nch):
                            for kk in range(kper):
                                dc = k0 + kk
                                k.op(PE, lambda: nc.tensor.matmul(
                                    out=ps[:, bank0 + c, :], lhsT=piece[:, kk, c * 128:(c + 1) * 128], rhs=hT[:, dc, :],
                                    start=(dc == 0), stop=(dc == 15)),
                                     reads=[bw] + B_hT, writes=[B_bk[bank0 + c]], signal=(kk == kper - 1))
                    if bank0 == 0:
                        for c in range(nch):
                            k.op(ACT, lambda: nc.scalar.activation(out=sg[:, c, :], in_=ps[:, c, :], func=AF.Silu),
                                 reads=[B_bk[c]], writes=[B_sg[c]])
                for c in range(nch):
                    k.op(DVE, lambda: nc.vector.tensor_tensor(out=big[:, s0 + c, :], in0=sg[:, c, :], in1=ps[:, 4 + c, :],
                                                              op=ALU.mult),
                         reads=[B_sg[c], B_bk[4 + c]], writes=[B_big[s0 + c]])
            early = (fh == 1 and g + 1 < NG)
            if early:
                load_wbc(attn_norm_w)
            for db in range(4):
                if early:
                    if db > 0:
                        early_norm_transposes(db - 1)
                    early_norm_chain(g + 1, db)
                proj_token_major(
                    w_down_v[:, f0:f0 + 22, :], 22, db * 512, 512,
                    lambda kk, t: big[:, kk, t * 128:(t + 1) * 128], lambda kk, t: [B_big[kk]],
                    lambda t, db=db: k.op(DVE, lambda: nc.vector.tensor_tensor(
                        out=xg[:, t, db * 512:(db + 1) * 512], in0=ps[:, t, :], in1=xg[:, t, db * 512:(db + 1) * 512],
                        op=ALU.add), reads=[B_ps[t], B_xg[t]], writes=[B_xg[t]]))
        if g + 1 < NG:
            early_norm_transposes(3)
        load_wbc(final_norm_w)
        for t in range(4):
            stats_tile(t)
        for t in range(4):
            k.op(DVE, lambda t=t: nc.vector.scalar_tensor_tensor(out=xg[:, t, :], in0=xg[:, t, :], scalar=rstd[:, t:t + 1],
                                                                 in1=wbc[:], op0=ALU.mult, op1=ALU.mult),
                 reads=[B_rstdt[t], B_wbc], writes=[B_xg[t]])
            pending_io.append(lambda t=t: k.op(
                SP, lambda: nc.sync.dma_start(out=out[(T0 + t) * 128:(T0 + t + 1) * 128, :], in_=xg[:, t, :]),
                reads=[B_xg[t]], dsem=S_o[t]))
        if g == NG - 1:
            for f_ in pending_io:
                f_()
            del pending_io[:]

    B_rr, B_rr2 = Buf("rr"), Buf("rr2")
    B_osth = [[Buf(f"ost{q}_{h}") for h in range(8)] for q in range(2)]
    B_ssq, B_vq, B_rq = Buf("ssq"), Buf("vq"), Buf("rq")
    tp_state = {"i": 0}

    def transpose_out(qt, chunk0, bank=None):
        if bank is None:
            b = 2 + tp_state["i"] % 2
            tp_state["i"] += 1
        else:
            b = bank
        tp = ps[:, b, :].bitcast(BF16).rearrange("p (c n) -> p c n", c=8)
        for c in range(8):
            k.op(PE, lambda: nc.tensor.transpose(out=tp[:, c, :], in_=obf[:, c * 128:(c + 1) * 128], identity=ident[:]),
                 reads=[B_obf], writes=[B_ps[b]], signal=(c == 7))
        evac_copy(big[:, chunk0:chunk0 + 8, qt * 128:(qt + 1) * 128], tp, reads=[B_ps[b]],
                  writes=B_big[chunk0:chunk0 + 8], waw=False, eng="dve")

    def diff_attention(g):
        T0 = g * 4
        items = []
        for qt in range(4):
            for h in range(8):
                T = T0 + qt
                kmin = 0
                while kmin < T and SLOPE_D[h] * (128 * (T - kmin - 1) + 1) > 124.0:
                    kmin += 1
                for k0 in range(kmin, T + 1, 4):
                    items.append((qt, h, k0, min(k0 + 4, T + 1), kmin))
        if DEBUG_MAXBLK is not None:
            items = items[:DEBUG_MAXBLK]
        st = {"p": 0, "u": -1}
        ostv = [ost[:], sg[:, 2:4, :].rearrange("p a (b e) -> p (a b) e", e=128)]
        for i_ in range(16):
            for pb_ in B_pTs:
                for s_, v_ in list(pb_.r.items()) + list(pb_.w.items()):
                    _merge(B_pT[i_].r, (s_, v_))
        for h_ in range(8):
            for sgb in B_sg[2:4]:
                for s_, v_ in list(sgb.r.items()) + list(sgb.w.items()):
                    _merge(B_osth[1][h_].r, (s_, v_))

        def emit_scores(idx):
            qt, h, k0, k1, kmin = items[idx]
            p = idx % 3
            for kt in range(k0, k1):
                for m in range(2):
                    bk = 2 + 2 * p + m
                    j = kt - k0
                    k.op(PE, lambda: nc.tensor.matmul(out=ps[:, bk, j * 128:(j + 1) * 128],
                                                      lhsT=KTd[m * 64:(m + 1) * 64, h, kt * 128:(kt + 1) * 128],
                                                      rhs=big[m * 64:(m + 1) * 64, h, qt * 128:(qt + 1) * 128],
                                                      start=True, stop=True),
                         reads=[B_KTd[h][kt // 4], B_big[h]], writes=[B_bk[bk]], signal=True)

        def finish_unit(qt, h, ob):
            o1 = ps[:, ob, 0:130]
            o2 = ps[:, ob, 130:260]
            ov = ostv[qt % 2][:, h, :]
            bo = B_osth[qt % 2][h]
            k.op(DVE, lambda: nc.vector.reciprocal(out=rr[:, 0:1], in_=o1[:, 128:129]), reads=[B_bk[ob]], writes=[B_rr])
            k.op(DVE, lambda: nc.vector.reciprocal(out=rr[:, 1:2], in_=o2[:, 128:129]), reads=[B_bk[ob]], writes=[B_rr],
                 waw=False)
            k.op(DVE, lambda: nc.vector.tensor_tensor(out=rr[:, 2:3], in0=rr[:, 1:2], in1=nlam[:], op=ALU.mult),
                 reads=[B_rr], writes=[B_rr2])
            k.op(DVE, lambda: nc.vector.tensor_scalar(out=ov, in0=o1[:, 0:128], scalar1=rr[:, 0:1], scalar2=None,
                                                      op0=ALU.mult), reads=[B_bk[ob], B_rr], writes=[bo])
            k.op(DVE, lambda: nc.vector.scalar_tensor_tensor(out=ov, in0=o2[:, 0:128], scalar=rr[:, 2:3],
                                                             in1=ov, op0=ALU.mult, op1=ALU.add),
                 reads=[B_bk[ob], B_rr2, bo], writes=[bo])

        def finish_qtile(qt):
            ov = ostv[qt % 2]
            bos = B_osth[qt % 2]
            sqv = sg[:, 0:2, :].rearrange("p a (b e) -> p (a b) e", e=128)
            k.op(DVE, lambda: nc.vector.tensor_tensor(out=sqv, in0=ov, in1=ov, op=ALU.mult),
                 reads=bos, writes=B_sg[0:2])
            k.op(DVE, lambda: nc.vector.tensor_reduce(out=ssq[:], in_=sqv, axis=AX.X, op=ALU.add),
                 reads=B_sg[0:2], writes=[B_ssq])
            k.op(DVE, lambda: nc.vector.tensor_scalar(out=vq[:], in0=ssq[:], scalar1=1.0 / 128, scalar2=EPS,
                                                      op0=ALU.mult, op1=ALU.add), reads=[B_ssq], writes=[B_vq])
            k.op(POOL, lambda: nc.gpsimd.tensor_tensor(out=rq[:], in0=vq[:], in1=nh[:], op=ALU.pow),
                 reads=[B_vq], writes=[B_rq])
            k.op(POOL, lambda: nc.gpsimd.tensor_tensor(out=ov, in0=ov,
                                                       in1=rq[:].unsqueeze(2).broadcast_to([128, 8, 128]), op=ALU.mult),
                 reads=[B_rq] + bos, writes=bos)
            k.op(POOL, lambda: nc.gpsimd.tensor_tensor(out=obf[:].rearrange("p (h e) -> p h e", h=8), in0=ov,
                                                       in1=sublnbc[:].unsqueeze(1).broadcast_to([128, 8, 128]),
                                                       op=ALU.mult),
                 reads=bos, writes=[B_obf])
            def tr_():
                st["u"] += 1
                transpose_out(qt, 8, bank=st["u"] % 2)
            deferred.append([6, tr_])

        def emit_pv(idx):
            qt, h, k0, k1, kmin = items[idx]
            T = T0 + qt
            p = idx % 3
            if k0 == kmin:
                st["u"] += 1
                st["ob"] = st["u"] % 2
            ob = st["ob"]
            wide = h >= 3
            wbase = {}
            if wide:
                nkt = k1 - k0
                for m in range(2):
                    bk = 2 + 2 * p + m
                    if st["p"] % 4:
                        st["p"] += 4 - st["p"] % 4
                    base = st["p"] % 16
                    st["p"] += 4
                    wbase[m] = base
                    k.op(ACT, lambda: nc.scalar.activation(
                        out=pT[:, base:base + nkt, :], in_=ps[:, bk, 0:nkt * 128].rearrange("p (j q) -> p j q", q=128),
                        func=AF.Exp, scale=0.125, bias=qbias[:, h - 3, T:T + 1]),
                         reads=[B_bk[bk]], writes=B_pT[base:base + nkt])
            for kt in range(k0, k1):
                for m in range(2):
                    bk = 2 + 2 * p + m
                    j = kt - k0
                    if wide:
                        pi = wbase[m] + j
                    else:
                        pi = st["p"] % 16
                        st["p"] += 1
                        k.op(ACT, lambda: nc.scalar.activation(out=pT[:, pi, :], in_=ps[:, bk, j * 128:(j + 1) * 128],
                                                               func=AF.Exp, scale=0.125,
                                                               bias=biasd[:, h, T - kt:T - kt + 1]),
                             reads=[B_bk[bk]], writes=[B_pT[pi]])
                    if kt == T:
                        k.op(DVE, lambda: nc.vector.tensor_tensor(out=pT[:, pi, :], in0=pT[:, pi, :], in1=maskC4[:, 0, :],
                                                                  op=ALU.mult), reads=[B_pT[pi]], writes=[B_pT[pi]])
                    k.op(PE, lambda: nc.tensor.matmul(out=ps[:, ob, m * 130:(m + 1) * 130], lhsT=pT[:, pi, :],
                                                      rhs=Vd[:, kt, h, :], start=(kt == kmin and m == 0), stop=(kt == T),
                                                      skip_group_check=True),
                         reads=[B_pT[pi], B_Vd[kt]], writes=[B_bk[ob]], signal=True)
            if k1 == T + 1:
                finish_unit(qt, h, ob)
                if h == 7:
                    finish_qtile(qt)

        n = len(items)
        deferred = []
        for i in range(min(2, n)):
            emit_scores(i)
        for i in range(n):
            if i + 2 < n:
                emit_scores(i + 2)
            emit_pv(i)
            for d_ in list(deferred):
                d_[0] -= 1
                if d_[0] <= 0:
                    deferred.remove(d_)
                    d_[1]()
        for d_ in deferred:
            d_[1]()
        for i_ in range(16):
            for s_, v_ in list(B_pT[i_].r.items()) + list(B_pT[i_].w.items()):
                for pb_ in B_pTs:
                    _merge(pb_.r, (s_, v_))
        for h_ in range(8):
            for s_, v_ in list(B_osth[1][h_].r.items()) + list(B_osth[1][h_].w.items()):
                for sgb in B_sg[2:4]:
                    _merge(sgb.r, (s_, v_))

    HMAP = [0, 2, 1, 3]

    def swa_attention(g):
        T0 = g * 4
        items = [(qt, kvh) for qt in range(4) for kvh in range(4)]

        def sreg(p, cur, half):
            return ps[:, 4 + 2 * p + half, cur * 256:(cur + 1) * 256]

        def emit_scores(idx):
            qt, kvh = items[idx]
            T = T0 + qt
            p = idx % 2
            for cur in ([0, 1] if T >= 1 else [1]):
                kcols = (qt + cur) * 128
                for half in range(2):
                    r0 = half * 64
                    k.op(PE, lambda: nc.tensor.matmul(
                        out=sreg(p, cur, half), lhsT=KTs[r0:r0 + 64, kvh, kcols:kcols + 128],
                        rhs=big[r0:r0 + 64, 2 * kvh:2 * kvh + 2, qt * 128:(qt + 1) * 128], start=True, stop=True),
                         reads=[B_KTs[qt + cur]] + B_big[2 * kvh:2 * kvh + 2], writes=[B_bk[4 + 2 * p + half]], signal=True)

        def emit_pv(idx):
            qt, kvh = items[idx]
            T = T0 + qt
            has_prev = T >= 1
            p = idx % 2
            pb = idx % 2
            ob = idx % 2
            osw = ps[:, ob, 0:264].rearrange("p (h e) -> p h e", h=4)
            for s_ in range(4):
                head = kvh * 4 + HMAP[s_]
                hf, jj = s_ // 2, s_ % 2
                for cur in ([0, 1] if has_prev else [1]):
                    k.op(ACT, lambda: nc.scalar.activation(
                        out=pTs[:, pb, cur, s_, :], in_=sreg(p, cur, hf)[:, jj * 128:(jj + 1) * 128], func=AF.Exp,
                        scale=0.125, bias=biass[:, head, 1 - cur:2 - cur]),
                         reads=[B_bk[4 + 2 * p + hf]], writes=[B_pTs[pb]], waw=False)
            if has_prev:
                k.op(DVE, lambda: nc.vector.tensor_tensor(out=pTs[:, pb, 0, :, :], in0=pTs[:, pb, 0, :, :], in1=maskP4[:],
                                                          op=ALU.mult), reads=[B_pTs[pb]], writes=[B_pTs[pb]])
            k.op(DVE, lambda: nc.vector.tensor_tensor(out=pTs[:, pb, 1, :, :], in0=pTs[:, pb, 1, :, :], in1=maskC4[:],
                                                      op=ALU.mult), reads=[B_pTs[pb]], writes=[B_pTs[pb]])
            first = True
            for s_ in range(4):
                i = HMAP[s_]
                if has_prev:
                    k.op(PE, lambda: nc.tensor.matmul(out=osw[:, i, :], lhsT=pTs[:, pb, 0, s_, :],
                                                      rhs=Vs[:, qt, kvh, :], start=first, stop=False,
                                                      skip_group_check=True),
                         reads=[B_pTs[pb], B_Vs[qt]], writes=[B_bk[ob]], signal=False)
                    first = False
                k.op(PE, lambda: nc.tensor.matmul(out=osw[:, i, :], lhsT=pTs[:, pb, 1, s_, :],
                                                  rhs=Vs[:, qt + 1, kvh, :], start=first, stop=True,
                                                  skip_group_check=True),
                     reads=[B_pTs[pb], B_Vs[qt + 1]], writes=[B_bk[ob]], signal=(s_ == 3))
                first = False
            dd = den[:, ob, :]
            rd = rden[:, ob, :]
            k.op(DVE, lambda: nc.vector.tensor_tensor(out=dd, in0=osw[:, :, 64], in1=sinkp[:, kvh * 4:kvh * 4 + 4],
                                                      op=ALU.add), reads=[B_bk[ob]], writes=[B_den[ob]])
            k.op(DVE, lambda: nc.vector.reciprocal(out=rd, in_=dd), reads=[B_den[ob]], writes=[B_rden[ob]])
            k.op(DVE, lambda: nc.vector.tensor_tensor(
                out=obf[:, kvh * 256:(kvh + 1) * 256].rearrange("p (h e) -> p h e", h=4), in0=osw[:, :, 0:64],
                in1=rd.unsqueeze(2).broadcast_to([128, 4, 64]), op=ALU.mult),
                 reads=[B_bk[ob], B_rden[ob]], writes=[B_obf], waw=(kvh == 0))
            if kvh == 3:
                transpose_out(qt, 16)

        n = len(items)
        deferred = []
        emit_scores(0)
        for i in range(n):
            if i + 1 < n:
                emit_scores(i + 1)
            pend = list(deferred)
            del deferred[:]
            emit_pv(i)
            for f_ in pend:
                f_()
        for f_ in deferred:
            f_()

    setup()
    for g in range(NG):
        group(g)
    for t in range(4):
        k.wait(SP, S_o[t], S_o[t].cnt)
    if DEBUG_DUMP:
        tens = {"hT": hT, "big": big, "KTd": KTd, "Vd": Vd, "KTs": KTs, "Vs": Vs, "xg": xg, "biasd": biasd,
                "sinkp": sinkp, "nlam": nlam, "maskC4": maskC4, "maskP4": maskP4, "ident": ident, "rstd": rstd,
                "xb": xb, "ost": ost, "obf": obf}
        for E in [PE, ACT, DVE, POOL]:
            k.wait(SP, E.sem, E.sem.cnt)
        for i_ in range(3):
            k.wait(SP, S_wr[i_], S_wr[i_].cnt)
        S_d = SemC(nc, "s_dbg")
        for name in DEBUG_DUMP:
            tt = tens[name]
            shp = list(tt.shape)
            flat = 1
            for d_ in shp[1:]:
                flat *= d_
            dd_ = nc.dram_tensor("dbg_" + name, [128, flat], tt.dtype, kind="ExternalOutput").ap()
            letters = "abcde"[:len(shp) - 1]
            src = tt[:] if len(shp) == 2 else tt[:].rearrange("p " + " ".join(letters) + " -> p (" + " ".join(letters) + ")")
            k.op(SP, lambda: nc.sync.dma_start(out=dd_, in_=src), dsem=S_d)
        k.wait(SP, S_d, S_d.cnt)
    return nc


_NC_CACHE = {}


def kernel(x, attn_norm_w, w_in, lambda_q1, lambda_k1, lambda_q2, lambda_k2, subln_w, sinks, w_out,
           ffn_norm_w, w_gate, w_up, w_down, final_norm_w):
    f = lambda a: np.ascontiguousarray(np.asarray(a, dtype=np.float32))
    x = f(x)
    shared = {
        "w_in": f(w_in)[0], "w_out": f(w_out)[0], "w_gate": f(w_gate)[0], "w_up": f(w_up)[0], "w_down": f(w_down)[0],
        "attn_norm_w": f(attn_norm_w)[0], "ffn_norm_w": f(ffn_norm_w)[0], "final_norm_w": f(final_norm_w),
        "lambda_q1": f(lambda_q1)[0], "lambda_k1": f(lambda_k1)[0], "lambda_q2": f(lambda_q2)[0],
        "lambda_k2": f(lambda_k2)[0], "subln_w": f(subln_w)[0], "sinks": f(sinks)[0],
    }
    if "nc" not in _NC_CACHE:
        _NC_CACHE["nc"] = build_nc()
    nc = _NC_CACHE["nc"]
    in_maps = [dict(shared, x=x[b]) for b in range(8)]
    res = run_bass_kernel_spmd(nc, in_maps, core_ids=list(range(8)))
    return np.stack([np.asarray(res.results[b]["out"], dtype=np.float32) for b in range(8)], axis=0)
```

```python
import math
import numpy as np
import concourse.bass as bass
import concourse.mybir as mybir
from concourse.bass_utils import run_bass_kernel_spmd

F32 = mybir.dt.float32
BF16 = mybir.dt.bfloat16
AF = mybir.ActivationFunctionType
ALU = mybir.AluOpType
AX = mybir.AxisListType

S = 2048
D = 2048
DFF = 5632
NG = 4
GT = 512
EPS = 1e-5
LAMBDA_INIT = 0.8 - 0.6 * math.exp(-0.3 * 0)
SLOPE_D = [2.0 ** (-8.0 * (h + 1) / 8) for h in range(8)]
SLOPE_S = [2.0 ** (-8.0 * (h + 1) / 16) for h in range(16)]
NEG = -30000.0
LOOKAHEAD = 4
DEBUG_STOP = None
DEBUG_DUMP = []
DEBUG_MAXBLK = None


class SemC:
    def __init__(self, nc, name):
        self.h = nc.alloc_semaphore(name)
        self.cnt = 0


class Eng:
    def __init__(self, nc, e, name, inorder_safe=False):
        self.e = e
        self.name = name
        self.sem = SemC(nc, "prog_" + name)
        self.seen = {}
        self.inorder_safe = inorder_safe


class Buf:
    __slots__ = ("name", "w", "r")

    def __init__(self, name):
        self.name = name
        self.w = {}
        self.r = {}


def _merge(d, tok):
    s, v = tok
    if d.get(s, 0) < v:
        d[s] = v


class K:
    def __init__(self, nc):
        self.nc = nc
        self.PE = Eng(nc, nc.tensor, "pe", inorder_safe=True)
        self.ACT = Eng(nc, nc.scalar, "act")
        self.DVE = Eng(nc, nc.vector, "dve")
        self.POOL = Eng(nc, nc.gpsimd, "pool")
        self.SP = Eng(nc, nc.sync, "sp")
        self.nwait = 0

    def wait(self, E, sem, val):
        if sem is E.sem and E.inorder_safe:
            return
        if E.seen.get(sem, 0) >= val:
            return
        E.e.wait_ge(sem.h, val)
        E.seen[sem] = val
        self.nwait += 1

    def op(self, E, fn, reads=(), writes=(), signal=True, dsem=None, waw=True):
        deps = {}
        for b in reads:
            for s, v in b.w.items():
                _merge(deps, (s, v))
        for b in writes:
            for s, v in b.r.items():
                _merge(deps, (s, v))
            if waw or b.r:
                for s, v in b.w.items():
                    _merge(deps, (s, v))
        for s, v in deps.items():
            self.wait(E, s, v)
        ins = fn()
        if dsem is not None:
            dsem.cnt += 16
            ins.then_inc(dsem.h, 16)
            tok = (dsem, dsem.cnt)
        elif signal:
            E.sem.cnt += 1
            ins.then_inc(E.sem.h, 1)
            tok = (E.sem, E.sem.cnt)
        else:
            tok = (E.sem, E.sem.cnt + 1)
        for b in writes:
            if b.r:
                b.r = {}
                b.w = {}
            _merge(b.w, tok)
        for b in reads:
            _merge(b.r, tok)
        return tok


def build_nc():
    nc = bass.Bass("TRN2", target_bir_lowering=False)
    k = K(nc)
    PE, ACT, DVE, POOL, SP = k.PE, k.ACT, k.DVE, k.POOL, k.SP

    def din(name, shape):
        return nc.dram_tensor(name, shape, F32, kind="ExternalInput").ap()

    x = din("x", [S, D])
    w_in = din("w_in", [D, 4608])
    w_out = din("w_out", [D, D])
    w_gate = din("w_gate", [D, DFF])
    w_up = din("w_up", [D, DFF])
    w_down = din("w_down", [DFF, D])
    attn_norm_w = din("attn_norm_w", [D])
    ffn_norm_w = din("ffn_norm_w", [D])
    final_norm_w = din("final_norm_w", [D])
    lq1 = din("lambda_q1", [64])
    lk1 = din("lambda_k1", [64])
    lq2 = din("lambda_q2", [64])
    lk2 = din("lambda_k2", [64])
    subln_w = din("subln_w", [128])
    sinks = din("sinks", [16])
    out = nc.dram_tensor("out", [S, D], F32, kind="ExternalOutput").ap()

    w_in_v = w_in.rearrange("(k p) n -> p k n", p=128)
    w_out_v = w_out.rearrange("(k p) n -> p k n", p=128)
    w_gate_v = w_gate.rearrange("(k p) n -> p k n", p=128)
    w_up_v = w_up.rearrange("(k p) n -> p k n", p=128)
    w_down_v = w_down.rearrange("(k p) n -> p k n", p=128)

    def sb(name, shape, dt):
        return nc.alloc_sbuf_tensor(name, shape, dt)

    xg = sb("xg", [128, 4, 2048], F32)
    hT = sb("hT", [128, 16, 512], BF16)
    big = sb("big", [128, 24, 512], BF16)
    KTd = sb("KTd", [128, 8, 2048], BF16)
    Vd = sb("Vd", [128, 16, 8, 130], BF16)
    KTs = sb("KTs", [128, 4, 640], BF16)
    Vs = sb("Vs", [128, 5, 4, 66], BF16)
    wr = sb("wr", [128, 3, 4096], BF16)
    wbc = sb("wbc", [128, 2048], F32)
    xb = sb("xb", [128, 2048], BF16)
    sg = sb("sg", [128, 4, 512], F32)
    pTs = sb("pTs", [128, 2, 2, 4, 128], BF16)
    pT = pTs[:].rearrange("p a b c q -> p (a b c) q")
    ost = sb("ost", [128, 8, 128], F32)
    obf = sb("obf", [128, 1024], BF16)
    ident = sb("ident", [128, 128], BF16)
    maskC4 = sb("maskC4", [128, 4, 128], BF16)
    maskP4 = sb("maskP4", [128, 4, 128], BF16)
    Tt = sb("Tt", [128, 16], F32)
    Tu = sb("Tu", [128, 16], F32)
    varg = sb("varg", [128, 5, 16], F32)
    qbias = sb("qbias", [128, 5, 16], F32)
    vfac = sb("vfac", [128, 5, 16], F32)
    biasd = sb("biasd", [128, 8, 16], F32)
    biass = sb("biass", [128, 16, 2], F32)
    sinkbc = sb("sinkbc", [128, 16], F32)
    sinkp = sb("sinkp", [128, 16], F32)
    sublnbc = sb("sublnbc", [128, 128], F32)
    lam4 = sb("lam4", [128, 4, 64], F32)
    lsm = sb("lsm", [128, 8], F32)
    nlam = sb("nlam", [128, 1], F32)
    nh = sb("nh", [128, 8], F32)
    epsc = sb("epsc", [128, 1], F32)
    ss = sb("ss", [128, 4], F32)
    vv = sb("vv", [128, 4], F32)
    rstd = sb("rstd", [128, 4], F32)
    ssq = sb("ssq", [128, 8], F32)
    vq = sb("vq", [128, 8], F32)
    rq = sb("rq", [128, 8], F32)
    rr = sb("rr", [128, 4], F32)
    den = sb("den", [128, 2, 4], F32)
    rden = sb("rden", [128, 2, 4], F32)

    ps = nc.alloc_psum_tensor("ps", [128, 8, 512], F32)

    B_xg = [Buf(f"xg{t}") for t in range(4)]
    B_hT = [Buf(f"hT{t}") for t in range(4)]
    B_big = [Buf(f"big{c}") for c in range(24)]
    B_KTd = [[Buf(f"KTd{h}_{g}") for g in range(NG)] for h in range(8)]
    B_Vd = [Buf(f"Vd{t}") for t in range(16)]
    B_KTs = [Buf(f"KTs{t}") for t in range(5)]
    B_Vs = [Buf(f"Vs{t}") for t in range(5)]
    B_wr = [Buf(f"wr{i}") for i in range(3)]
    S_wr = [SemC(nc, f"s_wr{i}") for i in range(3)]
    B_wbc = Buf("wbc")
    S_wbc = SemC(nc, "s_wbc")
    B_xb = Buf("xb")
    B_sg = [Buf(f"sg{i}") for i in range(4)]
    B_pT = [Buf(f"pT{i}") for i in range(16)]
    B_pTs = [Buf("pTs0"), Buf("pTs1")]
    B_ost = Buf("ost")
    B_obf = Buf("obf")
    B_junk = Buf("junk")
    B_const = Buf("const")
    B_stat = Buf("stat")
    B_qstat = Buf("qstat")
    B_den = [Buf("den0"), Buf("den1")]
    B_rden = [Buf("rden0"), Buf("rden1")]
    B_vv, B_rstd = Buf("vv"), Buf("rstd")
    B_bk = [Buf(f"bank{b}") for b in range(8)]
    B_ps = B_bk[0:4]
    S_x = [SemC(nc, f"s_x{t}") for t in range(4)]
    S_o = [SemC(nc, f"s_o{t}") for t in range(4)]
    S_set = SemC(nc, "s_set")
    S_set2 = SemC(nc, "s_set2")
    S_set3 = SemC(nc, "s_set3")

    def setup():
        Bi, Bm, Bt, Bn, Bl, Bsk, Bsu = Buf("i"), Buf("m"), Buf("t"), Buf("n"), Buf("l"), Buf("sk"), Buf("su")
        k.op(POOL, lambda: nc.gpsimd.memset(ident[:], 0.0), writes=[Bi])
        k.op(POOL, lambda: nc.gpsimd.affine_select(out=ident[:], in_=ident[:], compare_op=ALU.not_equal,
                                                   fill=1.0, base=0, pattern=[[-1, 128]], channel_multiplier=1),
             reads=[Bi], writes=[Bi])
        k.op(POOL, lambda: nc.gpsimd.memset(maskC4[:], 1.0), writes=[Bm])
        k.op(POOL, lambda: nc.gpsimd.affine_select(out=maskC4[:], in_=maskC4[:], compare_op=ALU.is_ge,
                                                   fill=0.0, base=0, pattern=[[0, 4], [1, 128]],
                                                   channel_multiplier=-1), reads=[Bm], writes=[Bm])
        k.op(POOL, lambda: nc.gpsimd.memset(maskP4[:], 1.0), writes=[Bn])
        k.op(POOL, lambda: nc.gpsimd.affine_select(out=maskP4[:], in_=maskP4[:], compare_op=ALU.is_ge,
                                                   fill=0.0, base=-1, pattern=[[0, 4], [-1, 128]],
                                                   channel_multiplier=1), reads=[Bn], writes=[Bn])
        k.op(POOL, lambda: nc.gpsimd.iota(Tt[:], pattern=[[-128, 16]], base=-64, channel_multiplier=1, allow_small_or_imprecise_dtypes=True),
             writes=[Bt])
        k.op(POOL, lambda: nc.gpsimd.memset(nh[:], -0.5), writes=[Buf("x")])
        k.op(POOL, lambda: nc.gpsimd.memset(epsc[:], EPS), writes=[Buf("x")])
        Bvones = Buf("vones")
        k.op(POOL, lambda: nc.gpsimd.memset(Vd[:, :, :, 128:130], 1.0), writes=[Bvones])
        k.op(POOL, lambda: nc.gpsimd.memset(Vs[:, :, :, 64:66], 1.0), writes=[Buf("x")])
        for i, v in enumerate([lq1, lk1, lq2, lk2]):
            k.op(SP, lambda: nc.sync.dma_start(out=lam4[:, i, :], in_=v.partition_broadcast(128)),
                 writes=[Bl], dsem=S_set, waw=False)
        k.op(SP, lambda: nc.sync.dma_start(out=sinkbc[:], in_=sinks.partition_broadcast(128)),
             writes=[Bsk], dsem=S_set2, waw=False)
        k.op(SP, lambda: nc.sync.dma_start(out=sublnbc[:], in_=subln_w.partition_broadcast(128)),
             writes=[Bsu], dsem=S_set3, waw=False)
        Bu, Bva, Bvf = Buf("u"), Buf("va"), Buf("vf")
        k.op(POOL, lambda: nc.gpsimd.iota(Tu[:], pattern=[[128, 16]], base=-1024, channel_multiplier=1,
                                          allow_small_or_imprecise_dtypes=True), writes=[Bu])
        for hh in range(5):
            k.op(DVE, lambda: nc.vector.tensor_scalar(out=varg[:, hh, :], in0=Tu[:], scalar1=SLOPE_D[3 + hh], scalar2=None,
                                                      op0=ALU.mult), reads=[Bu], writes=[Bva], waw=False)
        k.op(ACT, lambda: nc.scalar.activation(out=vfac[:], in_=varg[:], func=AF.Exp), reads=[Bva], writes=[Bvf])
        Bq = Buf("q")
        k.op(POOL, lambda: nc.gpsimd.iota(qbias[:, 0, :], pattern=[[128, 16]], base=64 - 1024, channel_multiplier=0,
                                          allow_small_or_imprecise_dtypes=True), writes=[Bq])
        for hh in range(1, 5):
            k.op(POOL, lambda: nc.gpsimd.tensor_scalar(out=qbias[:, hh, :], in0=qbias[:, 0, :], scalar1=-SLOPE_D[3 + hh],
                                                       scalar2=None, op0=ALU.mult), reads=[Bq], writes=[Buf("x")])
        k.op(POOL, lambda: nc.gpsimd.tensor_scalar(out=qbias[:, 0, :], in0=qbias[:, 0, :], scalar1=-SLOPE_D[3],
                                                   scalar2=None, op0=ALU.mult), reads=[Bq], writes=[Bq])
        for hh in range(5):
            k.op(DVE, lambda: nc.vector.tensor_copy(out=Vd[:, :, 3 + hh, 128:130],
                                                    in_=vfac[:, hh, :].unsqueeze(2).broadcast_to([128, 16, 2])),
                 reads=[Bvf, Bvones], writes=[Buf("x")])
        for h in range(8):
            k.op(DVE, lambda: nc.vector.tensor_scalar(out=biasd[:, h, :], in0=Tt[:], scalar1=SLOPE_D[h],
                                                      scalar2=None, op0=ALU.mult), reads=[Bt], writes=[Buf("x")])
        for h in range(16):
            k.op(DVE, lambda: nc.vector.tensor_scalar(out=biass[:, h, :], in0=Tt[:, 0:2], scalar1=SLOPE_S[h],
                                                      scalar2=None, op0=ALU.mult), reads=[Bt], writes=[Buf("x")])
        B_l2, B_l3, B_l4, B_l5 = Buf("l2"), Buf("l3"), Buf("l4"), Buf("l5")
        k.op(DVE, lambda: nc.vector.tensor_tensor(out=lam4[:, 0, :], in0=lam4[:, 0, :], in1=lam4[:, 1, :], op=ALU.mult),
             reads=[Bl], writes=[B_l2], waw=False)
        k.op(DVE, lambda: nc.vector.tensor_tensor(out=lam4[:, 2, :], in0=lam4[:, 2, :], in1=lam4[:, 3, :], op=ALU.mult),
             reads=[Bl], writes=[B_l2], waw=False)
        k.op(DVE, lambda: nc.vector.tensor_reduce(out=lsm[:, 0:1], in_=lam4[:, 0, :], axis=AX.X, op=ALU.add),
             reads=[B_l2], writes=[B_l3], waw=False)
        k.op(DVE, lambda: nc.vector.tensor_reduce(out=lsm[:, 1:2], in_=lam4[:, 2, :], axis=AX.X, op=ALU.add),
             reads=[B_l2], writes=[B_l3], waw=False)
        k.op(ACT, lambda: nc.scalar.activation(out=lsm[:, 2:4], in_=lsm[:, 0:2], func=AF.Exp),
             reads=[B_l3], writes=[B_l4])
        k.op(DVE, lambda: nc.vector.tensor_tensor(out=lsm[:, 4:5], in0=lsm[:, 3:4], in1=lsm[:, 2:3], op=ALU.subtract),
             reads=[B_l4], writes=[B_l5])
        k.op(DVE, lambda: nc.vector.tensor_scalar(out=nlam[:], in0=lsm[:, 4:5], scalar1=-LAMBDA_INIT, scalar2=None,
                                                  op0=ALU.add), reads=[B_l5], writes=[Buf("x")])
        k.op(DVE, lambda: nc.vector.tensor_scalar(out=sublnbc[:], in0=sublnbc[:], scalar1=1.0 - LAMBDA_INIT,
                                                  scalar2=None, op0=ALU.mult), reads=[Bsu], writes=[Bsu])
        for h in range(16):
            k.op(ACT, lambda: nc.scalar.activation(out=sinkp[:, h:h + 1], in_=Tt[:, 0:1], func=AF.Exp,
                                                   scale=SLOPE_S[h], bias=sinkbc[:, h:h + 1]),
                 reads=[Bt, Bsk], writes=[Buf("x")])
        engs = [PE, ACT, DVE, POOL, SP]
        for E in engs:
            for Fe in engs:
                if Fe is not E and Fe.sem.cnt > 0:
                    k.wait(E, Fe.sem, Fe.sem.cnt)
            for ss_ in (S_set, S_set2, S_set3):
                k.wait(E, ss_, ss_.cnt)

    ring_state = {"i": 0, "g": 0, "pid": 0}
    NPIECE = 95
    wcache = nc.dram_tensor("wcache", [NPIECE, 128, 4096], BF16, kind="Internal").ap()
    B_wc = [Buf(f"wc{i}") for i in range(NPIECE)]
    S_ws = [SemC(nc, f"s_ws{i}") for i in range(3)]
    S_wrh = [SemC(nc, f"s_wrh{i}") for i in range(3)]

    def cache_group(pid):
        return 0 if pid % 3 == 0 else 1

    def ring_next():
        i = ring_state["i"] % 3
        ring_state["i"] += 1
        pid = ring_state["pid"]
        ring_state["pid"] += 1
        return i, pid

    def cache_store(i, pid, nel):
        k.op(SP, lambda: nc.sync.dma_start(out=wcache[pid, :, 0:nel], in_=wr[:, i, 0:nel]),
             reads=[B_wr[i]], writes=[B_wc[pid]], dsem=S_ws[i])

    def cache_load(i, pid, nel):
        k.op(SP, lambda: nc.sync.dma_start(out=wr[:, i, 0:nel], in_=wcache[pid, :, 0:nel]),
             reads=[B_wc[pid]], writes=[B_wr[i]], dsem=S_wrh[i])

    def load_piece(src_view, k0, k1, c0, c1):
        i, pid = ring_next()
        kk = k1 - k0
        n = c1 - c0
        assert kk * n <= 4096
        dst = wr[:, i, 0:kk * n].rearrange("p (k n) -> p k n", k=kk)
        gfill = cache_group(pid)
        if ring_state["g"] <= gfill:
            k.op(POOL, lambda: nc.gpsimd.dma_start(out=dst, in_=src_view[:, k0:k1, c0:c1]),
                 writes=[B_wr[i]], dsem=S_wr[i])
            if ring_state["g"] == gfill:
                cache_store(i, pid, kk * n)
        else:
            cache_load(i, pid, kk * n)
        return dst, B_wr[i]

    bank_state = {"i": 0}

    def next_bank():
        b = bank_state["i"] % 4
        bank_state["i"] += 1
        return b

    evac_state = {"i": 0}

    def evac_copy(out_ap, in_ap, reads, writes, waw=True, eng=None):
        evac_state["i"] += 1
        if (evac_state["i"] % 2 == 0 and eng is None) or eng == "act":
            return k.op(ACT, lambda: nc.scalar.activation(out=out_ap, in_=in_ap, func=AF.Copy),
                        reads=reads, writes=writes, waw=waw)
        return k.op(DVE, lambda: nc.vector.tensor_copy(out=out_ap, in_=in_ap), reads=reads, writes=writes, waw=waw)

    def load_wbc(vec):
        k.op(SP, lambda: nc.sync.dma_start(out=wbc[:], in_=vec.partition_broadcast(128)), writes=[B_wbc], dsem=S_wbc)

    B_statt = [Buf(f"stat{t}") for t in range(4)]
    B_vvt = [Buf(f"vv{t}") for t in range(4)]
    B_rstdt = [Buf(f"rstd{t}") for t in range(4)]
    B_xbh = [Buf("xbh0"), Buf("xbh1")]

    def stats_tile(t):
        k.op(ACT, lambda: nc.scalar.activation(out=xb[:], in_=xg[:, t, :], func=AF.Square, accum_out=ss[:, t:t + 1]),
             reads=[B_xg[t]], writes=B_xbh + [B_statt[t]])
        k.op(ACT, lambda: nc.scalar.activation(out=vv[:, t:t + 1], in_=ss[:, t:t + 1], func=AF.Sqrt, scale=1.0 / D,
                                               bias=epsc[:]), reads=[B_statt[t]], writes=[B_vvt[t]])
        k.op(DVE, lambda: nc.vector.reciprocal(out=rstd[:, t:t + 1], in_=vv[:, t:t + 1]),
             reads=[B_vvt[t]], writes=[B_rstdt[t]])

    def norm_to_hT():
        for t in range(4):
            stats_tile(t)
        for t in range(4):
            for r in range(2):
                cs = slice(r * 1024, (r + 1) * 1024)
                k.op(DVE, lambda: nc.vector.scalar_tensor_tensor(out=xb[:, cs], in0=xg[:, t, cs], scalar=rstd[:, t:t + 1],
                                                                 in1=wbc[:, cs], op0=ALU.mult, op1=ALU.mult),
                     reads=[B_xg[t], B_rstdt[t], B_wbc], writes=[B_xbh[r]])
                b = next_bank()
                tp = ps[:, b, :].bitcast(BF16).rearrange("p (c n) -> p c n", c=8)
                for c in range(8):
                    dc = r * 8 + c
                    k.op(PE, lambda: nc.tensor.transpose(out=tp[:, c, :], in_=xb[:, dc * 128:(dc + 1) * 128],
                                                         identity=ident[:]),
                         reads=[B_xbh[r]], writes=[B_ps[b]], signal=(c == 7))
                evac_copy(hT[:, r * 8:(r + 1) * 8, t * 128:(t + 1) * 128], tp, reads=[B_ps[b]], writes=[B_hT[t]], waw=False)

    ss2 = sb("ss2", [128, 4], F32)
    vv2 = sb("vv2", [128, 4], F32)
    rstd2 = sb("rstd2", [128, 4], F32)
    B_e1 = [Buf(f"e1_{t}") for t in range(4)]
    B_e2 = [Buf(f"e2_{t}") for t in range(4)]
    B_e3 = [Buf(f"e3_{t}") for t in range(4)]
    S_xe = SemC(nc, "s_xe")
    en_state = {"i": 0}

    def early_norm_chain(gn, t):
        sgx = sg[:].rearrange("p a n -> p (a n)")
        junk16 = ost[:].rearrange("p a e -> p (a e)").bitcast(BF16)
        row0 = (gn * 4 + t) * 128
        k.op(ACT, lambda: nc.scalar.dma_start(out=sgx, in_=x[row0:row0 + 128, :]), writes=B_sg, dsem=S_xe)
        k.op(ACT, lambda: nc.scalar.activation(out=junk16, in_=sgx, func=AF.Square, accum_out=ss2[:, t:t + 1]),
             reads=B_sg, writes=B_osth[0] + [B_e1[t]])
        k.op(ACT, lambda: nc.scalar.activation(out=vv2[:, t:t + 1], in_=ss2[:, t:t + 1], func=AF.Sqrt, scale=1.0 / D,
                                               bias=epsc[:]), reads=[B_e1[t]], writes=[B_e2[t]])
        k.op(DVE, lambda: nc.vector.reciprocal(out=rstd2[:, t:t + 1], in_=vv2[:, t:t + 1]),
             reads=[B_e2[t]], writes=[B_e3[t]])
        for r in range(2):
            cs = slice(r * 1024, (r + 1) * 1024)
            k.op(DVE, lambda: nc.vector.scalar_tensor_tensor(out=xb[:, cs], in0=sgx[:, cs], scalar=rstd2[:, t:t + 1],
                                                             in1=wbc[:, cs], op0=ALU.mult, op1=ALU.mult),
                 reads=B_sg + [B_e3[t], B_wbc], writes=[B_xbh[r]])

    def early_norm_transposes(t):
        for r in range(2):
            b = 4 + en_state["i"] % 2
            en_state["i"] += 1
            tp = ps[:, b, :].bitcast(BF16).rearrange("p (c n) -> p c n", c=8)
            for c in range(8):
                dc = r * 8 + c
                k.op(PE, lambda: nc.tensor.transpose(out=tp[:, c, :], in_=xb[:, dc * 128:(dc + 1) * 128], identity=ident[:]),
                     reads=[B_xbh[r]], writes=[B_bk[b]], signal=(c == 7))
            evac_copy(hT[:, r * 8:(r + 1) * 8, t * 128:(t + 1) * 128], tp, reads=[B_bk[b]], writes=[B_hT[t]], waw=False)

    def proj_feature_major(src_view, col0, nchunks, dest_fn, dest_bufs_fn):
        for s0 in range(0, nchunks, 4):
            nch = min(4, nchunks - s0)
            ncols = nch * 128
            kper = 4096 // ncols
            for k0 in range(0, 16, kper):
                piece, bw = load_piece(src_view, k0, k0 + kper, col0 + s0 * 128, col0 + s0 * 128 + ncols)
                for c in range(nch):
                    for kk in range(kper):
                        dc = k0 + kk
                        k.op(PE, lambda: nc.tensor.matmul(
                            out=ps[:, c, :], lhsT=piece[:, kk, c * 128:(c + 1) * 128], rhs=hT[:, dc, :],
                            start=(dc == 0), stop=(dc == 15)),
                             reads=[bw] + B_hT, writes=[B_ps[c]], signal=(kk == kper - 1))
            for c in range(nch):
                evac_copy(dest_fn(s0 + c), ps[:, c, :], reads=[B_ps[c]], writes=dest_bufs_fn(s0 + c))

    def proj_token_major(src_view, kchunks, col0, ncols, lhs_fn, lhs_bufs_fn, evac_fn):
        kper = 4096 // ncols
        pieces = [(a, min(a + kper, kchunks)) for a in range(0, kchunks, kper)]
        for (k0, k1) in pieces:
            piece, bw = load_piece(src_view, k0, k1, col0, col0 + ncols)
            for t in range(4):
                for kk in range(k0, k1):
                    last = (kk == kchunks - 1)
                    k.op(PE, lambda t=t, kk=kk, k0=k0, piece=piece: nc.tensor.matmul(
                        out=ps[:, t, 0:ncols], lhsT=lhs_fn(kk, t), rhs=piece[:, kk - k0, :],
                        start=(kk == 0), stop=(kk == kchunks - 1)),
                         reads=[bw] + lhs_bufs_fn(kk, t), writes=[B_ps[t]], signal=(kk == k1 - 1))
        for t in range(4):
            evac_fn(t)

    pending_io = []

    def group(g):
        T0 = g * 4
        ring_state["g"] = g
        ring_state["pid"] = 0
        def load_x():
            for t in range(4):
                k.op(SP, lambda t=t: nc.sync.dma_start(out=xg[:, t, :], in_=x[(T0 + t) * 128:(T0 + t + 1) * 128, :]),
                     writes=[B_xg[t]], dsem=S_x[t])

        if g == 0:
            load_x()
            load_wbc(attn_norm_w)
            norm_to_hT()
            load_wbc(ffn_norm_w)

        proj_feature_major(w_in_v, 0, 8, lambda c: big[:, c, :], lambda c: [B_big[c]])
        proj_feature_major(w_in_v, 1024, 8, lambda c: KTd[:, c, g * GT:(g + 1) * GT], lambda c: [B_KTd[c][g]])
        def evac_v(t, vb):
            tile_ = T0 + t
            pv = ps[:, t, :].rearrange("p (h e) -> p h e", h=4)
            if vb == 0:
                k.op(DVE, lambda: nc.vector.tensor_copy(out=Vd[:, tile_, 0:3, 0:128], in_=pv[:, 0:3, :]),
                     reads=[B_ps[t]], writes=[B_Vd[tile_]], waw=False)
                k.op(DVE, lambda: nc.vector.tensor_scalar(out=Vd[:, tile_, 3, 0:128], in0=pv[:, 3, :],
                                                          scalar1=vfac[:, 0, tile_:tile_ + 1], scalar2=None, op0=ALU.mult),
                     reads=[B_ps[t]], writes=[B_Vd[tile_]], waw=False)
            else:
                k.op(DVE, lambda: nc.vector.tensor_tensor(
                    out=Vd[:, tile_, 4:8, 0:128], in0=pv,
                    in1=vfac[:, 1:5, tile_].unsqueeze(2).broadcast_to([128, 4, 128]), op=ALU.mult),
                     reads=[B_ps[t]], writes=[B_Vd[tile_]], waw=False)

        for vb in range(2):
            proj_token_major(
                w_in_v, 16, 2048 + vb * 512, 512,
                lambda kk, t: hT[:, kk, t * 128:(t + 1) * 128], lambda kk, t: [B_hT[t]],
                lambda t, vb=vb: evac_v(t, vb))
        if DEBUG_STOP == "qkv":
            return
        if g > 0:
            for f_ in pending_io:
                f_()
            del pending_io[:]
            load_x()
            load_wbc(ffn_norm_w)
        diff_attention(g)
        if DEBUG_STOP == "dattn":
            return
        if g > 0:
            k.op(DVE, lambda: nc.vector.tensor_copy(out=KTs[:, :, 0:128], in_=KTs[:, :, 512:640]),
                 reads=[B_KTs[4]], writes=[B_KTs[0]])
            k.op(DVE, lambda: nc.vector.tensor_copy(out=Vs[:, 0, :, :], in_=Vs[:, 4, :, :]),
                 reads=[B_Vs[4]], writes=[B_Vs[0]])
        proj_feature_major(w_in_v, 3072, 8, lambda c: big[:, c, :], lambda c: [B_big[c]])
        for kh in range(2):
            i, pid = ring_next()
            dst = wr[:, i, :].rearrange("p (k h u e) -> p k h u e", k=8, h=4, u=2)
            srcv = w_in_v[:, kh * 8:kh * 8 + 8, 4096:4352].rearrange("p k (h e) -> p k h e", h=4)
            if g <= cache_group(pid):
                for u in range(2):
                    for hh in range(4):
                        k.op(POOL, lambda: nc.gpsimd.dma_start(out=dst[:, :, hh, u, :], in_=srcv[:, :, hh, :]),
                             writes=[B_wr[i]], dsem=S_wr[i], waw=(u == 0 and hh == 0))
                if g == cache_group(pid):
                    cache_store(i, pid, 4096)
            else:
                cache_load(i, pid, 4096)
            piece = wr[:, i, :].rearrange("p (k n) -> p k n", k=8)
            for kvh in range(4):
                for kk in range(8):
                    dc = kh * 8 + kk
                    k.op(PE, lambda: nc.tensor.matmul(out=ps[:, kvh, :], lhsT=piece[:, kk, kvh * 128:(kvh + 1) * 128],
                                                      rhs=hT[:, dc, :], start=(dc == 0), stop=(dc == 15)),
                         reads=[B_wr[i]] + B_hT, writes=[B_ps[kvh]], signal=(kk == 7))
        for kvh in range(4):
            evac_copy(KTs[:, kvh, 128:640], ps[:, kvh, :], reads=[B_ps[kvh]], writes=B_KTs[1:5], waw=False)
        proj_token_major(
            w_in_v, 16, 4352, 256,
            lambda kk, t: hT[:, kk, t * 128:(t + 1) * 128], lambda kk, t: [B_hT[t]],
            lambda t: evac_copy(Vs[:, 1 + t, :, 0:64], ps[:, t, 0:256].rearrange("p (h e) -> p h e", h=4),
                                reads=[B_ps[t]], writes=[B_Vs[1 + t]], waw=False))
        if DEBUG_STOP == "sproj":
            return
        swa_attention(g)
        if DEBUG_STOP == "attn":
            return
        for db in range(4):
            proj_token_major(
                w_out_v, 16, db * 512, 512,
                lambda kk, t: big[:, 8 + kk, t * 128:(t + 1) * 128], lambda kk, t: [B_big[8 + kk]],
                lambda t, db=db: k.op(DVE, lambda: nc.vector.tensor_tensor(
                    out=xg[:, t, db * 512:(db + 1) * 512], in0=ps[:, t, :], in1=xg[:, t, db * 512:(db + 1) * 512],
                    op=ALU.add), reads=[B_ps[t], B_xg[t]], writes=[B_xg[t]]))
        if DEBUG_STOP == "oproj":
            return
        norm_to_hT()
        for fh in range(2):
            f0 = fh * 22
            for s0 in range(0, 22, 4):
                nch = min(4, 22 - s0)
                ncols = nch * 128
                kper = 4096 // ncols
                c0 = (f0 + s0) * 128
                for (wv, bank0) in ((w_gate_v, 0), (w_up_v, 4)):
                    for k0 in range(0, 16, kper):
                        piece, bw = load_piece(wv, k0, k0 + kper, c0, c0 + ncols)
                        for c in range(nch):
                            for kk in range(kper):
                                dc = k0 + kk
                                k.op(PE, lambda: nc.tensor.matmul(
                                    out=ps[:, bank0 + c, :], lhsT=piece[:, kk, c * 128:(c + 1) * 128], rhs=hT[:, dc, :],
                                    start=(dc == 0), stop=(dc == 15)),
                                     reads=[bw] + B_hT, writes=[B_bk[bank0 + c]], signal=(kk == kper - 1))
                    if bank0 == 0:
                        for c in range(nch):
                            k.op(ACT, lambda: nc.scalar.activation(out=sg[:, c, :], in_=ps[:, c, :], func=AF.Silu),
                                 reads=[B_bk[c]], writes=[B_sg[c]])
                for c in range(nch):
                    k.op(DVE, lambda: nc.vector.tensor_tensor(out=big[:, s0 + c, :], in0=sg[:, c, :], in1=ps[:, 4 + c, :],
                                                              op=ALU.mult),
                         reads=[B_sg[c], B_bk[4 + c]], writes=[B_big[s0 + c]])
            early = (fh == 1 and g + 1 < NG)
            if early:
                load_wbc(attn_norm_w)
            for db in range(4):
                if early:
                    if db > 0:
                        early_norm_transposes(db - 1)
                    early_norm_chain(g + 1, db)
                proj_token_major(
                    w_down_v[:, f0:f0 + 22, :], 22, db * 512, 512,
                    lambda kk, t: big[:, kk, t * 128:(t + 1) * 128], lambda kk, t: [B_big[kk]],
                    lambda t, db=db: k.op(DVE, lambda: nc.vector.tensor_tensor(
                        out=xg[:, t, db * 512:(db + 1) * 512], in0=ps[:, t, :], in1=xg[:, t, db * 512:(db + 1) * 512],
                        op=ALU.add), reads=[B_ps[t], B_xg[t]], writes=[B_xg[t]]))
        if g + 1 < NG:
            early_norm_transposes(3)
        load_wbc(final_norm_w)
        for t in range(4):
            stats_tile(t)
        for t in range(4):
            k.op(DVE, lambda t=t: nc.vector.scalar_tensor_tensor(out=xg[:, t, :], in0=xg[:, t, :], scalar=rstd[:, t:t + 1],
                                                                 in1=wbc[:], op0=ALU.mult, op1=ALU.mult),
                 reads=[B_rstdt[t], B_wbc], writes=[B_xg[t]])
            pending_io.append(lambda t=t: k.op(
                SP, lambda: nc.sync.dma_start(out=out[(T0 + t) * 128:(T0 + t + 1) * 128, :], in_=xg[:, t, :]),
                reads=[B_xg[t]], dsem=S_o[t]))
        if g == NG - 1:
            for f_ in pending_io:
                f_()
            del pending_io[:]

    B_rr, B_rr2 = Buf("rr"), Buf("rr2")
    B_osth = [[Buf(f"ost{q}_{h}") for h in range(8)] for q in range(2)]
    B_ssq, B_vq, B_rq = Buf("ssq"), Buf("vq"), Buf("rq")
    tp_state = {"i": 0}

    def transpose_out(qt, chunk0, bank=None):
        if bank is None:
            b = 2 + tp_state["i"] % 2
            tp_state["i"] += 1
        else:
            b = bank
        tp = ps[:, b, :].bitcast(BF16).rearrange("p (c n) -> p c n", c=8)
        for c in range(8):
            k.op(PE, lambda: nc.tensor.transpose(out=tp[:, c, :], in_=obf[:, c * 128:(c + 1) * 128], identity=ident[:]),
                 reads=[B_obf], writes=[B_ps[b]], signal=(c == 7))
        evac_copy(big[:, chunk0:chunk0 + 8, qt * 128:(qt + 1) * 128], tp, reads=[B_ps[b]],
                  writes=B_big[chunk0:chunk0 + 8], waw=False, eng="dve")

    def diff_attention(g):
        T0 = g * 4
        items = []
        for qt in range(4):
            for h in range(8):
                T = T0 + qt
                kmin = 0
                while kmin < T and SLOPE_D[h] * (128 * (T - kmin - 1) + 1) > 124.0:
                    kmin += 1
                for k0 in range(kmin, T + 1, 4):
                    items.append((qt, h, k0, min(k0 + 4, T + 1), kmin))
        if DEBUG_MAXBLK is not None:
            items = items[:DEBUG_MAXBLK]
        st = {"p": 0, "u": -1}
        ostv = [ost[:], sg[:, 2:4, :].rearrange("p a (b e) -> p (a b) e", e=128)]
        for i_ in range(16):
            for pb_ in B_pTs:
                for s_, v_ in list(pb_.r.items()) + list(pb_.w.items()):
                    _merge(B_pT[i_].r, (s_, v_))
        for h_ in range(8):
            for sgb in B_sg[2:4]:
                for s_, v_ in list(sgb.r.items()) + list(sgb.w.items()):
                    _merge(B_osth[1][h_].r, (s_, v_))

        def emit_scores(idx):
            qt, h, k0, k1, kmin = items[idx]
            p = idx % 3
            for kt in range(k0, k1):
                for m in range(2):
                    bk = 2 + 2 * p + m
                    j = kt - k0
                    k.op(PE, lambda: nc.tensor.matmul(out=ps[:, bk, j * 128:(j + 1) * 128],
                                                      lhsT=KTd[m * 64:(m + 1) * 64, h, kt * 128:(kt + 1) * 128],
                                                      rhs=big[m * 64:(m + 1) * 64, h, qt * 128:(qt + 1) * 128],
                                                      start=True, stop=True),
                         reads=[B_KTd[h][kt // 4], B_big[h]], writes=[B_bk[bk]], signal=True)

        def finish_unit(qt, h, ob):
            o1 = ps[:, ob, 0:130]
            o2 = ps[:, ob, 130:260]
            ov = ostv[qt % 2][:, h, :]
            bo = B_osth[qt % 2][h]
            k.op(DVE, lambda: nc.vector.reciprocal(out=rr[:, 0:1], in_=o1[:, 128:129]), reads=[B_bk[ob]], writes=[B_rr])
            k.op(DVE, lambda: nc.vector.reciprocal(out=rr[:, 1:2], in_=o2[:, 128:129]), reads=[B_bk[ob]], writes=[B_rr],
                 waw=False)
            k.op(DVE, lambda: nc.vector.tensor_tensor(out=rr[:, 2:3], in0=rr[:, 1:2], in1=nlam[:], op=ALU.mult),
                 reads=[B_rr], writes=[B_rr2])
            k.op(DVE, lambda: nc.vector.tensor_scalar(out=ov, in0=o1[:, 0:128], scalar1=rr[:, 0:1], scalar2=None,
                                                      op0=ALU.mult), reads=[B_bk[ob], B_rr], writes=[bo])
            k.op(DVE, lambda: nc.vector.scalar_tensor_tensor(out=ov, in0=o2[:, 0:128], scalar=rr[:, 2:3],
                                                             in1=ov, op0=ALU.mult, op1=ALU.add),
                 reads=[B_bk[ob], B_rr2, bo], writes=[bo])

        def finish_qtile(qt):
            ov = ostv[qt % 2]
            bos = B_osth[qt % 2]
            sqv = sg[:, 0:2, :].rearrange("p a (b e) -> p (a b) e", e=128)
            k.op(DVE, lambda: nc.vector.tensor_tensor(out=sqv, in0=ov, in1=ov, op=ALU.mult),
                 reads=bos, writes=B_sg[0:2])
            k.op(DVE, lambda: nc.vector.tensor_reduce(out=ssq[:], in_=sqv, axis=AX.X, op=ALU.add),
                 reads=B_sg[0:2], writes=[B_ssq])
            k.op(DVE, lambda: nc.vector.tensor_scalar(out=vq[:], in0=ssq[:], scalar1=1.0 / 128, scalar2=EPS,
                                                      op0=ALU.mult, op1=ALU.add), reads=[B_ssq], writes=[B_vq])
            k.op(POOL, lambda: nc.gpsimd.tensor_tensor(out=rq[:], in0=vq[:], in1=nh[:], op=ALU.pow),
                 reads=[B_vq], writes=[B_rq])
            k.op(POOL, lambda: nc.gpsimd.tensor_tensor(out=ov, in0=ov,
                                                       in1=rq[:].unsqueeze(2).broadcast_to([128, 8, 128]), op=ALU.mult),
                 reads=[B_rq] + bos, writes=bos)
            k.op(POOL, lambda: nc.gpsimd.tensor_tensor(out=obf[:].rearrange("p (h e) -> p h e", h=8), in0=ov,
                                                       in1=sublnbc[:].unsqueeze(1).broadcast_to([128, 8, 128]),
                                                       op=ALU.mult),
                 reads=bos, writes=[B_obf])
            def tr_():
                st["u"] += 1
                transpose_out(qt, 8, bank=st["u"] % 2)
            deferred.append([6, tr_])

        def emit_pv(idx):
            qt, h, k0, k1, kmin = items[idx]
            T = T0 + qt
            p = idx % 3
            if k0 == kmin:
                st["u"] += 1
                st["ob"] = st["u"] % 2
            ob = st["ob"]
            wide = h >= 3
            wbase = {}
            if wide:
                nkt = k1 - k0
                for m in range(2):
                    bk = 2 + 2 * p + m
                    if st["p"] % 4:
                        st["p"] += 4 - st["p"] % 4
                    base = st["p"] % 16
                    st["p"] += 4
                    wbase[m] = base
                    k.op(ACT, lambda: nc.scalar.activation(
                        out=pT[:, base:base + nkt, :], in_=ps[:, bk, 0:nkt * 128].rearrange("p (j q) -> p j q", q=128),
                        func=AF.Exp, scale=0.125, bias=qbias[:, h - 3, T:T + 1]),
                         reads=[B_bk[bk]], writes=B_pT[base:base + nkt])
            for kt in range(k0, k1):
                for m in range(2):
                    bk = 2 + 2 * p + m
                    j = kt - k0
                    if wide:
                        pi = wbase[m] + j
                    else:
                        pi = st["p"] % 16
                        st["p"] += 1
                        k.op(ACT, lambda: nc.scalar.activation(out=pT[:, pi, :], in_=ps[:, bk, j * 128:(j + 1) * 128],
                                                               func=AF.Exp, scale=0.125,
                                                               bias=biasd[:, h, T - kt:T - kt + 1]),
                             reads=[B_bk[bk]], writes=[B_pT[pi]])
                    if kt == T:
                        k.op(DVE, lambda: nc.vector.tensor_tensor(out=pT[:, pi, :], in0=pT[:, pi, :], in1=maskC4[:, 0, :],
                                                                  op=ALU.mult), reads=[B_pT[pi]], writes=[B_pT[pi]])
                    k.op(PE, lambda: nc.tensor.matmul(out=ps[:, ob, m * 130:(m + 1) * 130], lhsT=pT[:, pi, :],
                                                      rhs=Vd[:, kt, h, :], start=(kt == kmin and m == 0), stop=(kt == T),
                                                      skip_group_check=True),
                         reads=[B_pT[pi], B_Vd[kt]], writes=[B_bk[ob]], signal=True)
            if k1 == T + 1:
                finish_unit(qt, h, ob)
                if h == 7:
                    finish_qtile(qt)

        n = len(items)
        deferred = []
        for i in range(min(2, n)):
            emit_scores(i)
        for i in range(n):
            if i + 2 < n:
                emit_scores(i + 2)
            emit_pv(i)
            for d_ in list(deferred):
                d_[0] -= 1
                if d_[0] <= 0:
                    deferred.remove(d_)
                    d_[1]()
        for d_ in deferred:
            d_[1]()
        for i_ in range(16):
            for s_, v_ in list(B_pT[i_].r.items()) + list(B_pT[i_].w.items()):
                for pb_ in B_pTs:
                    _merge(pb_.r, (s_, v_))
        for h_ in range(8):
            for s_, v_ in list(B_osth[1][h_].r.items()) + list(B_osth[1][h_].w.items()):
                for sgb in B_sg[2:4]:
                    _merge(sgb.r, (s_, v_))

    HMAP = [0, 2, 1, 3]

    def swa_attention(g):
        T0 = g * 4
        items = [(qt, kvh) for qt in range(4) for kvh in range(4)]

        def sreg(p, cur, half):
            return ps[:, 4 + 2 * p + half, cur * 256:(cur + 1) * 256]

        def emit_scores(idx):
            qt, kvh = items[idx]
            T = T0 + qt
            p = idx % 2
            for cur in ([0, 1] if T >= 1 else [1]):
                kcols = (qt + cur) * 128
                for half in range(2):
                    r0 = half * 64
                    k.op(PE, lambda: nc.tensor.matmul(
                        out=sreg(p, cur, half), lhsT=KTs[r0:r0 + 64, kvh, kcols:kcols + 128],
                        rhs=big[r0:r0 + 64, 2 * kvh:2 * kvh + 2, qt * 128:(qt + 1) * 128], start=True, stop=True),
                         reads=[B_KTs[qt + cur]] + B_big[2 * kvh:2 * kvh + 2], writes=[B_bk[4 + 2 * p + half]], signal=True)

        def emit_pv(idx):
            qt, kvh = items[idx]
            T = T0 + qt
            has_prev = T >= 1
            p = idx % 2
            pb = idx % 2
            ob = idx % 2
            osw = ps[:, ob, 0:264].rearrange("p (h e) -> p h e", h=4)
            for s_ in range(4):
                head = kvh * 4 + HMAP[s_]
                hf, jj = s_ // 2, s_ % 2
                for cur in ([0, 1] if has_prev else [1]):
                    k.op(ACT, lambda: nc.scalar.activation(
                        out=pTs[:, pb, cur, s_, :], in_=sreg(p, cur, hf)[:, jj * 128:(jj + 1) * 128], func=AF.Exp,
                        scale=0.125, bias=biass[:, head, 1 - cur:2 - cur]),
                         reads=[B_bk[4 + 2 * p + hf]], writes=[B_pTs[pb]], waw=False)
            if has_prev:
                k.op(DVE, lambda: nc.vector.tensor_tensor(out=pTs[:, pb, 0, :, :], in0=pTs[:, pb, 0, :, :], in1=maskP4[:],
                                                          op=ALU.mult), reads=[B_pTs[pb]], writes=[B_pTs[pb]])
            k.op(DVE, lambda: nc.vector.tensor_tensor(out=pTs[:, pb, 1, :, :], in0=pTs[:, pb, 1, :, :], in1=maskC4[:],
                                                      op=ALU.mult), reads=[B_pTs[pb]], writes=[B_pTs[pb]])
            first = True
            for s_ in range(4):
                i = HMAP[s_]
                if has_prev:
                    k.op(PE, lambda: nc.tensor.matmul(out=osw[:, i, :], lhsT=pTs[:, pb, 0, s_, :],
                                                      rhs=Vs[:, qt, kvh, :], start=first, stop=False,
                                                      skip_group_check=True),
                         reads=[B_pTs[pb], B_Vs[qt]], writes=[B_bk[ob]], signal=False)
                    first = False
                k.op(PE, lambda: nc.tensor.matmul(out=osw[:, i, :], lhsT=pTs[:, pb, 1, s_, :],
                                                  rhs=Vs[:, qt + 1, kvh, :], start=first, stop=True,
                                                  skip_group_check=True),
                     reads=[B_pTs[pb], B_Vs[qt + 1]], writes=[B_bk[ob]], signal=(s_ == 3))
                first = False
            dd = den[:, ob, :]
            rd = rden[:, ob, :]

            def fin():
                k.op(DVE, lambda: nc.vector.tensor_tensor(out=dd, in0=osw[:, :, 64], in1=sinkp[:, kvh * 4:kvh * 4 + 4],
                                                          op=ALU.add), reads=[B_bk[ob]], writes=[B_den[ob]])
                k.op(DVE, lambda: nc.vector.reciprocal(out=rd, in_=dd), reads=[B_den[ob]], writes=[B_rden[ob]])
                k.op(DVE, lambda: nc.vector.tensor_tensor(
                    out=obf[:, kvh * 256:(kvh + 1) * 256].rearrange("p (h e) -> p h e", h=4), in0=osw[:, :, 0:64],
                    in1=rd.unsqueeze(2).broadcast_to([128, 4, 64]), op=ALU.mult),
                     reads=[B_bk[ob], B_rden[ob]], writes=[B_obf], waw=(kvh == 0))
                if kvh == 3:
                    transpose_out(qt, 16)
            return fin

        n = len(items)
        pending_fin = None
        emit_scores(0)
        for i in range(n):
            if i + 1 < n:
                emit_scores(i + 1)
            fin_ = emit_pv(i)
            if pending_fin is not None:
                pending_fin()
            pending_fin = fin_
        if pending_fin is not None:
            pending_fin()

    setup()
    for g in range(NG):
        group(g)
    for t in range(4):
        k.wait(SP, S_o[t], S_o[t].cnt)
    if DEBUG_DUMP:
        tens = {"hT": hT, "big": big, "KTd": KTd, "Vd": Vd, "KTs": KTs, "Vs": Vs, "xg": xg, "biasd": biasd,
                "sinkp": sinkp, "nlam": nlam, "maskC4": maskC4, "maskP4": maskP4, "ident": ident, "rstd": rstd,
                "xb": xb, "ost": ost, "obf": obf}
        for E in [PE, ACT, DVE, POOL]:
            k.wait(SP, E.sem, E.sem.cnt)
        for i_ in range(3):
            k.wait(SP, S_wr[i_], S_wr[i_].cnt)
        S_d = SemC(nc, "s_dbg")
        for name in DEBUG_DUMP:
            tt = tens[name]
            shp = list(tt.shape)
            flat = 1
            for d_ in shp[1:]:
                flat *= d_
            dd_ = nc.dram_tensor("dbg_" + name, [128, flat], tt.dtype, kind="ExternalOutput").ap()
            letters = "abcde"[:len(shp) - 1]
            src = tt[:] if len(shp) == 2 else tt[:].rearrange("p " + " ".join(letters) + " -> p (" + " ".join(letters) + ")")
            k.op(SP, lambda: nc.sync.dma_start(out=dd_, in_=src), dsem=S_d)
        k.wait(SP, S_d, S_d.cnt)
    return nc


_NC_CACHE = {}


def kernel(x, attn_norm_w, w_in, lambda_q1, lambda_k1, lambda_q2, lambda_k2, subln_w, sinks, w_out,
           ffn_norm_w, w_gate, w_up, w_down, final_norm_w):
    f = lambda a: np.ascontiguousarray(np.asarray(a, dtype=np.float32))
    x = f(x)
    shared = {
        "w_in": f(w_in)[0], "w_out": f(w_out)[0], "w_gate": f(w_gate)[0], "w_up": f(w_up)[0], "w_down": f(w_down)[0],
        "attn_norm_w": f(attn_norm_w)[0], "ffn_norm_w": f(ffn_norm_w)[0], "final_norm_w": f(final_norm_w),
        "lambda_q1": f(lambda_q1)[0], "lambda_k1": f(lambda_k1)[0], "lambda_q2": f(lambda_q2)[0],
        "lambda_k2": f(lambda_k2)[0], "subln_w": f(subln_w)[0], "sinks": f(sinks)[0],
    }
    if "nc" not in _NC_CACHE:
        _NC_CACHE["nc"] = build_nc()
    nc = _NC_CACHE["nc"]
    in_maps = [dict(shared, x=x[b]) for b in range(8)]
    res = run_bass_kernel_spmd(nc, in_maps, core_ids=list(range(8)))
    return np.stack([np.asarray(res.results[b]["out"], dtype=np.float32) for b in range(8)], axis=0)
```

```python
import math
import numpy as np
import concourse.bass as bass
import concourse.mybir as mybir
from concourse.bass_utils import run_bass_kernel_spmd

F32 = mybir.dt.float32
BF16 = mybir.dt.bfloat16
AF = mybir.ActivationFunctionType
ALU = mybir.AluOpType
AX = mybir.AxisListType

S = 2048
D = 2048
DFF = 5632
NG = 4
GT = 512
EPS = 1e-5
LAMBDA_INIT = 0.8 - 0.6 * math.exp(-0.3 * 0)
SLOPE_D = [2.0 ** (-8.0 * (h + 1) / 8) for h in range(8)]
SLOPE_S = [2.0 ** (-8.0 * (h + 1) / 16) for h in range(16)]
NEG = -30000.0
LOOKAHEAD = 4
DEBUG_STOP = None
DEBUG_DUMP = []
DEBUG_MAXBLK = None


class SemC:
    def __init__(self, nc, name):
        self.h = nc.alloc_semaphore(name)
        self.cnt = 0


class Eng:
    def __init__(self, nc, e, name, inorder_safe=False):
        self.e = e
        self.name = name
        self.sem = SemC(nc, "prog_" + name)
        self.seen = {}
        self.inorder_safe = inorder_safe


class Buf:
    __slots__ = ("name", "w", "r")

    def __init__(self, name):
        self.name = name
        self.w = {}
        self.r = {}


def _merge(d, tok):
    s, v = tok
    if d.get(s, 0) < v:
        d[s] = v


class K:
    def __init__(self, nc):
        self.nc = nc
        self.PE = Eng(nc, nc.tensor, "pe", inorder_safe=True)
        self.ACT = Eng(nc, nc.scalar, "act")
        self.DVE = Eng(nc, nc.vector, "dve")
        self.POOL = Eng(nc, nc.gpsimd, "pool")
        self.SP = Eng(nc, nc.sync, "sp")
        self.nwait = 0

    def wait(self, E, sem, val):
        if sem is E.sem and E.inorder_safe:
            return
        if E.seen.get(sem, 0) >= val:
            return
        E.e.wait_ge(sem.h, val)
        E.seen[sem] = val
        self.nwait += 1

    def op(self, E, fn, reads=(), writes=(), signal=True, dsem=None, waw=True):
        deps = {}
        for b in reads:
            for s, v in b.w.items():
                _merge(deps, (s, v))
        for b in writes:
            for s, v in b.r.items():
                _merge(deps, (s, v))
            if waw or b.r:
                for s, v in b.w.items():
                    _merge(deps, (s, v))
        for s, v in deps.items():
            self.wait(E, s, v)
        ins = fn()
        if dsem is not None:
            dsem.cnt += 16
            ins.then_inc(dsem.h, 16)
            tok = (dsem, dsem.cnt)
        elif signal:
            E.sem.cnt += 1
            ins.then_inc(E.sem.h, 1)
            tok = (E.sem, E.sem.cnt)
        else:
            tok = (E.sem, E.sem.cnt + 1)
        for b in writes:
            if b.r:
                b.r = {}
                b.w = {}
            _merge(b.w, tok)
        for b in reads:
            _merge(b.r, tok)
        return tok


def build_nc():
    nc = bass.Bass("TRN2", target_bir_lowering=False)
    k = K(nc)
    PE, ACT, DVE, POOL, SP = k.PE, k.ACT, k.DVE, k.POOL, k.SP

    def din(name, shape):
        return nc.dram_tensor(name, shape, F32, kind="ExternalInput").ap()

    x = din("x", [S, D])
    w_in = din("w_in", [D, 4608])
    w_out = din("w_out", [D, D])
    w_gate = din("w_gate", [D, DFF])
    w_up = din("w_up", [D, DFF])
    w_down = din("w_down", [DFF, D])
    attn_norm_w = din("attn_norm_w", [D])
    ffn_norm_w = din("ffn_norm_w", [D])
    final_norm_w = din("final_norm_w", [D])
    lq1 = din("lambda_q1", [64])
    lk1 = din("lambda_k1", [64])
    lq2 = din("lambda_q2", [64])
    lk2 = din("lambda_k2", [64])
    subln_w = din("subln_w", [128])
    sinks = din("sinks", [16])
    out = nc.dram_tensor("out", [S, D], F32, kind="ExternalOutput").ap()

    w_in_v = w_in.rearrange("(k p) n -> p k n", p=128)
    w_out_v = w_out.rearrange("(k p) n -> p k n", p=128)
    w_gate_v = w_gate.rearrange("(k p) n -> p k n", p=128)
    w_up_v = w_up.rearrange("(k p) n -> p k n", p=128)
    w_down_v = w_down.rearrange("(k p) n -> p k n", p=128)

    def sb(name, shape, dt):
        return nc.alloc_sbuf_tensor(name, shape, dt)

    xg = sb("xg", [128, 4, 2048], F32)
    hT = sb("hT", [128, 16, 512], BF16)
    big = sb("big", [128, 24, 512], BF16)
    KTd = sb("KTd", [128, 8, 2048], BF16)
    Vd = sb("Vd", [128, 16, 8, 130], BF16)
    KTs = sb("KTs", [128, 4, 640], BF16)
    Vs = sb("Vs", [128, 5, 4, 66], BF16)
    wr = sb("wr", [128, 3, 4096], BF16)
    wbc = sb("wbc", [128, 2048], F32)
    xb = sb("xb", [128, 2048], BF16)
    sg = sb("sg", [128, 4, 512], F32)
    pTs = sb("pTs", [128, 2, 2, 4, 128], BF16)
    pT = pTs[:].rearrange("p a b c q -> p (a b c) q")
    ost = sb("ost", [128, 8, 128], F32)
    obf = sb("obf", [128, 1024], BF16)
    ident = sb("ident", [128, 128], BF16)
    maskC4 = sb("maskC4", [128, 4, 128], BF16)
    maskP4 = sb("maskP4", [128, 4, 128], BF16)
    Tt = sb("Tt", [128, 16], F32)
    Tu = sb("Tu", [128, 16], F32)
    varg = sb("varg", [128, 5, 16], F32)
    qbias = sb("qbias", [128, 5, 16], F32)
    vfac = sb("vfac", [128, 5, 16], F32)
    biasd = sb("biasd", [128, 8, 16], F32)
    biass = sb("biass", [128, 16, 2], F32)
    sinkbc = sb("sinkbc", [128, 16], F32)
    sinkp = sb("sinkp", [128, 16], F32)
    sublnbc = sb("sublnbc", [128, 128], F32)
    lam4 = sb("lam4", [128, 4, 64], F32)
    lsm = sb("lsm", [128, 8], F32)
    nlam = sb("nlam", [128, 1], F32)
    nh = sb("nh", [128, 8], F32)
    epsc = sb("epsc", [128, 1], F32)
    ss = sb("ss", [128, 4], F32)
    vv = sb("vv", [128, 4], F32)
    rstd = sb("rstd", [128, 4], F32)
    ssq = sb("ssq", [128, 8], F32)
    vq = sb("vq", [128, 8], F32)
    rq = sb("rq", [128, 8], F32)
    rr = sb("rr", [128, 4], F32)
    den = sb("den", [128, 2, 4], F32)
    rden = sb("rden", [128, 2, 4], F32)

    ps = nc.alloc_psum_tensor("ps", [128, 8, 512], F32)

    B_xg = [Buf(f"xg{t}") for t in range(4)]
    B_hT = [Buf(f"hT{t}") for t in range(4)]
    B_big = [Buf(f"big{c}") for c in range(24)]
    B_KTd = [[Buf(f"KTd{h}_{g}") for g in range(NG)] for h in range(8)]
    B_Vd = [Buf(f"Vd{t}") for t in range(16)]
    B_KTs = [Buf(f"KTs{t}") for t in range(5)]
    B_Vs = [Buf(f"Vs{t}") for t in range(5)]
    B_wr = [Buf(f"wr{i}") for i in range(3)]
    S_wr = [SemC(nc, f"s_wr{i}") for i in range(3)]
    B_wbc = Buf("wbc")
    S_wbc = SemC(nc, "s_wbc")
    B_xb = Buf("xb")
    B_sg = [Buf(f"sg{i}") for i in range(4)]
    B_pT = [Buf(f"pT{i}") for i in range(16)]
    B_pTs = [Buf("pTs0"), Buf("pTs1")]
    B_ost = Buf("ost")
    B_obf = Buf("obf")
    B_junk = Buf("junk")
    B_const = Buf("const")
    B_stat = Buf("stat")
    B_qstat = Buf("qstat")
    B_den = [Buf("den0"), Buf("den1")]
    B_rden = [Buf("rden0"), Buf("rden1")]
    B_vv, B_rstd = Buf("vv"), Buf("rstd")
    B_bk = [Buf(f"bank{b}") for b in range(8)]
    B_ps = B_bk[0:4]
    S_x = [SemC(nc, f"s_x{t}") for t in range(4)]
    S_o = [SemC(nc, f"s_o{t}") for t in range(4)]
    S_set = SemC(nc, "s_set")
    S_set2 = SemC(nc, "s_set2")
    S_set3 = SemC(nc, "s_set3")

    def setup():
        Bi, Bm, Bt, Bn, Bl, Bsk, Bsu = Buf("i"), Buf("m"), Buf("t"), Buf("n"), Buf("l"), Buf("sk"), Buf("su")
        k.op(POOL, lambda: nc.gpsimd.memset(ident[:], 0.0), writes=[Bi])
        k.op(POOL, lambda: nc.gpsimd.affine_select(out=ident[:], in_=ident[:], compare_op=ALU.not_equal,
                                                   fill=1.0, base=0, pattern=[[-1, 128]], channel_multiplier=1),
             reads=[Bi], writes=[Bi])
        k.op(POOL, lambda: nc.gpsimd.memset(maskC4[:], 1.0), writes=[Bm])
        k.op(POOL, lambda: nc.gpsimd.affine_select(out=maskC4[:], in_=maskC4[:], compare_op=ALU.is_ge,
                                                   fill=0.0, base=0, pattern=[[0, 4], [1, 128]],
                                                   channel_multiplier=-1), reads=[Bm], writes=[Bm])
        k.op(POOL, lambda: nc.gpsimd.memset(maskP4[:], 1.0), writes=[Bn])
        k.op(POOL, lambda: nc.gpsimd.affine_select(out=maskP4[:], in_=maskP4[:], compare_op=ALU.is_ge,
                                                   fill=0.0, base=-1, pattern=[[0, 4], [-1, 128]],
                                                   channel_multiplier=1), reads=[Bn], writes=[Bn])
        k.op(POOL, lambda: nc.gpsimd.iota(Tt[:], pattern=[[-128, 16]], base=-64, channel_multiplier=1, allow_small_or_imprecise_dtypes=True),
             writes=[Bt])
        k.op(POOL, lambda: nc.gpsimd.memset(nh[:], -0.5), writes=[Buf("x")])
        k.op(POOL, lambda: nc.gpsimd.memset(epsc[:], EPS), writes=[Buf("x")])
        Bvones = Buf("vones")
        k.op(POOL, lambda: nc.gpsimd.memset(Vd[:, :, :, 128:130], 1.0), writes=[Bvones])
        k.op(POOL, lambda: nc.gpsimd.memset(Vs[:, :, :, 64:66], 1.0), writes=[Buf("x")])
        for i, v in enumerate([lq1, lk1, lq2, lk2]):
            k.op(SP, lambda: nc.sync.dma_start(out=lam4[:, i, :], in_=v.partition_broadcast(128)),
                 writes=[Bl], dsem=S_set, waw=False)
        k.op(SP, lambda: nc.sync.dma_start(out=sinkbc[:], in_=sinks.partition_broadcast(128)),
             writes=[Bsk], dsem=S_set2, waw=False)
        k.op(SP, lambda: nc.sync.dma_start(out=sublnbc[:], in_=subln_w.partition_broadcast(128)),
             writes=[Bsu], dsem=S_set3, waw=False)
        Bu, Bva, Bvf = Buf("u"), Buf("va"), Buf("vf")
        k.op(POOL, lambda: nc.gpsimd.iota(Tu[:], pattern=[[128, 16]], base=-1024, channel_multiplier=1,
                                          allow_small_or_imprecise_dtypes=True), writes=[Bu])
        for hh in range(5):
            k.op(DVE, lambda: nc.vector.tensor_scalar(out=varg[:, hh, :], in0=Tu[:], scalar1=SLOPE_D[3 + hh], scalar2=None,
                                                      op0=ALU.mult), reads=[Bu], writes=[Bva], waw=False)
        k.op(ACT, lambda: nc.scalar.activation(out=vfac[:], in_=varg[:], func=AF.Exp), reads=[Bva], writes=[Bvf])
        Bq = Buf("q")
        k.op(POOL, lambda: nc.gpsimd.iota(qbias[:, 0, :], pattern=[[128, 16]], base=64 - 1024, channel_multiplier=0,
                                          allow_small_or_imprecise_dtypes=True), writes=[Bq])
        for hh in range(1, 5):
            k.op(POOL, lambda: nc.gpsimd.tensor_scalar(out=qbias[:, hh, :], in0=qbias[:, 0, :], scalar1=-SLOPE_D[3 + hh],
                                                       scalar2=None, op0=ALU.mult), reads=[Bq], writes=[Buf("x")])
        k.op(POOL, lambda: nc.gpsimd.tensor_scalar(out=qbias[:, 0, :], in0=qbias[:, 0, :], scalar1=-SLOPE_D[3],
                                                   scalar2=None, op0=ALU.mult), reads=[Bq], writes=[Bq])
        for hh in range(5):
            k.op(DVE, lambda: nc.vector.tensor_copy(out=Vd[:, :, 3 + hh, 128:130],
                                                    in_=vfac[:, hh, :].unsqueeze(2).broadcast_to([128, 16, 2])),
                 reads=[Bvf, Bvones], writes=[Buf("x")])
        for h in range(8):
            k.op(DVE, lambda: nc.vector.tensor_scalar(out=biasd[:, h, :], in0=Tt[:], scalar1=SLOPE_D[h],
                                                      scalar2=None, op0=ALU.mult), reads=[Bt], writes=[Buf("x")])
        for h in range(16):
            k.op(DVE, lambda: nc.vector.tensor_scalar(out=biass[:, h, :], in0=Tt[:, 0:2], scalar1=SLOPE_S[h],
                                                      scalar2=None, op0=ALU.mult), reads=[Bt], writes=[Buf("x")])
        B_l2, B_l3, B_l4, B_l5 = Buf("l2"), Buf("l3"), Buf("l4"), Buf("l5")
        k.op(DVE, lambda: nc.vector.tensor_tensor(out=lam4[:, 0, :], in0=lam4[:, 0, :], in1=lam4[:, 1, :], op=ALU.mult),
             reads=[Bl], writes=[B_l2], waw=False)
        k.op(DVE, lambda: nc.vector.tensor_tensor(out=lam4[:, 2, :], in0=lam4[:, 2, :], in1=lam4[:, 3, :], op=ALU.mult),
             reads=[Bl], writes=[B_l2], waw=False)
        k.op(DVE, lambda: nc.vector.tensor_reduce(out=lsm[:, 0:1], in_=lam4[:, 0, :], axis=AX.X, op=ALU.add),
             reads=[B_l2], writes=[B_l3], waw=False)
        k.op(DVE, lambda: nc.vector.tensor_reduce(out=lsm[:, 1:2], in_=lam4[:, 2, :], axis=AX.X, op=ALU.add),
             reads=[B_l2], writes=[B_l3], waw=False)
        k.op(ACT, lambda: nc.scalar.activation(out=lsm[:, 2:4], in_=lsm[:, 0:2], func=AF.Exp),
             reads=[B_l3], writes=[B_l4])
        k.op(DVE, lambda: nc.vector.tensor_tensor(out=lsm[:, 4:5], in0=lsm[:, 3:4], in1=lsm[:, 2:3], op=ALU.subtract),
             reads=[B_l4], writes=[B_l5])
        k.op(DVE, lambda: nc.vector.tensor_scalar(out=nlam[:], in0=lsm[:, 4:5], scalar1=-LAMBDA_INIT, scalar2=None,
                                                  op0=ALU.add), reads=[B_l5], writes=[Buf("x")])
        k.op(DVE, lambda: nc.vector.tensor_scalar(out=sublnbc[:], in0=sublnbc[:], scalar1=1.0 - LAMBDA_INIT,
                                                  scalar2=None, op0=ALU.mult), reads=[Bsu], writes=[Bsu])
        for h in range(16):
            k.op(ACT, lambda: nc.scalar.activation(out=sinkp[:, h:h + 1], in_=Tt[:, 0:1], func=AF.Exp,
                                                   scale=SLOPE_S[h], bias=sinkbc[:, h:h + 1]),
                 reads=[Bt, Bsk], writes=[Buf("x")])
        engs = [PE, ACT, DVE, POOL, SP]
        for E in engs:
            for Fe in engs:
                if Fe is not E and Fe.sem.cnt > 0:
                    k.wait(E, Fe.sem, Fe.sem.cnt)
            for ss_ in (S_set, S_set2, S_set3):
                k.wait(E, ss_, ss_.cnt)

    ring_state = {"i": 0, "g": 0, "pid": 0}
    NPIECE = 95
    wcache = nc.dram_tensor("wcache", [NPIECE, 128, 4096], BF16, kind="Internal").ap()
    B_wc = [Buf(f"wc{i}") for i in range(NPIECE)]
    S_ws = [SemC(nc, f"s_ws{i}") for i in range(3)]
    S_wrh = [SemC(nc, f"s_wrh{i}") for i in range(3)]

    def cache_group(pid):
        return 0 if pid % 3 == 0 else 1

    def ring_next():
        i = ring_state["i"] % 3
        ring_state["i"] += 1
        pid = ring_state["pid"]
        ring_state["pid"] += 1
        return i, pid

    def cache_store(i, pid, nel):
        k.op(SP, lambda: nc.sync.dma_start(out=wcache[pid, :, 0:nel], in_=wr[:, i, 0:nel]),
             reads=[B_wr[i]], writes=[B_wc[pid]], dsem=S_ws[i])

    def cache_load(i, pid, nel):
        k.op(SP, lambda: nc.sync.dma_start(out=wr[:, i, 0:nel], in_=wcache[pid, :, 0:nel]),
             reads=[B_wc[pid]], writes=[B_wr[i]], dsem=S_wrh[i])

    def load_piece(src_view, k0, k1, c0, c1):
        i, pid = ring_next()
        kk = k1 - k0
        n = c1 - c0
        assert kk * n <= 4096
        dst = wr[:, i, 0:kk * n].rearrange("p (k n) -> p k n", k=kk)
        gfill = cache_group(pid)
        if ring_state["g"] <= gfill:
            k.op(POOL, lambda: nc.gpsimd.dma_start(out=dst, in_=src_view[:, k0:k1, c0:c1]),
                 writes=[B_wr[i]], dsem=S_wr[i])
            if ring_state["g"] == gfill:
                cache_store(i, pid, kk * n)
        else:
            cache_load(i, pid, kk * n)
        return dst, B_wr[i]

    bank_state = {"i": 0}

    def next_bank():
        b = bank_state["i"] % 4
        bank_state["i"] += 1
        return b

    evac_state = {"i": 0}

    def evac_copy(out_ap, in_ap, reads, writes, waw=True, eng=None):
        evac_state["i"] += 1
        if (evac_state["i"] % 2 == 0 and eng is None) or eng == "act":
            return k.op(ACT, lambda: nc.scalar.activation(out=out_ap, in_=in_ap, func=AF.Copy),
                        reads=reads, writes=writes, waw=waw)
        return k.op(DVE, lambda: nc.vector.tensor_copy(out=out_ap, in_=in_ap), reads=reads, writes=writes, waw=waw)

    def load_wbc(vec):
        k.op(SP, lambda: nc.sync.dma_start(out=wbc[:], in_=vec.partition_broadcast(128)), writes=[B_wbc], dsem=S_wbc)

    B_statt = [Buf(f"stat{t}") for t in range(4)]
    B_vvt = [Buf(f"vv{t}") for t in range(4)]
    B_rstdt = [Buf(f"rstd{t}") for t in range(4)]
    B_xbh = [Buf("xbh0"), Buf("xbh1")]

    def stats_tile(t):
        k.op(ACT, lambda: nc.scalar.activation(out=xb[:], in_=xg[:, t, :], func=AF.Square, accum_out=ss[:, t:t + 1]),
             reads=[B_xg[t]], writes=B_xbh + [B_statt[t]])
        k.op(ACT, lambda: nc.scalar.activation(out=vv[:, t:t + 1], in_=ss[:, t:t + 1], func=AF.Sqrt, scale=1.0 / D,
                                               bias=epsc[:]), reads=[B_statt[t]], writes=[B_vvt[t]])
        k.op(DVE, lambda: nc.vector.reciprocal(out=rstd[:, t:t + 1], in_=vv[:, t:t + 1]),
             reads=[B_vvt[t]], writes=[B_rstdt[t]])

    def norm_to_hT():
        for t in range(4):
            stats_tile(t)
        for t in range(4):
            for r in range(2):
                cs = slice(r * 1024, (r + 1) * 1024)
                k.op(DVE, lambda: nc.vector.scalar_tensor_tensor(out=xb[:, cs], in0=xg[:, t, cs], scalar=rstd[:, t:t + 1],
                                                                 in1=wbc[:, cs], op0=ALU.mult, op1=ALU.mult),
                     reads=[B_xg[t], B_rstdt[t], B_wbc], writes=[B_xbh[r]])
                b = next_bank()
                tp = ps[:, b, :].bitcast(BF16).rearrange("p (c n) -> p c n", c=8)
                for c in range(8):
                    dc = r * 8 + c
                    k.op(PE, lambda: nc.tensor.transpose(out=tp[:, c, :], in_=xb[:, dc * 128:(dc + 1) * 128],
                                                         identity=ident[:]),
                         reads=[B_xbh[r]], writes=[B_ps[b]], signal=(c == 7))
                evac_copy(hT[:, r * 8:(r + 1) * 8, t * 128:(t + 1) * 128], tp, reads=[B_ps[b]], writes=[B_hT[t]], waw=False)

    ss2 = sb("ss2", [128, 4], F32)
    vv2 = sb("vv2", [128, 4], F32)
    rstd2 = sb("rstd2", [128, 4], F32)
    B_e1 = [Buf(f"e1_{t}") for t in range(4)]
    B_e2 = [Buf(f"e2_{t}") for t in range(4)]
    B_e3 = [Buf(f"e3_{t}") for t in range(4)]
    S_xe = SemC(nc, "s_xe")
    en_state = {"i": 0}

    def early_norm_dma(gn, t):
        sgx = sg[:].rearrange("p a n -> p (a n)")
        row0 = (gn * 4 + t) * 128
        k.op(ACT, lambda: nc.scalar.dma_start(out=sgx, in_=x[row0:row0 + 128, :]), writes=B_sg, dsem=S_xe)

    def early_norm_chain(gn, t):
        sgx = sg[:].rearrange("p a n -> p (a n)")
        junk16 = ost[:].rearrange("p a e -> p (a e)").bitcast(BF16)
        row0 = (gn * 4 + t) * 128
        k.op(ACT, lambda: nc.scalar.activation(out=junk16, in_=sgx, func=AF.Square, accum_out=ss2[:, t:t + 1]),
             reads=B_sg, writes=B_osth[0] + [B_e1[t]])
        k.op(ACT, lambda: nc.scalar.activation(out=vv2[:, t:t + 1], in_=ss2[:, t:t + 1], func=AF.Sqrt, scale=1.0 / D,
                                               bias=epsc[:]), reads=[B_e1[t]], writes=[B_e2[t]])
        k.op(DVE, lambda: nc.vector.reciprocal(out=rstd2[:, t:t + 1], in_=vv2[:, t:t + 1]),
             reads=[B_e2[t]], writes=[B_e3[t]])
        for r in range(2):
            cs = slice(r * 1024, (r + 1) * 1024)
            k.op(DVE, lambda: nc.vector.scalar_tensor_tensor(out=xb[:, cs], in0=sgx[:, cs], scalar=rstd2[:, t:t + 1],
                                                             in1=wbc[:, cs], op0=ALU.mult, op1=ALU.mult),
                 reads=B_sg + [B_e3[t], B_wbc], writes=[B_xbh[r]])

    def early_norm_transposes(t):
        for r in range(2):
            b = 4 + en_state["i"] % 2
            en_state["i"] += 1
            tp = ps[:, b, :].bitcast(BF16).rearrange("p (c n) -> p c n", c=8)
            for c in range(8):
                dc = r * 8 + c
                k.op(PE, lambda: nc.tensor.transpose(out=tp[:, c, :], in_=xb[:, dc * 128:(dc + 1) * 128], identity=ident[:]),
                     reads=[B_xbh[r]], writes=[B_bk[b]], signal=(c == 7))
            evac_copy(hT[:, r * 8:(r + 1) * 8, t * 128:(t + 1) * 128], tp, reads=[B_bk[b]], writes=[B_hT[t]], waw=False)

    def proj_feature_major(src_view, col0, nchunks, dest_fn, dest_bufs_fn):
        for s0 in range(0, nchunks, 4):
            nch = min(4, nchunks - s0)
            ncols = nch * 128
            kper = 4096 // ncols
            for k0 in range(0, 16, kper):
                piece, bw = load_piece(src_view, k0, k0 + kper, col0 + s0 * 128, col0 + s0 * 128 + ncols)
                for c in range(nch):
                    for kk in range(kper):
                        dc = k0 + kk
                        k.op(PE, lambda: nc.tensor.matmul(
                            out=ps[:, c, :], lhsT=piece[:, kk, c * 128:(c + 1) * 128], rhs=hT[:, dc, :],
                            start=(dc == 0), stop=(dc == 15)),
                             reads=[bw] + B_hT, writes=[B_ps[c]], signal=(kk == kper - 1))
            for c in range(nch):
                evac_copy(dest_fn(s0 + c), ps[:, c, :], reads=[B_ps[c]], writes=dest_bufs_fn(s0 + c))

    def proj_token_major(src_view, kchunks, col0, ncols, lhs_fn, lhs_bufs_fn, evac_fn):
        kper = 4096 // ncols
        pieces = [(a, min(a + kper, kchunks)) for a in range(0, kchunks, kper)]
        for (k0, k1) in pieces:
            piece, bw = load_piece(src_view, k0, k1, col0, col0 + ncols)
            for t in range(4):
                for kk in range(k0, k1):
                    last = (kk == kchunks - 1)
                    k.op(PE, lambda t=t, kk=kk, k0=k0, piece=piece: nc.tensor.matmul(
                        out=ps[:, t, 0:ncols], lhsT=lhs_fn(kk, t), rhs=piece[:, kk - k0, :],
                        start=(kk == 0), stop=(kk == kchunks - 1)),
                         reads=[bw] + lhs_bufs_fn(kk, t), writes=[B_ps[t]], signal=(kk == k1 - 1))
        for t in range(4):
            evac_fn(t)

    pending_io = []

    def group(g):
        T0 = g * 4
        ring_state["g"] = g
        ring_state["pid"] = 0
        def load_x():
            for t in range(4):
                k.op(SP, lambda t=t: nc.sync.dma_start(out=xg[:, t, :], in_=x[(T0 + t) * 128:(T0 + t + 1) * 128, :]),
                     writes=[B_xg[t]], dsem=S_x[t])

        if g == 0:
            load_x()
            load_wbc(attn_norm_w)
            norm_to_hT()
            load_wbc(ffn_norm_w)

        proj_feature_major(w_in_v, 0, 8, lambda c: big[:, c, :], lambda c: [B_big[c]])
        proj_feature_major(w_in_v, 1024, 8, lambda c: KTd[:, c, g * GT:(g + 1) * GT], lambda c: [B_KTd[c][g]])
        def evac_v(t, vb):
            tile_ = T0 + t
            pv = ps[:, t, :].rearrange("p (h e) -> p h e", h=4)
            if vb == 0:
                k.op(DVE, lambda: nc.vector.tensor_copy(out=Vd[:, tile_, 0:3, 0:128], in_=pv[:, 0:3, :]),
                     reads=[B_ps[t]], writes=[B_Vd[tile_]], waw=False)
                k.op(DVE, lambda: nc.vector.tensor_scalar(out=Vd[:, tile_, 3, 0:128], in0=pv[:, 3, :],
                                                          scalar1=vfac[:, 0, tile_:tile_ + 1], scalar2=None, op0=ALU.mult),
                     reads=[B_ps[t]], writes=[B_Vd[tile_]], waw=False)
            else:
                k.op(DVE, lambda: nc.vector.tensor_tensor(
                    out=Vd[:, tile_, 4:8, 0:128], in0=pv,
                    in1=vfac[:, 1:5, tile_].unsqueeze(2).broadcast_to([128, 4, 128]), op=ALU.mult),
                     reads=[B_ps[t]], writes=[B_Vd[tile_]], waw=False)

        for vb in range(2):
            proj_token_major(
                w_in_v, 16, 2048 + vb * 512, 512,
                lambda kk, t: hT[:, kk, t * 128:(t + 1) * 128], lambda kk, t: [B_hT[t]],
                lambda t, vb=vb: evac_v(t, vb))
        if DEBUG_STOP == "qkv":
            return
        if g > 0:
            for f_ in pending_io:
                f_()
            del pending_io[:]
            load_x()
            load_wbc(ffn_norm_w)
        diff_attention(g)
        if DEBUG_STOP == "dattn":
            return
        if g > 0:
            k.op(DVE, lambda: nc.vector.tensor_copy(out=KTs[:, :, 0:128], in_=KTs[:, :, 512:640]),
                 reads=[B_KTs[4]], writes=[B_KTs[0]])
            k.op(DVE, lambda: nc.vector.tensor_copy(out=Vs[:, 0, :, :], in_=Vs[:, 4, :, :]),
                 reads=[B_Vs[4]], writes=[B_Vs[0]])
        proj_feature_major(w_in_v, 3072, 8, lambda c: big[:, c, :], lambda c: [B_big[c]])
        for kh in range(2):
            i, pid = ring_next()
            dst = wr[:, i, :].rearrange("p (k h u e) -> p k h u e", k=8, h=4, u=2)
            srcv = w_in_v[:, kh * 8:kh * 8 + 8, 4096:4352].rearrange("p k (h e) -> p k h e", h=4)
            if g <= cache_group(pid):
                for u in range(2):
                    for hh in range(4):
                        k.op(POOL, lambda: nc.gpsimd.dma_start(out=dst[:, :, hh, u, :], in_=srcv[:, :, hh, :]),
                             writes=[B_wr[i]], dsem=S_wr[i], waw=(u == 0 and hh == 0))
                if g == cache_group(pid):
                    cache_store(i, pid, 4096)
            else:
                cache_load(i, pid, 4096)
            piece = wr[:, i, :].rearrange("p (k n) -> p k n", k=8)
            for kvh in range(4):
                for kk in range(8):
                    dc = kh * 8 + kk
                    k.op(PE, lambda: nc.tensor.matmul(out=ps[:, kvh, :], lhsT=piece[:, kk, kvh * 128:(kvh + 1) * 128],
                                                      rhs=hT[:, dc, :], start=(dc == 0), stop=(dc == 15)),
                         reads=[B_wr[i]] + B_hT, writes=[B_ps[kvh]], signal=(kk == 7))
        for kvh in range(4):
            evac_copy(KTs[:, kvh, 128:640], ps[:, kvh, :], reads=[B_ps[kvh]], writes=B_KTs[1:5], waw=False)
        proj_token_major(
            w_in_v, 16, 4352, 256,
            lambda kk, t: hT[:, kk, t * 128:(t + 1) * 128], lambda kk, t: [B_hT[t]],
            lambda t: evac_copy(Vs[:, 1 + t, :, 0:64], ps[:, t, 0:256].rearrange("p (h e) -> p h e", h=4),
                                reads=[B_ps[t]], writes=[B_Vs[1 + t]], waw=False))
        if DEBUG_STOP == "sproj":
            return
        swa_attention(g)
        if DEBUG_STOP == "attn":
            return
        for db in range(4):
            proj_token_major(
                w_out_v, 16, db * 512, 512,
                lambda kk, t: big[:, 8 + kk, t * 128:(t + 1) * 128], lambda kk, t: [B_big[8 + kk]],
                lambda t, db=db: k.op(DVE, lambda: nc.vector.tensor_tensor(
                    out=xg[:, t, db * 512:(db + 1) * 512], in0=ps[:, t, :], in1=xg[:, t, db * 512:(db + 1) * 512],
                    op=ALU.add), reads=[B_ps[t], B_xg[t]], writes=[B_xg[t]]))
        if DEBUG_STOP == "oproj":
            return
        norm_to_hT()
        for fh in range(2):
            f0 = fh * 22
            for s0 in range(0, 22, 4):
                nch = min(4, 22 - s0)
                ncols = nch * 128
                kper = 4096 // ncols
                c0 = (f0 + s0) * 128
                for (wv, bank0) in ((w_gate_v, 0), (w_up_v, 4)):
                    for k0 in range(0, 16, kper):
                        piece, bw = load_piece(wv, k0, k0 + kper, c0, c0 + ncols)
                        for c in range(nch):
                            for kk in range(kper):
                                dc = k0 + kk
                                k.op(PE, lambda: nc.tensor.matmul(
                                    out=ps[:, bank0 + c, :], lhsT=piece[:, kk, c * 128:(c + 1) * 128], rhs=hT[:, dc, :],
                                    start=(dc == 0), stop=(dc == 15)),
                                     reads=[bw] + B_hT, writes=[B_bk[bank0 + c]], signal=(kk == kper - 1))
                    if bank0 == 0:
                        for c in range(nch):
                            k.op(ACT, lambda: nc.scalar.activation(out=sg[:, c, :], in_=ps[:, c, :], func=AF.Silu),
                                 reads=[B_bk[c]], writes=[B_sg[c]])
                for c in range(nch):
                    k.op(DVE, lambda: nc.vector.tensor_tensor(out=big[:, s0 + c, :], in0=sg[:, c, :], in1=ps[:, 4 + c, :],
                                                              op=ALU.mult),
                         reads=[B_sg[c], B_bk[4 + c]], writes=[B_big[s0 + c]])
            early = (fh == 1 and g + 1 < NG)
            if early:
                load_wbc(attn_norm_w)
                early_norm_dma(g + 1, 0)
            for db in range(4):
                if early:
                    if db > 0:
                        early_norm_transposes(db - 1)
                    early_norm_chain(g + 1, db)
                    if db < 3:
                        early_norm_dma(g + 1, db + 1)
                proj_token_major(
                    w_down_v[:, f0:f0 + 22, :], 22, db * 512, 512,
                    lambda kk, t: big[:, kk, t * 128:(t + 1) * 128], lambda kk, t: [B_big[kk]],
                    lambda t, db=db: k.op(DVE, lambda: nc.vector.tensor_tensor(
                        out=xg[:, t, db * 512:(db + 1) * 512], in0=ps[:, t, :], in1=xg[:, t, db * 512:(db + 1) * 512],
                        op=ALU.add), reads=[B_ps[t], B_xg[t]], writes=[B_xg[t]]))
        if g + 1 < NG:
            early_norm_transposes(3)
        load_wbc(final_norm_w)
        for t in range(4):
            stats_tile(t)
        for t in range(4):
            k.op(DVE, lambda t=t: nc.vector.scalar_tensor_tensor(out=xg[:, t, :], in0=xg[:, t, :], scalar=rstd[:, t:t + 1],
                                                                 in1=wbc[:], op0=ALU.mult, op1=ALU.mult),
                 reads=[B_rstdt[t], B_wbc], writes=[B_xg[t]])
            pending_io.append(lambda t=t: k.op(
                SP, lambda: nc.sync.dma_start(out=out[(T0 + t) * 128:(T0 + t + 1) * 128, :], in_=xg[:, t, :]),
                reads=[B_xg[t]], dsem=S_o[t]))
        if g == NG - 1:
            for f_ in pending_io:
                f_()
            del pending_io[:]

    B_rr, B_rr2 = Buf("rr"), Buf("rr2")
    B_osth = [[Buf(f"ost{q}_{h}") for h in range(8)] for q in range(2)]
    B_ssq, B_vq, B_rq = Buf("ssq"), Buf("vq"), Buf("rq")
    tp_state = {"i": 0}

    def transpose_out(qt, chunk0, bank=None):
        if bank is None:
            b = 2 + tp_state["i"] % 2
            tp_state["i"] += 1
        else:
            b = bank
        tp = ps[:, b, :].bitcast(BF16).rearrange("p (c n) -> p c n", c=8)
        for c in range(8):
            k.op(PE, lambda: nc.tensor.transpose(out=tp[:, c, :], in_=obf[:, c * 128:(c + 1) * 128], identity=ident[:]),
                 reads=[B_obf], writes=[B_ps[b]], signal=(c == 7))
        evac_copy(big[:, chunk0:chunk0 + 8, qt * 128:(qt + 1) * 128], tp, reads=[B_ps[b]],
                  writes=B_big[chunk0:chunk0 + 8], waw=False, eng="dve")

    def diff_attention(g):
        T0 = g * 4
        items = []
        for qt in range(4):
            for h in range(8):
                T = T0 + qt
                kmin = 0
                while kmin < T and SLOPE_D[h] * (128 * (T - kmin - 1) + 1) > 124.0:
                    kmin += 1
                for k0 in range(kmin, T + 1, 4):
                    items.append((qt, h, k0, min(k0 + 4, T + 1), kmin))
        if DEBUG_MAXBLK is not None:
            items = items[:DEBUG_MAXBLK]
        st = {"p": 0, "u": -1}
        ostv = [ost[:], sg[:, 2:4, :].rearrange("p a (b e) -> p (a b) e", e=128)]
        for i_ in range(16):
            for pb_ in B_pTs:
                for s_, v_ in list(pb_.r.items()) + list(pb_.w.items()):
                    _merge(B_pT[i_].r, (s_, v_))
        for h_ in range(8):
            for sgb in B_sg[2:4]:
                for s_, v_ in list(sgb.r.items()) + list(sgb.w.items()):
                    _merge(B_osth[1][h_].r, (s_, v_))

        def emit_scores(idx):
            qt, h, k0, k1, kmin = items[idx]
            p = idx % 3
            for kt in range(k0, k1):
                for m in range(2):
                    bk = 2 + 2 * p + m
                    j = kt - k0
                    k.op(PE, lambda: nc.tensor.matmul(out=ps[:, bk, j * 128:(j + 1) * 128],
                                                      lhsT=KTd[m * 64:(m + 1) * 64, h, kt * 128:(kt + 1) * 128],
                                                      rhs=big[m * 64:(m + 1) * 64, h, qt * 128:(qt + 1) * 128],
                                                      start=True, stop=True),
                         reads=[B_KTd[h][kt // 4], B_big[h]], writes=[B_bk[bk]], signal=True)

        def finish_unit(qt, h, ob):
            o1 = ps[:, ob, 0:130]
            o2 = ps[:, ob, 130:260]
            ov = ostv[qt % 2][:, h, :]
            bo = B_osth[qt % 2][h]
            k.op(DVE, lambda: nc.vector.reciprocal(out=rr[:, 0:1], in_=o1[:, 128:129]), reads=[B_bk[ob]], writes=[B_rr])
            k.op(DVE, lambda: nc.vector.reciprocal(out=rr[:, 1:2], in_=o2[:, 128:129]), reads=[B_bk[ob]], writes=[B_rr],
                 waw=False)
            k.op(DVE, lambda: nc.vector.tensor_tensor(out=rr[:, 2:3], in0=rr[:, 1:2], in1=nlam[:], op=ALU.mult),
                 reads=[B_rr], writes=[B_rr2])
            k.op(DVE, lambda: nc.vector.tensor_scalar(out=ov, in0=o1[:, 0:128], scalar1=rr[:, 0:1], scalar2=None,
                                                      op0=ALU.mult), reads=[B_bk[ob], B_rr], writes=[bo])
            k.op(DVE, lambda: nc.vector.scalar_tensor_tensor(out=ov, in0=o2[:, 0:128], scalar=rr[:, 2:3],
                                                             in1=ov, op0=ALU.mult, op1=ALU.add),
                 reads=[B_bk[ob], B_rr2, bo], writes=[bo])

        def finish_qtile(qt):
            ov = ostv[qt % 2]
            bos = B_osth[qt % 2]
            sqv = sg[:, 0:2, :].rearrange("p a (b e) -> p (a b) e", e=128)
            k.op(DVE, lambda: nc.vector.tensor_tensor(out=sqv, in0=ov, in1=ov, op=ALU.mult),
                 reads=bos, writes=B_sg[0:2])
            k.op(DVE, lambda: nc.vector.tensor_reduce(out=ssq[:], in_=sqv, axis=AX.X, op=ALU.add),
                 reads=B_sg[0:2], writes=[B_ssq])
            k.op(DVE, lambda: nc.vector.tensor_scalar(out=vq[:], in0=ssq[:], scalar1=1.0 / 128, scalar2=EPS,
                                                      op0=ALU.mult, op1=ALU.add), reads=[B_ssq], writes=[B_vq])
            k.op(POOL, lambda: nc.gpsimd.tensor_tensor(out=rq[:], in0=vq[:], in1=nh[:], op=ALU.pow),
                 reads=[B_vq], writes=[B_rq])
            k.op(POOL, lambda: nc.gpsimd.tensor_tensor(out=ov, in0=ov,
                                                       in1=rq[:].unsqueeze(2).broadcast_to([128, 8, 128]), op=ALU.mult),
                 reads=[B_rq] + bos, writes=bos)
            k.op(POOL, lambda: nc.gpsimd.tensor_tensor(out=obf[:].rearrange("p (h e) -> p h e", h=8), in0=ov,
                                                       in1=sublnbc[:].unsqueeze(1).broadcast_to([128, 8, 128]),
                                                       op=ALU.mult),
                 reads=bos, writes=[B_obf])
            def tr_():
                st["u"] += 1
                transpose_out(qt, 8, bank=st["u"] % 2)
            deferred.append([6, tr_])

        def emit_pv(idx):
            qt, h, k0, k1, kmin = items[idx]
            T = T0 + qt
            p = idx % 3
            if k0 == kmin:
                st["u"] += 1
                st["ob"] = st["u"] % 2
            ob = st["ob"]
            wide = h >= 3
            wbase = {}
            if wide:
                nkt = k1 - k0
                for m in range(2):
                    bk = 2 + 2 * p + m
                    if st["p"] % 4:
                        st["p"] += 4 - st["p"] % 4
                    base = st["p"] % 16
                    st["p"] += 4
                    wbase[m] = base
                    k.op(ACT, lambda: nc.scalar.activation(
                        out=pT[:, base:base + nkt, :], in_=ps[:, bk, 0:nkt * 128].rearrange("p (j q) -> p j q", q=128),
                        func=AF.Exp, scale=0.125, bias=qbias[:, h - 3, T:T + 1]),
                         reads=[B_bk[bk]], writes=B_pT[base:base + nkt])
            for kt in range(k0, k1):
                for m in range(2):
                    bk = 2 + 2 * p + m
                    j = kt - k0
                    if wide:
                        pi = wbase[m] + j
                    else:
                        pi = st["p"] % 16
                        st["p"] += 1
                        k.op(ACT, lambda: nc.scalar.activation(out=pT[:, pi, :], in_=ps[:, bk, j * 128:(j + 1) * 128],
                                                               func=AF.Exp, scale=0.125,
                                                               bias=biasd[:, h, T - kt:T - kt + 1]),
                             reads=[B_bk[bk]], writes=[B_pT[pi]])
                    if kt == T:
                        k.op(DVE, lambda: nc.vector.tensor_tensor(out=pT[:, pi, :], in0=pT[:, pi, :], in1=maskC4[:, 0, :],
                                                                  op=ALU.mult), reads=[B_pT[pi]], writes=[B_pT[pi]])
                    k.op(PE, lambda: nc.tensor.matmul(out=ps[:, ob, m * 130:(m + 1) * 130], lhsT=pT[:, pi, :],
                                                      rhs=Vd[:, kt, h, :], start=(kt == kmin and m == 0), stop=(kt == T),
                                                      skip_group_check=True),
                         reads=[B_pT[pi], B_Vd[kt]], writes=[B_bk[ob]], signal=True)
            if k1 == T + 1:
                finish_unit(qt, h, ob)
                if h == 7:
                    finish_qtile(qt)

        n = len(items)
        deferred = []
        for i in range(min(2, n)):
            emit_scores(i)
        for i in range(n):
            if i + 2 < n:
                emit_scores(i + 2)
            emit_pv(i)
            for d_ in list(deferred):
                d_[0] -= 1
                if d_[0] <= 0:
                    deferred.remove(d_)
                    d_[1]()
        for d_ in deferred:
            d_[1]()
        for i_ in range(16):
            for s_, v_ in list(B_pT[i_].r.items()) + list(B_pT[i_].w.items()):
                for pb_ in B_pTs:
                    _merge(pb_.r, (s_, v_))
        for h_ in range(8):
            for s_, v_ in list(B_osth[1][h_].r.items()) + list(B_osth[1][h_].w.items()):
                for sgb in B_sg[2:4]:
                    _merge(sgb.r, (s_, v_))

    HMAP = [0, 2, 1, 3]

    def swa_attention(g):
        T0 = g * 4
        items = [(qt, kvh) for qt in range(4) for kvh in range(4)]

        def sreg(p, cur, half):
            return ps[:, 4 + 2 * p + half, cur * 256:(cur + 1) * 256]

        def emit_scores(idx):
            qt, kvh = items[idx]
            T = T0 + qt
            p = idx % 2
            for cur in ([0, 1] if T >= 1 else [1]):
                kcols = (qt + cur) * 128
                for half in range(2):
                    r0 = half * 64
                    k.op(PE, lambda: nc.tensor.matmul(
                        out=sreg(p, cur, half), lhsT=KTs[r0:r0 + 64, kvh, kcols:kcols + 128],
                        rhs=big[r0:r0 + 64, 2 * kvh:2 * kvh + 2, qt * 128:(qt + 1) * 128], start=True, stop=True),
                         reads=[B_KTs[qt + cur]] + B_big[2 * kvh:2 * kvh + 2], writes=[B_bk[4 + 2 * p + half]], signal=True)

        def emit_pv(idx):
            qt, kvh = items[idx]
            T = T0 + qt
            has_prev = T >= 1
            p = idx % 2
            pb = idx % 2
            ob = idx % 2
            osw = ps[:, ob, 0:264].rearrange("p (h e) -> p h e", h=4)
            for s_ in range(4):
                head = kvh * 4 + HMAP[s_]
                hf, jj = s_ // 2, s_ % 2
                for cur in ([0, 1] if has_prev else [1]):
                    k.op(ACT, lambda: nc.scalar.activation(
                        out=pTs[:, pb, cur, s_, :], in_=sreg(p, cur, hf)[:, jj * 128:(jj + 1) * 128], func=AF.Exp,
                        scale=0.125, bias=biass[:, head, 1 - cur:2 - cur]),
                         reads=[B_bk[4 + 2 * p + hf]], writes=[B_pTs[pb]], waw=False)
            if has_prev:
                k.op(DVE, lambda: nc.vector.tensor_tensor(out=pTs[:, pb, 0, :, :], in0=pTs[:, pb, 0, :, :], in1=maskP4[:],
                                                          op=ALU.mult), reads=[B_pTs[pb]], writes=[B_pTs[pb]])
            k.op(DVE, lambda: nc.vector.tensor_tensor(out=pTs[:, pb, 1, :, :], in0=pTs[:, pb, 1, :, :], in1=maskC4[:],
                                                      op=ALU.mult), reads=[B_pTs[pb]], writes=[B_pTs[pb]])
            first = True
            for s_ in range(4):
                i = HMAP[s_]
                if has_prev:
                    k.op(PE, lambda: nc.tensor.matmul(out=osw[:, i, :], lhsT=pTs[:, pb, 0, s_, :],
                                                      rhs=Vs[:, qt, kvh, :], start=first, stop=False,
                                                      skip_group_check=True),
                         reads=[B_pTs[pb], B_Vs[qt]], writes=[B_bk[ob]], signal=False)
                    first = False
                k.op(PE, lambda: nc.tensor.matmul(out=osw[:, i, :], lhsT=pTs[:, pb, 1, s_, :],
                                                  rhs=Vs[:, qt + 1, kvh, :], start=first, stop=True,
                                                  skip_group_check=True),
                     reads=[B_pTs[pb], B_Vs[qt + 1]], writes=[B_bk[ob]], signal=(s_ == 3))
                first = False
            dd = den[:, ob, :]
            rd = rden[:, ob, :]

            def fin():
                k.op(DVE, lambda: nc.vector.tensor_tensor(out=dd, in0=osw[:, :, 64], in1=sinkp[:, kvh * 4:kvh * 4 + 4],
                                                          op=ALU.add), reads=[B_bk[ob]], writes=[B_den[ob]])
                k.op(DVE, lambda: nc.vector.reciprocal(out=rd, in_=dd), reads=[B_den[ob]], writes=[B_rden[ob]])
                k.op(DVE, lambda: nc.vector.tensor_tensor(
                    out=obf[:, kvh * 256:(kvh + 1) * 256].rearrange("p (h e) -> p h e", h=4), in0=osw[:, :, 0:64],
                    in1=rd.unsqueeze(2).broadcast_to([128, 4, 64]), op=ALU.mult),
                     reads=[B_bk[ob], B_rden[ob]], writes=[B_obf], waw=(kvh == 0))
                if kvh == 3:
                    transpose_out(qt, 16)
            return fin

        n = len(items)
        pending_fin = None
        emit_scores(0)
        for i in range(n):
            if i + 1 < n:
                emit_scores(i + 1)
            fin_ = emit_pv(i)
            if pending_fin is not None:
                pending_fin()
            pending_fin = fin_
        if pending_fin is not None:
            pending_fin()

    setup()
    for g in range(NG):
        group(g)
    for t in range(4):
        k.wait(SP, S_o[t], S_o[t].cnt)
    if DEBUG_DUMP:
        tens = {"hT": hT, "big": big, "KTd": KTd, "Vd": Vd, "KTs": KTs, "Vs": Vs, "xg": xg, "biasd": biasd,
                "sinkp": sinkp, "nlam": nlam, "maskC4": maskC4, "maskP4": maskP4, "ident": ident, "rstd": rstd,
                "xb": xb, "ost": ost, "obf": obf}
        for E in [PE, ACT, DVE, POOL]:
            k.wait(SP, E.sem, E.sem.cnt)
        for i_ in range(3):
            k.wait(SP, S_wr[i_], S_wr[i_].cnt)
        S_d = SemC(nc, "s_dbg")
        for name in DEBUG_DUMP:
            tt = tens[name]
            shp = list(tt.shape)
            flat = 1
            for d_ in shp[1:]:
                flat *= d_
            dd_ = nc.dram_tensor("dbg_" + name, [128, flat], tt.dtype, kind="ExternalOutput").ap()
            letters = "abcde"[:len(shp) - 1]
            src = tt[:] if len(shp) == 2 else tt[:].rearrange("p " + " ".join(letters) + " -> p (" + " ".join(letters) + ")")
            k.op(SP, lambda: nc.sync.dma_start(out=dd_, in_=src), dsem=S_d)
        k.wait(SP, S_d, S_d.cnt)
    return nc


_NC_CACHE = {}


def kernel(x, attn_norm_w, w_in, lambda_q1, lambda_k1, lambda_q2, lambda_k2, subln_w, sinks, w_out,
           ffn_norm_w, w_gate, w_up, w_down, final_norm_w):
    f = lambda a: np.ascontiguousarray(np.asarray(a, dtype=np.float32))
    x = f(x)
    shared = {
        "w_in": f(w_in)[0], "w_out": f(w_out)[0], "w_gate": f(w_gate)[0], "w_up": f(w_up)[0], "w_down": f(w_down)[0],
        "attn_norm_w": f(attn_norm_w)[0], "ffn_norm_w": f(ffn_norm_w)[0], "final_norm_w": f(final_norm_w),
        "lambda_q1": f(lambda_q1)[0], "lambda_k1": f(lambda_k1)[0], "lambda_q2": f(lambda_q2)[0],
        "lambda_k2": f(lambda_k2)[0], "subln_w": f(subln_w)[0], "sinks": f(sinks)[0],
    }
    if "nc" not in _NC_CACHE:
        _NC_CACHE["nc"] = build_nc()
    nc = _NC_CACHE["nc"]
    in_maps = [dict(shared, x=x[b]) for b in range(8)]
    res = run_bass_kernel_spmd(nc, in_maps, core_ids=list(range(8)))
    return np.stack([np.asarray(res.results[b]["out"], dtype=np.float32) for b in range(8)], axis=0)
```

```python
import math
import numpy as np
import concourse.bass as bass
import concourse.mybir as mybir
from concourse.bass_utils import run_bass_kernel_spmd

F32 = mybir.dt.float32
BF16 = mybir.dt.bfloat16
AF = mybir.ActivationFunctionType
ALU = mybir.AluOpType
AX = mybir.AxisListType

S = 2048
D = 2048
DFF = 5632
NG = 4
GT = 512
EPS = 1e-5
LAMBDA_INIT = 0.8 - 0.6 * math.exp(-0.3 * 0)
SLOPE_D = [2.0 ** (-8.0 * (h + 1) / 8) for h in range(8)]
SLOPE_S = [2.0 ** (-8.0 * (h + 1) / 16) for h in range(16)]
NEG = -30000.0
LOOKAHEAD = 4
DEBUG_STOP = None
DEBUG_DUMP = []
DEBUG_MAXBLK = None


class SemC:
    def __init__(self, nc, name):
        self.h = nc.alloc_semaphore(name)
        self.cnt = 0


class Eng:
    def __init__(self, nc, e, name, inorder_safe=False):
        self.e = e
        self.name = name
        self.sem = SemC(nc, "prog_" + name)
        self.seen = {}
        self.inorder_safe = inorder_safe


class Buf:
    __slots__ = ("name", "w", "r")

    def __init__(self, name):
        self.name = name
        self.w = {}
        self.r = {}


def _merge(d, tok):
    s, v = tok
    if d.get(s, 0) < v:
        d[s] = v


class K:
    def __init__(self, nc):
        self.nc = nc
        self.PE = Eng(nc, nc.tensor, "pe", inorder_safe=True)
        self.ACT = Eng(nc, nc.scalar, "act")
        self.DVE = Eng(nc, nc.vector, "dve")
        self.POOL = Eng(nc, nc.gpsimd, "pool")
        self.SP = Eng(nc, nc.sync, "sp")
        self.nwait = 0

    def wait(self, E, sem, val):
        if sem is E.sem and E.inorder_safe:
            return
        if E.seen.get(sem, 0) >= val:
            return
        E.e.wait_ge(sem.h, val)
        E.seen[sem] = val
        self.nwait += 1

    def op(self, E, fn, reads=(), writes=(), signal=True, dsem=None, waw=True):
        deps = {}
        for b in reads:
            for s, v in b.w.items():
                _merge(deps, (s, v))
        for b in writes:
            for s, v in b.r.items():
                _merge(deps, (s, v))
            if waw or b.r:
                for s, v in b.w.items():
                    _merge(deps, (s, v))
        for s, v in deps.items():
            self.wait(E, s, v)
        ins = fn()
        if dsem is not None:
            dsem.cnt += 16
            ins.then_inc(dsem.h, 16)
            tok = (dsem, dsem.cnt)
        elif signal:
            E.sem.cnt += 1
            ins.then_inc(E.sem.h, 1)
            tok = (E.sem, E.sem.cnt)
        else:
            tok = (E.sem, E.sem.cnt + 1)
        for b in writes:
            if b.r:
                b.r = {}
                b.w = {}
            _merge(b.w, tok)
        for b in reads:
            _merge(b.r, tok)
        return tok


def build_nc():
    nc = bass.Bass("TRN2", target_bir_lowering=False)
    k = K(nc)
    PE, ACT, DVE, POOL, SP = k.PE, k.ACT, k.DVE, k.POOL, k.SP

    def din(name, shape):
        return nc.dram_tensor(name, shape, F32, kind="ExternalInput").ap()

    x = din("x", [S, D])
    w_in = din("w_in", [D, 4608])
    w_out = din("w_out", [D, D])
    w_gate = din("w_gate", [D, DFF])
    w_up = din("w_up", [D, DFF])
    w_down = din("w_down", [DFF, D])
    attn_norm_w = din("attn_norm_w", [D])
    ffn_norm_w = din("ffn_norm_w", [D])
    final_norm_w = din("final_norm_w", [D])
    lq1 = din("lambda_q1", [64])
    lk1 = din("lambda_k1", [64])
    lq2 = din("lambda_q2", [64])
    lk2 = din("lambda_k2", [64])
    subln_w = din("subln_w", [128])
    sinks = din("sinks", [16])
    out = nc.dram_tensor("out", [S, D], F32, kind="ExternalOutput").ap()

    w_in_v = w_in.rearrange("(k p) n -> p k n", p=128)
    w_out_v = w_out.rearrange("(k p) n -> p k n", p=128)
    w_gate_v = w_gate.rearrange("(k p) n -> p k n", p=128)
    w_up_v = w_up.rearrange("(k p) n -> p k n", p=128)
    w_down_v = w_down.rearrange("(k p) n -> p k n", p=128)

    def sb(name, shape, dt):
        return nc.alloc_sbuf_tensor(name, shape, dt)

    xg = sb("xg", [128, 4, 2048], F32)
    hT = sb("hT", [128, 16, 512], BF16)
    big = sb("big", [128, 24, 512], BF16)
    KTd = sb("KTd", [128, 8, 2048], BF16)
    Vd = sb("Vd", [128, 16, 8, 130], BF16)
    KTs = sb("KTs", [128, 4, 640], BF16)
    Vs = sb("Vs", [128, 5, 4, 66], BF16)
    wr = sb("wr", [128, 3, 4096], BF16)
    wbc = sb("wbc", [128, 2048], F32)
    xb = sb("xb", [128, 2048], BF16)
    sg = sb("sg", [128, 4, 512], F32)
    pTs = sb("pTs", [128, 2, 2, 4, 128], BF16)
    pT = pTs[:].rearrange("p a b c q -> p (a b c) q")
    ost = sb("ost", [128, 8, 128], F32)
    obf = sb("obf", [128, 1024], BF16)
    ident = sb("ident", [128, 128], BF16)
    maskC4 = sb("maskC4", [128, 4, 128], BF16)
    maskP4 = sb("maskP4", [128, 4, 128], BF16)
    Tt = sb("Tt", [128, 16], F32)
    Tu = sb("Tu", [128, 16], F32)
    varg = sb("varg", [128, 5, 16], F32)
    qbias = sb("qbias", [128, 5, 16], F32)
    vfac = sb("vfac", [128, 5, 16], F32)
    biasd = sb("biasd", [128, 8, 16], F32)
    biass = sb("biass", [128, 16, 2], F32)
    sinkbc = sb("sinkbc", [128, 16], F32)
    sinkp = sb("sinkp", [128, 16], F32)
    sublnbc = sb("sublnbc", [128, 128], F32)
    lam4 = sb("lam4", [128, 4, 64], F32)
    lsm = sb("lsm", [128, 8], F32)
    nlam = sb("nlam", [128, 1], F32)
    nh = sb("nh", [128, 8], F32)
    epsc = sb("epsc", [128, 1], F32)
    ss = sb("ss", [128, 4], F32)
    vv = sb("vv", [128, 4], F32)
    rstd = sb("rstd", [128, 4], F32)
    ssq = sb("ssq", [128, 8], F32)
    vq = sb("vq", [128, 8], F32)
    rq = sb("rq", [128, 8], F32)
    rr = sb("rr", [128, 4], F32)
    den = sb("den", [128, 2, 4], F32)
    rden = sb("rden", [128, 2, 4], F32)

    ps = nc.alloc_psum_tensor("ps", [128, 8, 512], F32)

    B_xg = [Buf(f"xg{t}") for t in range(4)]
    B_hT = [Buf(f"hT{t}") for t in range(4)]
    B_big = [Buf(f"big{c}") for c in range(24)]
    B_KTd = [[Buf(f"KTd{h}_{g}") for g in range(NG)] for h in range(8)]
    B_Vd = [Buf(f"Vd{t}") for t in range(16)]
    B_KTs = [Buf(f"KTs{t}") for t in range(5)]
    B_Vs = [Buf(f"Vs{t}") for t in range(5)]
    B_wr = [Buf(f"wr{i}") for i in range(3)]
    S_wr = [SemC(nc, f"s_wr{i}") for i in range(3)]
    B_wbc = Buf("wbc")
    S_wbc = SemC(nc, "s_wbc")
    B_xb = Buf("xb")
    B_sg = [Buf(f"sg{i}") for i in range(4)]
    B_pT = [Buf(f"pT{i}") for i in range(16)]
    B_pTs = [Buf("pTs0"), Buf("pTs1")]
    B_ost = Buf("ost")
    B_obf = Buf("obf")
    B_junk = Buf("junk")
    B_const = Buf("const")
    B_stat = Buf("stat")
    B_qstat = Buf("qstat")
    B_den = [Buf("den0"), Buf("den1")]
    B_rden = [Buf("rden0"), Buf("rden1")]
    B_vv, B_rstd = Buf("vv"), Buf("rstd")
    B_bk = [Buf(f"bank{b}") for b in range(8)]
    B_ps = B_bk[0:4]
    S_x = [SemC(nc, f"s_x{t}") for t in range(4)]
    S_o = [SemC(nc, f"s_o{t}") for t in range(4)]
    S_set = SemC(nc, "s_set")
    S_set2 = SemC(nc, "s_set2")
    S_set3 = SemC(nc, "s_set3")

    def setup():
        Bi, Bm, Bt, Bn, Bl, Bsk, Bsu = Buf("i"), Buf("m"), Buf("t"), Buf("n"), Buf("l"), Buf("sk"), Buf("su")
        k.op(POOL, lambda: nc.gpsimd.memset(ident[:], 0.0), writes=[Bi])
        k.op(POOL, lambda: nc.gpsimd.affine_select(out=ident[:], in_=ident[:], compare_op=ALU.not_equal,
                                                   fill=1.0, base=0, pattern=[[-1, 128]], channel_multiplier=1),
             reads=[Bi], writes=[Bi])
        k.op(POOL, lambda: nc.gpsimd.memset(maskC4[:], 1.0), writes=[Bm])
        k.op(POOL, lambda: nc.gpsimd.affine_select(out=maskC4[:], in_=maskC4[:], compare_op=ALU.is_ge,
                                                   fill=0.0, base=0, pattern=[[0, 4], [1, 128]],
                                                   channel_multiplier=-1), reads=[Bm], writes=[Bm])
        k.op(POOL, lambda: nc.gpsimd.memset(maskP4[:], 1.0), writes=[Bn])
        k.op(POOL, lambda: nc.gpsimd.affine_select(out=maskP4[:], in_=maskP4[:], compare_op=ALU.is_ge,
                                                   fill=0.0, base=-1, pattern=[[0, 4], [-1, 128]],
                                                   channel_multiplier=1), reads=[Bn], writes=[Bn])
        k.op(POOL, lambda: nc.gpsimd.iota(Tt[:], pattern=[[-128, 16]], base=-64, channel_multiplier=1, allow_small_or_imprecise_dtypes=True),
             writes=[Bt])
        k.op(POOL, lambda: nc.gpsimd.memset(nh[:], -0.5), writes=[Buf("x")])
        k.op(POOL, lambda: nc.gpsimd.memset(epsc[:], EPS), writes=[Buf("x")])
        Bvones = Buf("vones")
        k.op(POOL, lambda: nc.gpsimd.memset(Vd[:, :, :, 128:130], 1.0), writes=[Bvones])
        k.op(POOL, lambda: nc.gpsimd.memset(Vs[:, :, :, 64:66], 1.0), writes=[Buf("x")])
        for i, v in enumerate([lq1, lk1, lq2, lk2]):
            k.op(SP, lambda: nc.sync.dma_start(out=lam4[:, i, :], in_=v.partition_broadcast(128)),
                 writes=[Bl], dsem=S_set, waw=False)
        k.op(SP, lambda: nc.sync.dma_start(out=sinkbc[:], in_=sinks.partition_broadcast(128)),
             writes=[Bsk], dsem=S_set2, waw=False)
        k.op(SP, lambda: nc.sync.dma_start(out=sublnbc[:], in_=subln_w.partition_broadcast(128)),
             writes=[Bsu], dsem=S_set3, waw=False)
        Bu, Bva, Bvf = Buf("u"), Buf("va"), Buf("vf")
        k.op(POOL, lambda: nc.gpsimd.iota(Tu[:], pattern=[[128, 16]], base=-1024, channel_multiplier=1,
                                          allow_small_or_imprecise_dtypes=True), writes=[Bu])
        for hh in range(5):
            k.op(DVE, lambda: nc.vector.tensor_scalar(out=varg[:, hh, :], in0=Tu[:], scalar1=SLOPE_D[3 + hh], scalar2=None,
                                                      op0=ALU.mult), reads=[Bu], writes=[Bva], waw=False)
        k.op(ACT, lambda: nc.scalar.activation(out=vfac[:], in_=varg[:], func=AF.Exp), reads=[Bva], writes=[Bvf])
        Bq = Buf("q")
        k.op(POOL, lambda: nc.gpsimd.iota(qbias[:, 0, :], pattern=[[128, 16]], base=64 - 1024, channel_multiplier=0,
                                          allow_small_or_imprecise_dtypes=True), writes=[Bq])
        for hh in range(1, 5):
            k.op(POOL, lambda: nc.gpsimd.tensor_scalar(out=qbias[:, hh, :], in0=qbias[:, 0, :], scalar1=-SLOPE_D[3 + hh],
                                                       scalar2=None, op0=ALU.mult), reads=[Bq], writes=[Buf("x")])
        k.op(POOL, lambda: nc.gpsimd.tensor_scalar(out=qbias[:, 0, :], in0=qbias[:, 0, :], scalar1=-SLOPE_D[3],
                                                   scalar2=None, op0=ALU.mult), reads=[Bq], writes=[Bq])
        for hh in range(5):
            k.op(DVE, lambda: nc.vector.tensor_copy(out=Vd[:, :, 3 + hh, 128:130],
                                                    in_=vfac[:, hh, :].unsqueeze(2).broadcast_to([128, 16, 2])),
                 reads=[Bvf, Bvones], writes=[Buf("x")])
        for h in range(8):
            k.op(DVE, lambda: nc.vector.tensor_scalar(out=biasd[:, h, :], in0=Tt[:], scalar1=SLOPE_D[h],
                                                      scalar2=None, op0=ALU.mult), reads=[Bt], writes=[Buf("x")])
        for h in range(16):
            k.op(DVE, lambda: nc.vector.tensor_scalar(out=biass[:, h, :], in0=Tt[:, 0:2], scalar1=SLOPE_S[h],
                                                      scalar2=None, op0=ALU.mult), reads=[Bt], writes=[Buf("x")])
        B_l2, B_l3, B_l4, B_l5 = Buf("l2"), Buf("l3"), Buf("l4"), Buf("l5")
        k.op(DVE, lambda: nc.vector.tensor_tensor(out=lam4[:, 0, :], in0=lam4[:, 0, :], in1=lam4[:, 1, :], op=ALU.mult),
             reads=[Bl], writes=[B_l2], waw=False)
        k.op(DVE, lambda: nc.vector.tensor_tensor(out=lam4[:, 2, :], in0=lam4[:, 2, :], in1=lam4[:, 3, :], op=ALU.mult),
             reads=[Bl], writes=[B_l2], waw=False)
        k.op(DVE, lambda: nc.vector.tensor_reduce(out=lsm[:, 0:1], in_=lam4[:, 0, :], axis=AX.X, op=ALU.add),
             reads=[B_l2], writes=[B_l3], waw=False)
        k.op(DVE, lambda: nc.vector.tensor_reduce(out=lsm[:, 1:2], in_=lam4[:, 2, :], axis=AX.X, op=ALU.add),
             reads=[B_l2], writes=[B_l3], waw=False)
        k.op(ACT, lambda: nc.scalar.activation(out=lsm[:, 2:4], in_=lsm[:, 0:2], func=AF.Exp),
             reads=[B_l3], writes=[B_l4])
        k.op(DVE, lambda: nc.vector.tensor_tensor(out=lsm[:, 4:5], in0=lsm[:, 3:4], in1=lsm[:, 2:3], op=ALU.subtract),
             reads=[B_l4], writes=[B_l5])
        k.op(DVE, lambda: nc.vector.tensor_scalar(out=nlam[:], in0=lsm[:, 4:5], scalar1=-LAMBDA_INIT, scalar2=None,
                                                  op0=ALU.add), reads=[B_l5], writes=[Buf("x")])
        k.op(DVE, lambda: nc.vector.tensor_scalar(out=sublnbc[:], in0=sublnbc[:], scalar1=1.0 - LAMBDA_INIT,
                                                  scalar2=None, op0=ALU.mult), reads=[Bsu], writes=[Bsu])
        for h in range(16):
            k.op(ACT, lambda: nc.scalar.activation(out=sinkp[:, h:h + 1], in_=Tt[:, 0:1], func=AF.Exp,
                                                   scale=SLOPE_S[h], bias=sinkbc[:, h:h + 1]),
                 reads=[Bt, Bsk], writes=[Buf("x")])
        engs = [PE, ACT, DVE, POOL, SP]
        for E in engs:
            for Fe in engs:
                if Fe is not E and Fe.sem.cnt > 0:
                    k.wait(E, Fe.sem, Fe.sem.cnt)
            for ss_ in (S_set, S_set2, S_set3):
                k.wait(E, ss_, ss_.cnt)

    ring_state = {"i": 0, "g": 0, "pid": 0}
    NPIECE = 95
    wcache = nc.dram_tensor("wcache", [NPIECE, 128, 4096], BF16, kind="Internal").ap()
    B_wc = [Buf(f"wc{i}") for i in range(NPIECE)]
    S_ws = [SemC(nc, f"s_ws{i}") for i in range(3)]
    S_wrh = [SemC(nc, f"s_wrh{i}") for i in range(3)]

    def cache_group(pid):
        return 0 if pid % 3 == 0 else 1

    def ring_next():
        i = ring_state["i"] % 3
        ring_state["i"] += 1
        pid = ring_state["pid"]
        ring_state["pid"] += 1
        return i, pid

    def cache_store(i, pid, nel):
        k.op(SP, lambda: nc.sync.dma_start(out=wcache[pid, :, 0:nel], in_=wr[:, i, 0:nel]),
             reads=[B_wr[i]], writes=[B_wc[pid]], dsem=S_ws[i])

    def cache_load(i, pid, nel):
        k.op(SP, lambda: nc.sync.dma_start(out=wr[:, i, 0:nel], in_=wcache[pid, :, 0:nel]),
             reads=[B_wc[pid]], writes=[B_wr[i]], dsem=S_wrh[i])

    def load_piece(src_view, k0, k1, c0, c1):
        i, pid = ring_next()
        kk = k1 - k0
        n = c1 - c0
        assert kk * n <= 4096
        dst = wr[:, i, 0:kk * n].rearrange("p (k n) -> p k n", k=kk)
        gfill = cache_group(pid)
        if ring_state["g"] <= gfill:
            k.op(POOL, lambda: nc.gpsimd.dma_start(out=dst, in_=src_view[:, k0:k1, c0:c1]),
                 writes=[B_wr[i]], dsem=S_wr[i])
            if ring_state["g"] == gfill:
                cache_store(i, pid, kk * n)
        else:
            cache_load(i, pid, kk * n)
        return dst, B_wr[i]

    bank_state = {"i": 0}

    def next_bank():
        b = bank_state["i"] % 4
        bank_state["i"] += 1
        return b

    evac_state = {"i": 0}

    def evac_copy(out_ap, in_ap, reads, writes, waw=True, eng=None):
        evac_state["i"] += 1
        if (evac_state["i"] % 2 == 0 and eng is None) or eng == "act":
            return k.op(ACT, lambda: nc.scalar.activation(out=out_ap, in_=in_ap, func=AF.Copy),
                        reads=reads, writes=writes, waw=waw)
        return k.op(DVE, lambda: nc.vector.tensor_copy(out=out_ap, in_=in_ap), reads=reads, writes=writes, waw=waw)

    def load_wbc(vec):
        k.op(SP, lambda: nc.sync.dma_start(out=wbc[:], in_=vec.partition_broadcast(128)), writes=[B_wbc], dsem=S_wbc)

    B_statt = [Buf(f"stat{t}") for t in range(4)]
    B_vvt = [Buf(f"vv{t}") for t in range(4)]
    B_rstdt = [Buf(f"rstd{t}") for t in range(4)]
    B_xbh = [Buf("xbh0"), Buf("xbh1")]

    def stats_tile(t):
        k.op(ACT, lambda: nc.scalar.activation(out=xb[:], in_=xg[:, t, :], func=AF.Square, accum_out=ss[:, t:t + 1]),
             reads=[B_xg[t]], writes=B_xbh + [B_statt[t]])
        k.op(ACT, lambda: nc.scalar.activation(out=vv[:, t:t + 1], in_=ss[:, t:t + 1], func=AF.Sqrt, scale=1.0 / D,
                                               bias=epsc[:]), reads=[B_statt[t]], writes=[B_vvt[t]])
        k.op(DVE, lambda: nc.vector.reciprocal(out=rstd[:, t:t + 1], in_=vv[:, t:t + 1]),
             reads=[B_vvt[t]], writes=[B_rstdt[t]])

    def norm_to_hT():
        for t in range(4):
            stats_tile(t)
        for t in range(4):
            for r in range(2):
                cs = slice(r * 1024, (r + 1) * 1024)
                k.op(DVE, lambda: nc.vector.scalar_tensor_tensor(out=xb[:, cs], in0=xg[:, t, cs], scalar=rstd[:, t:t + 1],
                                                                 in1=wbc[:, cs], op0=ALU.mult, op1=ALU.mult),
                     reads=[B_xg[t], B_rstdt[t], B_wbc], writes=[B_xbh[r]])
                b = next_bank()
                tp = ps[:, b, :].bitcast(BF16).rearrange("p (c n) -> p c n", c=8)
                for c in range(8):
                    dc = r * 8 + c
                    k.op(PE, lambda: nc.tensor.transpose(out=tp[:, c, :], in_=xb[:, dc * 128:(dc + 1) * 128],
                                                         identity=ident[:]),
                         reads=[B_xbh[r]], writes=[B_ps[b]], signal=(c == 7))
                evac_copy(hT[:, r * 8:(r + 1) * 8, t * 128:(t + 1) * 128], tp, reads=[B_ps[b]], writes=[B_hT[t]], waw=False)

    ss2 = sb("ss2", [128, 4], F32)
    vv2 = sb("vv2", [128, 4], F32)
    rstd2 = sb("rstd2", [128, 4], F32)
    B_e1 = [Buf(f"e1_{t}") for t in range(4)]
    B_e2 = [Buf(f"e2_{t}") for t in range(4)]
    B_e3 = [Buf(f"e3_{t}") for t in range(4)]
    S_xe = SemC(nc, "s_xe")
    en_state = {"i": 0}

    def early_norm_dma(gn, t):
        sgx = sg[:].rearrange("p a n -> p (a n)")
        row0 = (gn * 4 + t) * 128
        k.op(ACT, lambda: nc.scalar.dma_start(out=sgx, in_=x[row0:row0 + 128, :]), writes=B_sg, dsem=S_xe)

    def early_norm_chain(gn, t):
        sgx = sg[:].rearrange("p a n -> p (a n)")
        junk16 = ost[:].rearrange("p a e -> p (a e)").bitcast(BF16)
        row0 = (gn * 4 + t) * 128
        k.op(ACT, lambda: nc.scalar.activation(out=junk16, in_=sgx, func=AF.Square, accum_out=ss2[:, t:t + 1]),
             reads=B_sg, writes=B_osth[0] + [B_e1[t]])
        k.op(ACT, lambda: nc.scalar.activation(out=vv2[:, t:t + 1], in_=ss2[:, t:t + 1], func=AF.Sqrt, scale=1.0 / D,
                                               bias=epsc[:]), reads=[B_e1[t]], writes=[B_e2[t]])
        k.op(DVE, lambda: nc.vector.reciprocal(out=rstd2[:, t:t + 1], in_=vv2[:, t:t + 1]),
             reads=[B_e2[t]], writes=[B_e3[t]])
        for r in range(2):
            cs = slice(r * 1024, (r + 1) * 1024)
            k.op(DVE, lambda: nc.vector.scalar_tensor_tensor(out=xb[:, cs], in0=sgx[:, cs], scalar=rstd2[:, t:t + 1],
                                                             in1=wbc[:, cs], op0=ALU.mult, op1=ALU.mult),
                 reads=B_sg + [B_e3[t], B_wbc], writes=[B_xbh[r]])

    def early_norm_transposes(t):
        for r in range(2):
            b = 4 + en_state["i"] % 2
            en_state["i"] += 1
            tp = ps[:, b, :].bitcast(BF16).rearrange("p (c n) -> p c n", c=8)
            for c in range(8):
                dc = r * 8 + c
                k.op(PE, lambda: nc.tensor.transpose(out=tp[:, c, :], in_=xb[:, dc * 128:(dc + 1) * 128], identity=ident[:]),
                     reads=[B_xbh[r]], writes=[B_bk[b]], signal=(c == 7))
            evac_copy(hT[:, r * 8:(r + 1) * 8, t * 128:(t + 1) * 128], tp, reads=[B_bk[b]], writes=[B_hT[t]], waw=False)

    def proj_feature_major(src_view, col0, nchunks, dest_fn, dest_bufs_fn):
        for s0 in range(0, nchunks, 4):
            nch = min(4, nchunks - s0)
            ncols = nch * 128
            kper = 4096 // ncols
            for k0 in range(0, 16, kper):
                piece, bw = load_piece(src_view, k0, k0 + kper, col0 + s0 * 128, col0 + s0 * 128 + ncols)
                for c in range(nch):
                    for kk in range(kper):
                        dc = k0 + kk
                        k.op(PE, lambda: nc.tensor.matmul(
                            out=ps[:, c, :], lhsT=piece[:, kk, c * 128:(c + 1) * 128], rhs=hT[:, dc, :],
                            start=(dc == 0), stop=(dc == 15)),
                             reads=[bw] + B_hT, writes=[B_ps[c]], signal=(kk == kper - 1))
            for c in range(nch):
                evac_copy(dest_fn(s0 + c), ps[:, c, :], reads=[B_ps[c]], writes=dest_bufs_fn(s0 + c))

    def proj_token_major(src_view, kchunks, col0, ncols, lhs_fn, lhs_bufs_fn, evac_fn):
        kper = 4096 // ncols
        pieces = [(a, min(a + kper, kchunks)) for a in range(0, kchunks, kper)]
        for (k0, k1) in pieces:
            piece, bw = load_piece(src_view, k0, k1, col0, col0 + ncols)
            for t in range(4):
                for kk in range(k0, k1):
                    last = (kk == kchunks - 1)
                    k.op(PE, lambda t=t, kk=kk, k0=k0, piece=piece: nc.tensor.matmul(
                        out=ps[:, t, 0:ncols], lhsT=lhs_fn(kk, t), rhs=piece[:, kk - k0, :],
                        start=(kk == 0), stop=(kk == kchunks - 1)),
                         reads=[bw] + lhs_bufs_fn(kk, t), writes=[B_ps[t]], signal=(kk == k1 - 1))
        for t in range(4):
            evac_fn(t)

    pending_io = []

    def group(g):
        T0 = g * 4
        ring_state["g"] = g
        ring_state["pid"] = 0
        def load_x():
            for t in range(4):
                k.op(SP, lambda t=t: nc.sync.dma_start(out=xg[:, t, :], in_=x[(T0 + t) * 128:(T0 + t + 1) * 128, :]),
                     writes=[B_xg[t]], dsem=S_x[t])

        if g == 0:
            norm_to_hT()
            load_wbc(ffn_norm_w)

        proj_feature_major(w_in_v, 0, 8, lambda c: big[:, c, :], lambda c: [B_big[c]])
        proj_feature_major(w_in_v, 1024, 8, lambda c: KTd[:, c, g * GT:(g + 1) * GT], lambda c: [B_KTd[c][g]])
        def evac_v(t, vb):
            tile_ = T0 + t
            pv = ps[:, t, :].rearrange("p (h e) -> p h e", h=4)
            if vb == 0:
                k.op(DVE, lambda: nc.vector.tensor_copy(out=Vd[:, tile_, 0:3, 0:128], in_=pv[:, 0:3, :]),
                     reads=[B_ps[t]], writes=[B_Vd[tile_]], waw=False)
                k.op(DVE, lambda: nc.vector.tensor_scalar(out=Vd[:, tile_, 3, 0:128], in0=pv[:, 3, :],
                                                          scalar1=vfac[:, 0, tile_:tile_ + 1], scalar2=None, op0=ALU.mult),
                     reads=[B_ps[t]], writes=[B_Vd[tile_]], waw=False)
            else:
                k.op(DVE, lambda: nc.vector.tensor_tensor(
                    out=Vd[:, tile_, 4:8, 0:128], in0=pv,
                    in1=vfac[:, 1:5, tile_].unsqueeze(2).broadcast_to([128, 4, 128]), op=ALU.mult),
                     reads=[B_ps[t]], writes=[B_Vd[tile_]], waw=False)

        for vb in range(2):
            proj_token_major(
                w_in_v, 16, 2048 + vb * 512, 512,
                lambda kk, t: hT[:, kk, t * 128:(t + 1) * 128], lambda kk, t: [B_hT[t]],
                lambda t, vb=vb: evac_v(t, vb))
        if DEBUG_STOP == "qkv":
            return
        if g > 0:
            for f_ in pending_io:
                f_()
            del pending_io[:]
            load_x()
            load_wbc(ffn_norm_w)
        diff_attention(g)
        if DEBUG_STOP == "dattn":
            return
        if g > 0:
            k.op(DVE, lambda: nc.vector.tensor_copy(out=KTs[:, :, 0:128], in_=KTs[:, :, 512:640]),
                 reads=[B_KTs[4]], writes=[B_KTs[0]])
            k.op(DVE, lambda: nc.vector.tensor_copy(out=Vs[:, 0, :, :], in_=Vs[:, 4, :, :]),
                 reads=[B_Vs[4]], writes=[B_Vs[0]])
        proj_feature_major(w_in_v, 3072, 8, lambda c: big[:, c, :], lambda c: [B_big[c]])
        for kh in range(2):
            i, pid = ring_next()
            dst = wr[:, i, :].rearrange("p (k h u e) -> p k h u e", k=8, h=4, u=2)
            srcv = w_in_v[:, kh * 8:kh * 8 + 8, 4096:4352].rearrange("p k (h e) -> p k h e", h=4)
            if g <= cache_group(pid):
                for u in range(2):
                    for hh in range(4):
                        k.op(POOL, lambda: nc.gpsimd.dma_start(out=dst[:, :, hh, u, :], in_=srcv[:, :, hh, :]),
                             writes=[B_wr[i]], dsem=S_wr[i], waw=(u == 0 and hh == 0))
                if g == cache_group(pid):
                    cache_store(i, pid, 4096)
            else:
                cache_load(i, pid, 4096)
            piece = wr[:, i, :].rearrange("p (k n) -> p k n", k=8)
            for kvh in range(4):
                for kk in range(8):
                    dc = kh * 8 + kk
                    k.op(PE, lambda: nc.tensor.matmul(out=ps[:, kvh, :], lhsT=piece[:, kk, kvh * 128:(kvh + 1) * 128],
                                                      rhs=hT[:, dc, :], start=(dc == 0), stop=(dc == 15)),
                         reads=[B_wr[i]] + B_hT, writes=[B_ps[kvh]], signal=(kk == 7))
        for kvh in range(4):
            evac_copy(KTs[:, kvh, 128:640], ps[:, kvh, :], reads=[B_ps[kvh]], writes=B_KTs[1:5], waw=False)
        proj_token_major(
            w_in_v, 16, 4352, 256,
            lambda kk, t: hT[:, kk, t * 128:(t + 1) * 128], lambda kk, t: [B_hT[t]],
            lambda t: evac_copy(Vs[:, 1 + t, :, 0:64], ps[:, t, 0:256].rearrange("p (h e) -> p h e", h=4),
                                reads=[B_ps[t]], writes=[B_Vs[1 + t]], waw=False))
        if DEBUG_STOP == "sproj":
            return
        swa_attention(g)
        if DEBUG_STOP == "attn":
            return
        for db in range(4):
            proj_token_major(
                w_out_v, 16, db * 512, 512,
                lambda kk, t: big[:, 8 + kk, t * 128:(t + 1) * 128], lambda kk, t: [B_big[8 + kk]],
                lambda t, db=db: k.op(DVE, lambda: nc.vector.tensor_tensor(
                    out=xg[:, t, db * 512:(db + 1) * 512], in0=ps[:, t, :], in1=xg[:, t, db * 512:(db + 1) * 512],
                    op=ALU.add), reads=[B_ps[t], B_xg[t]], writes=[B_xg[t]]))
        if DEBUG_STOP == "oproj":
            return
        norm_to_hT()
        for fh in range(2):
            f0 = fh * 22
            for s0 in range(0, 22, 4):
                nch = min(4, 22 - s0)
                ncols = nch * 128
                kper = 4096 // ncols
                c0 = (f0 + s0) * 128
                for (wv, bank0) in ((w_gate_v, 0), (w_up_v, 4)):
                    for k0 in range(0, 16, kper):
                        piece, bw = load_piece(wv, k0, k0 + kper, c0, c0 + ncols)
                        for c in range(nch):
                            for kk in range(kper):
                                dc = k0 + kk
                                k.op(PE, lambda: nc.tensor.matmul(
                                    out=ps[:, bank0 + c, :], lhsT=piece[:, kk, c * 128:(c + 1) * 128], rhs=hT[:, dc, :],
                                    start=(dc == 0), stop=(dc == 15)),
                                     reads=[bw] + B_hT, writes=[B_bk[bank0 + c]], signal=(kk == kper - 1))
                    if bank0 == 0:
                        for c in range(nch):
                            k.op(ACT, lambda: nc.scalar.activation(out=sg[:, c, :], in_=ps[:, c, :], func=AF.Silu),
                                 reads=[B_bk[c]], writes=[B_sg[c]])
                for c in range(nch):
                    k.op(DVE, lambda: nc.vector.tensor_tensor(out=big[:, s0 + c, :], in0=sg[:, c, :], in1=ps[:, 4 + c, :],
                                                              op=ALU.mult),
                         reads=[B_sg[c], B_bk[4 + c]], writes=[B_big[s0 + c]])
            early = (fh == 1 and g + 1 < NG)
            if early:
                load_wbc(attn_norm_w)
                early_norm_dma(g + 1, 0)
            for db in range(4):
                if early:
                    if db > 0:
                        early_norm_transposes(db - 1)
                    early_norm_chain(g + 1, db)
                    if db < 3:
                        early_norm_dma(g + 1, db + 1)
                proj_token_major(
                    w_down_v[:, f0:f0 + 22, :], 22, db * 512, 512,
                    lambda kk, t: big[:, kk, t * 128:(t + 1) * 128], lambda kk, t: [B_big[kk]],
                    lambda t, db=db: k.op(DVE, lambda: nc.vector.tensor_tensor(
                        out=xg[:, t, db * 512:(db + 1) * 512], in0=ps[:, t, :], in1=xg[:, t, db * 512:(db + 1) * 512],
                        op=ALU.add), reads=[B_ps[t], B_xg[t]], writes=[B_xg[t]]))
        if g + 1 < NG:
            early_norm_transposes(3)
        load_wbc(final_norm_w)
        for t in range(4):
            stats_tile(t)
        for t in range(4):
            k.op(DVE, lambda t=t: nc.vector.scalar_tensor_tensor(out=xg[:, t, :], in0=xg[:, t, :], scalar=rstd[:, t:t + 1],
                                                                 in1=wbc[:], op0=ALU.mult, op1=ALU.mult),
                 reads=[B_rstdt[t], B_wbc], writes=[B_xg[t]])
            pending_io.append(lambda t=t: k.op(
                SP, lambda: nc.sync.dma_start(out=out[(T0 + t) * 128:(T0 + t + 1) * 128, :], in_=xg[:, t, :]),
                reads=[B_xg[t]], dsem=S_o[t]))
        if g == NG - 1:
            for f_ in pending_io:
                f_()
            del pending_io[:]

    B_rr, B_rr2 = Buf("rr"), Buf("rr2")
    B_osth = [[Buf(f"ost{q}_{h}") for h in range(8)] for q in range(2)]
    B_ssq, B_vq, B_rq = Buf("ssq"), Buf("vq"), Buf("rq")
    tp_state = {"i": 0}

    def transpose_out(qt, chunk0, bank=None):
        if bank is None:
            b = 2 + tp_state["i"] % 2
            tp_state["i"] += 1
        else:
            b = bank
        tp = ps[:, b, :].bitcast(BF16).rearrange("p (c n) -> p c n", c=8)
        for c in range(8):
            k.op(PE, lambda: nc.tensor.transpose(out=tp[:, c, :], in_=obf[:, c * 128:(c + 1) * 128], identity=ident[:]),
                 reads=[B_obf], writes=[B_ps[b]], signal=(c == 7))
        evac_copy(big[:, chunk0:chunk0 + 8, qt * 128:(qt + 1) * 128], tp, reads=[B_ps[b]],
                  writes=B_big[chunk0:chunk0 + 8], waw=False, eng="dve")

    def diff_attention(g):
        T0 = g * 4
        items = []
        for qt in range(4):
            for h in range(8):
                T = T0 + qt
                kmin = 0
                while kmin < T and SLOPE_D[h] * (128 * (T - kmin - 1) + 1) > 124.0:
                    kmin += 1
                for k0 in range(kmin, T + 1, 4):
                    items.append((qt, h, k0, min(k0 + 4, T + 1), kmin))
        if DEBUG_MAXBLK is not None:
            items = items[:DEBUG_MAXBLK]
        st = {"p": 0, "u": -1}
        ostv = [ost[:], sg[:, 2:4, :].rearrange("p a (b e) -> p (a b) e", e=128)]
        for i_ in range(16):
            for pb_ in B_pTs:
                for s_, v_ in list(pb_.r.items()) + list(pb_.w.items()):
                    _merge(B_pT[i_].r, (s_, v_))
        for h_ in range(8):
            for sgb in B_sg[2:4]:
                for s_, v_ in list(sgb.r.items()) + list(sgb.w.items()):
                    _merge(B_osth[1][h_].r, (s_, v_))

        def emit_scores(idx):
            qt, h, k0, k1, kmin = items[idx]
            p = idx % 3
            for kt in range(k0, k1):
                for m in range(2):
                    bk = 2 + 2 * p + m
                    j = kt - k0
                    k.op(PE, lambda: nc.tensor.matmul(out=ps[:, bk, j * 128:(j + 1) * 128],
                                                      lhsT=KTd[m * 64:(m + 1) * 64, h, kt * 128:(kt + 1) * 128],
                                                      rhs=big[m * 64:(m + 1) * 64, h, qt * 128:(qt + 1) * 128],
                                                      start=True, stop=True),
                         reads=[B_KTd[h][kt // 4], B_big[h]], writes=[B_bk[bk]], signal=True)

        def finish_unit(qt, h, ob):
            o1 = ps[:, ob, 0:130]
            o2 = ps[:, ob, 130:260]
            ov = ostv[qt % 2][:, h, :]
            bo = B_osth[qt % 2][h]
            k.op(DVE, lambda: nc.vector.reciprocal(out=rr[:, 0:1], in_=o1[:, 128:129]), reads=[B_bk[ob]], writes=[B_rr])
            k.op(DVE, lambda: nc.vector.reciprocal(out=rr[:, 1:2], in_=o2[:, 128:129]), reads=[B_bk[ob]], writes=[B_rr],
                 waw=False)
            k.op(DVE, lambda: nc.vector.tensor_tensor(out=rr[:, 2:3], in0=rr[:, 1:2], in1=nlam[:], op=ALU.mult),
                 reads=[B_rr], writes=[B_rr2])
            k.op(DVE, lambda: nc.vector.tensor_scalar(out=ov, in0=o1[:, 0:128], scalar1=rr[:, 0:1], scalar2=None,
                                                      op0=ALU.mult), reads=[B_bk[ob], B_rr], writes=[bo])
            k.op(DVE, lambda: nc.vector.scalar_tensor_tensor(out=ov, in0=o2[:, 0:128], scalar=rr[:, 2:3],
                                                             in1=ov, op0=ALU.mult, op1=ALU.add),
                 reads=[B_bk[ob], B_rr2, bo], writes=[bo])

        def finish_qtile(qt):
            ov = ostv[qt % 2]
            bos = B_osth[qt % 2]
            sqv = sg[:, 0:2, :].rearrange("p a (b e) -> p (a b) e", e=128)
            k.op(DVE, lambda: nc.vector.tensor_tensor(out=sqv, in0=ov, in1=ov, op=ALU.mult),
                 reads=bos, writes=B_sg[0:2])
            k.op(DVE, lambda: nc.vector.tensor_reduce(out=ssq[:], in_=sqv, axis=AX.X, op=ALU.add),
                 reads=B_sg[0:2], writes=[B_ssq])
            k.op(DVE, lambda: nc.vector.tensor_scalar(out=vq[:], in0=ssq[:], scalar1=1.0 / 128, scalar2=EPS,
                                                      op0=ALU.mult, op1=ALU.add), reads=[B_ssq], writes=[B_vq])
            k.op(POOL, lambda: nc.gpsimd.tensor_tensor(out=rq[:], in0=vq[:], in1=nh[:], op=ALU.pow),
                 reads=[B_vq], writes=[B_rq])
            k.op(POOL, lambda: nc.gpsimd.tensor_tensor(out=ov, in0=ov,
                                                       in1=rq[:].unsqueeze(2).broadcast_to([128, 8, 128]), op=ALU.mult),
                 reads=[B_rq] + bos, writes=bos)
            k.op(POOL, lambda: nc.gpsimd.tensor_tensor(out=obf[:].rearrange("p (h e) -> p h e", h=8), in0=ov,
                                                       in1=sublnbc[:].unsqueeze(1).broadcast_to([128, 8, 128]),
                                                       op=ALU.mult),
                 reads=bos, writes=[B_obf])
            def tr_():
                st["u"] += 1
                transpose_out(qt, 8, bank=st["u"] % 2)
            deferred.append([6, tr_])

        def emit_pv(idx):
            qt, h, k0, k1, kmin = items[idx]
            T = T0 + qt
            p = idx % 3
            if k0 == kmin:
                st["u"] += 1
                st["ob"] = st["u"] % 2
            ob = st["ob"]
            wide = h >= 3
            wbase = {}
            if wide:
                nkt = k1 - k0
                for m in range(2):
                    bk = 2 + 2 * p + m
                    if st["p"] % 4:
                        st["p"] += 4 - st["p"] % 4
                    base = st["p"] % 16
                    st["p"] += 4
                    wbase[m] = base
                    k.op(ACT, lambda: nc.scalar.activation(
                        out=pT[:, base:base + nkt, :], in_=ps[:, bk, 0:nkt * 128].rearrange("p (j q) -> p j q", q=128),
                        func=AF.Exp, scale=0.125, bias=qbias[:, h - 3, T:T + 1]),
                         reads=[B_bk[bk]], writes=B_pT[base:base + nkt])
            for kt in range(k0, k1):
                for m in range(2):
                    bk = 2 + 2 * p + m
                    j = kt - k0
                    if wide:
                        pi = wbase[m] + j
                    else:
                        pi = st["p"] % 16
                        st["p"] += 1
                        k.op(ACT, lambda: nc.scalar.activation(out=pT[:, pi, :], in_=ps[:, bk, j * 128:(j + 1) * 128],
                                                               func=AF.Exp, scale=0.125,
                                                               bias=biasd[:, h, T - kt:T - kt + 1]),
                             reads=[B_bk[bk]], writes=[B_pT[pi]])
                    if kt == T:
                        k.op(DVE, lambda: nc.vector.tensor_tensor(out=pT[:, pi, :], in0=pT[:, pi, :], in1=maskC4[:, 0, :],
                                                                  op=ALU.mult), reads=[B_pT[pi]], writes=[B_pT[pi]])
                    k.op(PE, lambda: nc.tensor.matmul(out=ps[:, ob, m * 130:(m + 1) * 130], lhsT=pT[:, pi, :],
                                                      rhs=Vd[:, kt, h, :], start=(kt == kmin and m == 0), stop=(kt == T),
                                                      skip_group_check=True),
                         reads=[B_pT[pi], B_Vd[kt]], writes=[B_bk[ob]], signal=True)
            if k1 == T + 1:
                finish_unit(qt, h, ob)
                if h == 7:
                    finish_qtile(qt)

        n = len(items)
        deferred = []
        for i in range(min(2, n)):
            emit_scores(i)
        for i in range(n):
            if i + 2 < n:
                emit_scores(i + 2)
            emit_pv(i)
            for d_ in list(deferred):
                d_[0] -= 1
                if d_[0] <= 0:
                    deferred.remove(d_)
                    d_[1]()
        for d_ in deferred:
            d_[1]()
        for i_ in range(16):
            for s_, v_ in list(B_pT[i_].r.items()) + list(B_pT[i_].w.items()):
                for pb_ in B_pTs:
                    _merge(pb_.r, (s_, v_))
        for h_ in range(8):
            for s_, v_ in list(B_osth[1][h_].r.items()) + list(B_osth[1][h_].w.items()):
                for sgb in B_sg[2:4]:
                    _merge(sgb.r, (s_, v_))

    HMAP = [0, 2, 1, 3]

    def swa_attention(g):
        T0 = g * 4
        items = [(qt, kvh) for qt in range(4) for kvh in range(4)]

        def sreg(p, cur, half):
            return ps[:, 4 + 2 * p + half, cur * 256:(cur + 1) * 256]

        def emit_scores(idx):
            qt, kvh = items[idx]
            T = T0 + qt
            p = idx % 2
            for cur in ([0, 1] if T >= 1 else [1]):
                kcols = (qt + cur) * 128
                for half in range(2):
                    r0 = half * 64
                    k.op(PE, lambda: nc.tensor.matmul(
                        out=sreg(p, cur, half), lhsT=KTs[r0:r0 + 64, kvh, kcols:kcols + 128],
                        rhs=big[r0:r0 + 64, 2 * kvh:2 * kvh + 2, qt * 128:(qt + 1) * 128], start=True, stop=True),
                         reads=[B_KTs[qt + cur]] + B_big[2 * kvh:2 * kvh + 2], writes=[B_bk[4 + 2 * p + half]], signal=True)

        def emit_pv(idx):
            qt, kvh = items[idx]
            T = T0 + qt
            has_prev = T >= 1
            p = idx % 2
            pb = idx % 2
            ob = idx % 2
            osw = ps[:, ob, 0:264].rearrange("p (h e) -> p h e", h=4)
            for s_ in range(4):
                head = kvh * 4 + HMAP[s_]
                hf, jj = s_ // 2, s_ % 2
                for cur in ([0, 1] if has_prev else [1]):
                    k.op(ACT, lambda: nc.scalar.activation(
                        out=pTs[:, pb, cur, s_, :], in_=sreg(p, cur, hf)[:, jj * 128:(jj + 1) * 128], func=AF.Exp,
                        scale=0.125, bias=biass[:, head, 1 - cur:2 - cur]),
                         reads=[B_bk[4 + 2 * p + hf]], writes=[B_pTs[pb]], waw=False)
            if has_prev:
                k.op(DVE, lambda: nc.vector.tensor_tensor(out=pTs[:, pb, 0, :, :], in0=pTs[:, pb, 0, :, :], in1=maskP4[:],
                                                          op=ALU.mult), reads=[B_pTs[pb]], writes=[B_pTs[pb]])
            k.op(DVE, lambda: nc.vector.tensor_tensor(out=pTs[:, pb, 1, :, :], in0=pTs[:, pb, 1, :, :], in1=maskC4[:],
                                                      op=ALU.mult), reads=[B_pTs[pb]], writes=[B_pTs[pb]])
            first = True
            for s_ in range(4):
                i = HMAP[s_]
                if has_prev:
                    k.op(PE, lambda: nc.tensor.matmul(out=osw[:, i, :], lhsT=pTs[:, pb, 0, s_, :],
                                                      rhs=Vs[:, qt, kvh, :], start=first, stop=False,
                                                      skip_group_check=True),
                         reads=[B_pTs[pb], B_Vs[qt]], writes=[B_bk[ob]], signal=False)
                    first = False
                k.op(PE, lambda: nc.tensor.matmul(out=osw[:, i, :], lhsT=pTs[:, pb, 1, s_, :],
                                                  rhs=Vs[:, qt + 1, kvh, :], start=first, stop=True,
                                                  skip_group_check=True),
                     reads=[B_pTs[pb], B_Vs[qt + 1]], writes=[B_bk[ob]], signal=(s_ == 3))
                first = False
            dd = den[:, ob, :]
            rd = rden[:, ob, :]

            def fin():
                k.op(DVE, lambda: nc.vector.tensor_tensor(out=dd, in0=osw[:, :, 64], in1=sinkp[:, kvh * 4:kvh * 4 + 4],
                                                          op=ALU.add), reads=[B_bk[ob]], writes=[B_den[ob]])
                k.op(DVE, lambda: nc.vector.reciprocal(out=rd, in_=dd), reads=[B_den[ob]], writes=[B_rden[ob]])
                k.op(DVE, lambda: nc.vector.tensor_tensor(
                    out=obf[:, kvh * 256:(kvh + 1) * 256].rearrange("p (h e) -> p h e", h=4), in0=osw[:, :, 0:64],
                    in1=rd.unsqueeze(2).broadcast_to([128, 4, 64]), op=ALU.mult),
                     reads=[B_bk[ob], B_rden[ob]], writes=[B_obf], waw=(kvh == 0))
                if kvh == 3:
                    transpose_out(qt, 16)
            return fin

        n = len(items)
        pending_fin = None
        emit_scores(0)
        for i in range(n):
            if i + 1 < n:
                emit_scores(i + 1)
            fin_ = emit_pv(i)
            if pending_fin is not None:
                pending_fin()
            pending_fin = fin_
        if pending_fin is not None:
            pending_fin()

    for t in range(4):
        k.op(SP, lambda t=t: nc.sync.dma_start(out=xg[:, t, :], in_=x[t * 128:(t + 1) * 128, :]),
             writes=[B_xg[t]], dsem=S_x[t])
    load_wbc(attn_norm_w)
    setup()
    for g in range(NG):
        group(g)
    for t in range(4):
        k.wait(SP, S_o[t], S_o[t].cnt)
    if DEBUG_DUMP:
        tens = {"hT": hT, "big": big, "KTd": KTd, "Vd": Vd, "KTs": KTs, "Vs": Vs, "xg": xg, "biasd": biasd,
                "sinkp": sinkp, "nlam": nlam, "maskC4": maskC4, "maskP4": maskP4, "ident": ident, "rstd": rstd,
                "xb": xb, "ost": ost, "obf": obf}
        for E in [PE, ACT, DVE, POOL]:
            k.wait(SP, E.sem, E.sem.cnt)
        for i_ in range(3):
            k.wait(SP, S_wr[i_], S_wr[i_].cnt)
        S_d = SemC(nc, "s_dbg")
        for name in DEBUG_DUMP:
            tt = tens[name]
            shp = list(tt.shape)
            flat = 1
            for d_ in shp[1:]:
                flat *= d_
            dd_ = nc.dram_tensor("dbg_" + name, [128, flat], tt.dtype, kind="ExternalOutput").ap()
            letters = "abcde"[:len(shp) - 1]
            src = tt[:] if len(shp) == 2 else tt[:].rearrange("p " + " ".join(letters) + " -> p (" + " ".join(letters) + ")")
            k.op(SP, lambda: nc.sync.dma_start(out=dd_, in_=src), dsem=S_d)
        k.wait(SP, S_d, S_d.cnt)
    return nc


_NC_CACHE = {}


def kernel(x, attn_norm_w, w_in, lambda_q1, lambda_k1, lambda_q2, lambda_k2, subln_w, sinks, w_out,
           ffn_norm_w, w_gate, w_up, w_down, final_norm_w):
    f = lambda a: np.ascontiguousarray(np.asarray(a, dtype=np.float32))
    x = f(x)
    shared = {
        "w_in": f(w_in)[0], "w_out": f(w_out)[0], "w_gate": f(w_gate)[0], "w_up": f(w_up)[0], "w_down": f(w_down)[0],
        "attn_norm_w": f(attn_norm_w)[0], "ffn_norm_w": f(ffn_norm_w)[0], "final_norm_w": f(final_norm_w),
        "lambda_q1": f(lambda_q1)[0], "lambda_k1": f(lambda_k1)[0], "lambda_q2": f(lambda_q2)[0],
        "lambda_k2": f(lambda_k2)[0], "subln_w": f(subln_w)[0], "sinks": f(sinks)[0],
    }
    if "nc" not in _NC_CACHE:
        _NC_CACHE["nc"] = build_nc()
    nc = _NC_CACHE["nc"]
    in_maps = [dict(shared, x=x[b]) for b in range(8)]
    res = run_bass_kernel_spmd(nc, in_maps, core_ids=list(range(8)))
    return np.stack([np.asarray(res.results[b]["out"], dtype=np.float32) for b in range(8)], axis=0)
```

```python
import math
import numpy as np
import concourse.bass as bass
import concourse.mybir as mybir
from concourse.bass_utils import run_bass_kernel_spmd

F32 = mybir.dt.float32
BF16 = mybir.dt.bfloat16
AF = mybir.ActivationFunctionType
ALU = mybir.AluOpType
AX = mybir.AxisListType

S = 2048
D = 2048
DFF = 5632
NG = 4
GT = 512
EPS = 1e-5
LAMBDA_INIT = 0.8 - 0.6 * math.exp(-0.3 * 0)
SLOPE_D = [2.0 ** (-8.0 * (h + 1) / 8) for h in range(8)]
SLOPE_S = [2.0 ** (-8.0 * (h + 1) / 16) for h in range(16)]
NEG = -30000.0
LOOKAHEAD = 4
DEBUG_STOP = None
DEBUG_DUMP = []
DEBUG_MAXBLK = None


class SemC:
    def __init__(self, nc, name):
        self.h = nc.alloc_semaphore(name)
        self.cnt = 0


class Eng:
    def __init__(self, nc, e, name, inorder_safe=False):
        self.e = e
        self.name = name
        self.sem = SemC(nc, "prog_" + name)
        self.seen = {}
        self.inorder_safe = inorder_safe


class Buf:
    __slots__ = ("name", "w", "r")

    def __init__(self, name):
        self.name = name
        self.w = {}
        self.r = {}


def _merge(d, tok):
    s, v = tok
    if d.get(s, 0) < v:
        d[s] = v


class K:
    def __init__(self, nc):
        self.nc = nc
        self.PE = Eng(nc, nc.tensor, "pe", inorder_safe=True)
        self.ACT = Eng(nc, nc.scalar, "act")
        self.DVE = Eng(nc, nc.vector, "dve")
        self.POOL = Eng(nc, nc.gpsimd, "pool")
        self.SP = Eng(nc, nc.sync, "sp")
        self.nwait = 0

    def wait(self, E, sem, val):
        if sem is E.sem and E.inorder_safe:
            return
        if E.seen.get(sem, 0) >= val:
            return
        E.e.wait_ge(sem.h, val)
        E.seen[sem] = val
        self.nwait += 1

    def op(self, E, fn, reads=(), writes=(), signal=True, dsem=None, waw=True):
        deps = {}
        for b in reads:
            for s, v in b.w.items():
                _merge(deps, (s, v))
        for b in writes:
            for s, v in b.r.items():
                _merge(deps, (s, v))
            if waw or b.r:
                for s, v in b.w.items():
                    _merge(deps, (s, v))
        for s, v in deps.items():
            self.wait(E, s, v)
        ins = fn()
        if dsem is not None:
            dsem.cnt += 16
            ins.then_inc(dsem.h, 16)
            tok = (dsem, dsem.cnt)
        elif signal:
            E.sem.cnt += 1
            ins.then_inc(E.sem.h, 1)
            tok = (E.sem, E.sem.cnt)
        else:
            tok = (E.sem, E.sem.cnt + 1)
        for b in writes:
            if b.r:
                b.r = {}
                b.w = {}
            _merge(b.w, tok)
        for b in reads:
            _merge(b.r, tok)
        return tok


def build_nc():
    nc = bass.Bass("TRN2", target_bir_lowering=False)
    k = K(nc)
    PE, ACT, DVE, POOL, SP = k.PE, k.ACT, k.DVE, k.POOL, k.SP

    def din(name, shape):
        return nc.dram_tensor(name, shape, F32, kind="ExternalInput").ap()

    x = din("x", [S, D])
    w_in = din("w_in", [D, 4608])
    w_out = din("w_out", [D, D])
    w_gate = din("w_gate", [D, DFF])
    w_up = din("w_up", [D, DFF])
    w_down = din("w_down", [DFF, D])
    attn_norm_w = din("attn_norm_w", [D])
    ffn_norm_w = din("ffn_norm_w", [D])
    final_norm_w = din("final_norm_w", [D])
    lq1 = din("lambda_q1", [64])
    lk1 = din("lambda_k1", [64])
    lq2 = din("lambda_q2", [64])
    lk2 = din("lambda_k2", [64])
    subln_w = din("subln_w", [128])
    sinks = din("sinks", [16])
    out = nc.dram_tensor("out", [S, D], F32, kind="ExternalOutput").ap()

    w_in_v = w_in.rearrange("(k p) n -> p k n", p=128)
    w_out_v = w_out.rearrange("(k p) n -> p k n", p=128)
    w_gate_v = w_gate.rearrange("(k p) n -> p k n", p=128)
    w_up_v = w_up.rearrange("(k p) n -> p k n", p=128)
    w_down_v = w_down.rearrange("(k p) n -> p k n", p=128)

    def sb(name, shape, dt):
        return nc.alloc_sbuf_tensor(name, shape, dt)

    xg = sb("xg", [128, 4, 2048], F32)
    hT = sb("hT", [128, 16, 512], BF16)
    big = sb("big", [128, 24, 512], BF16)
    KTd = sb("KTd", [128, 8, 2048], BF16)
    Vd = sb("Vd", [128, 16, 8, 130], BF16)
    KTs = sb("KTs", [128, 4, 640], BF16)
    Vs = sb("Vs", [128, 5, 4, 66], BF16)
    wr = sb("wr", [128, 3, 4096], BF16)
    wbc = sb("wbc", [128, 2048], F32)
    xb = sb("xb", [128, 2048], BF16)
    sg = sb("sg", [128, 4, 512], F32)
    pTs = sb("pTs", [128, 2, 2, 4, 128], BF16)
    pT = pTs[:].rearrange("p a b c q -> p (a b c) q")
    ost = sb("ost", [128, 8, 128], F32)
    obf = sb("obf", [128, 1024], BF16)
    ident = sb("ident", [128, 128], BF16)
    maskC4 = sb("maskC4", [128, 4, 128], BF16)
    maskP4 = sb("maskP4", [128, 4, 128], BF16)
    Tt = sb("Tt", [128, 16], F32)
    Tu = sb("Tu", [128, 16], F32)
    varg = sb("varg", [128, 5, 16], F32)
    qbias = sb("qbias", [128, 5, 16], F32)
    vfac = sb("vfac", [128, 5, 16], F32)
    biasd = sb("biasd", [128, 8, 16], F32)
    biass = sb("biass", [128, 16, 2], F32)
    sinkbc = sb("sinkbc", [128, 16], F32)
    sinkp = sb("sinkp", [128, 16], F32)
    sublnbc = sb("sublnbc", [128, 128], F32)
    lam4 = sb("lam4", [128, 4, 64], F32)
    lsm = sb("lsm", [128, 8], F32)
    nlam = sb("nlam", [128, 1], F32)
    nh = sb("nh", [128, 8], F32)
    epsc = sb("epsc", [128, 1], F32)
    ss = sb("ss", [128, 4], F32)
    vv = sb("vv", [128, 4], F32)
    rstd = sb("rstd", [128, 4], F32)
    ssq = sb("ssq", [128, 8], F32)
    vq = sb("vq", [128, 8], F32)
    rq = sb("rq", [128, 8], F32)
    rr = sb("rr", [128, 4], F32)
    den = sb("den", [128, 2, 4], F32)
    rden = sb("rden", [128, 2, 4], F32)

    ps = nc.alloc_psum_tensor("ps", [128, 8, 512], F32)

    B_xg = [Buf(f"xg{t}") for t in range(4)]
    B_hT = [Buf(f"hT{t}") for t in range(4)]
    B_big = [Buf(f"big{c}") for c in range(24)]
    B_KTd = [[Buf(f"KTd{h}_{g}") for g in range(NG)] for h in range(8)]
    B_Vd = [Buf(f"Vd{t}") for t in range(16)]
    B_KTs = [Buf(f"KTs{t}") for t in range(5)]
    B_Vs = [Buf(f"Vs{t}") for t in range(5)]
    B_wr = [Buf(f"wr{i}") for i in range(3)]
    S_wr = [SemC(nc, f"s_wr{i}") for i in range(3)]
    B_wbc = Buf("wbc")
    S_wbc = SemC(nc, "s_wbc")
    B_xb = Buf("xb")
    B_sg = [Buf(f"sg{i}") for i in range(4)]
    B_pT = [Buf(f"pT{i}") for i in range(16)]
    B_pTs = [Buf("pTs0"), Buf("pTs1")]
    B_ost = Buf("ost")
    B_obf = Buf("obf")
    B_junk = Buf("junk")
    B_const = Buf("const")
    B_stat = Buf("stat")
    B_qstat = Buf("qstat")
    B_den = [Buf("den0"), Buf("den1")]
    B_rden = [Buf("rden0"), Buf("rden1")]
    B_vv, B_rstd = Buf("vv"), Buf("rstd")
    B_bk = [Buf(f"bank{b}") for b in range(8)]
    B_ps = B_bk[0:4]
    S_x = [SemC(nc, f"s_x{t}") for t in range(4)]
    S_o = [SemC(nc, f"s_o{t}") for t in range(4)]
    S_set = SemC(nc, "s_set")
    S_set2 = SemC(nc, "s_set2")
    S_set3 = SemC(nc, "s_set3")

    def setup():
        Bi, Bm, Bt, Bn, Bl, Bsk, Bsu = Buf("i"), Buf("m"), Buf("t"), Buf("n"), Buf("l"), Buf("sk"), Buf("su")
        k.op(POOL, lambda: nc.gpsimd.memset(ident[:], 0.0), writes=[Bi])
        k.op(POOL, lambda: nc.gpsimd.affine_select(out=ident[:], in_=ident[:], compare_op=ALU.not_equal,
                                                   fill=1.0, base=0, pattern=[[-1, 128]], channel_multiplier=1),
             reads=[Bi], writes=[Bi])
        k.op(POOL, lambda: nc.gpsimd.memset(maskC4[:], 1.0), writes=[Bm])
        k.op(POOL, lambda: nc.gpsimd.affine_select(out=maskC4[:], in_=maskC4[:], compare_op=ALU.is_ge,
                                                   fill=0.0, base=0, pattern=[[0, 4], [1, 128]],
                                                   channel_multiplier=-1), reads=[Bm], writes=[Bm])
        k.op(POOL, lambda: nc.gpsimd.memset(maskP4[:], 1.0), writes=[Bn])
        k.op(POOL, lambda: nc.gpsimd.affine_select(out=maskP4[:], in_=maskP4[:], compare_op=ALU.is_ge,
                                                   fill=0.0, base=-1, pattern=[[0, 4], [-1, 128]],
                                                   channel_multiplier=1), reads=[Bn], writes=[Bn])
        k.op(POOL, lambda: nc.gpsimd.iota(Tt[:], pattern=[[-128, 16]], base=-64, channel_multiplier=1, allow_small_or_imprecise_dtypes=True),
             writes=[Bt])
        k.op(POOL, lambda: nc.gpsimd.memset(nh[:], -0.5), writes=[Buf("x")])
        k.op(POOL, lambda: nc.gpsimd.memset(epsc[:], EPS), writes=[Buf("x")])
        Bvones = Buf("vones")
        k.op(POOL, lambda: nc.gpsimd.memset(Vd[:, :, :, 128:130], 1.0), writes=[Bvones])
        k.op(POOL, lambda: nc.gpsimd.memset(Vs[:, :, :, 64:66], 1.0), writes=[Buf("x")])
        for i, v in enumerate([lq1, lk1, lq2, lk2]):
            k.op(SP, lambda: nc.sync.dma_start(out=lam4[:, i, :], in_=v.partition_broadcast(128)),
                 writes=[Bl], dsem=S_set, waw=False)
        k.op(SP, lambda: nc.sync.dma_start(out=sinkbc[:], in_=sinks.partition_broadcast(128)),
             writes=[Bsk], dsem=S_set2, waw=False)
        k.op(SP, lambda: nc.sync.dma_start(out=sublnbc[:], in_=subln_w.partition_broadcast(128)),
             writes=[Bsu], dsem=S_set3, waw=False)
        Bu, Bva, Bvf = Buf("u"), Buf("va"), Buf("vf")
        k.op(POOL, lambda: nc.gpsimd.iota(Tu[:], pattern=[[128, 16]], base=-1024, channel_multiplier=1,
                                          allow_small_or_imprecise_dtypes=True), writes=[Bu])
        for hh in range(5):
            k.op(DVE, lambda: nc.vector.tensor_scalar(out=varg[:, hh, :], in0=Tu[:], scalar1=SLOPE_D[3 + hh], scalar2=None,
                                                      op0=ALU.mult), reads=[Bu], writes=[Bva], waw=False)
        k.op(ACT, lambda: nc.scalar.activation(out=vfac[:], in_=varg[:], func=AF.Exp), reads=[Bva], writes=[Bvf])
        Bq = Buf("q")
        k.op(POOL, lambda: nc.gpsimd.iota(qbias[:, 0, :], pattern=[[128, 16]], base=64 - 1024, channel_multiplier=0,
                                          allow_small_or_imprecise_dtypes=True), writes=[Bq])
        for hh in range(1, 5):
            k.op(POOL, lambda: nc.gpsimd.tensor_scalar(out=qbias[:, hh, :], in0=qbias[:, 0, :], scalar1=-SLOPE_D[3 + hh],
                                                       scalar2=None, op0=ALU.mult), reads=[Bq], writes=[Buf("x")])
        k.op(POOL, lambda: nc.gpsimd.tensor_scalar(out=qbias[:, 0, :], in0=qbias[:, 0, :], scalar1=-SLOPE_D[3],
                                                   scalar2=None, op0=ALU.mult), reads=[Bq], writes=[Bq])
        for hh in range(5):
            k.op(DVE, lambda: nc.vector.tensor_copy(out=Vd[:, :, 3 + hh, 128:130],
                                                    in_=vfac[:, hh, :].unsqueeze(2).broadcast_to([128, 16, 2])),
                 reads=[Bvf, Bvones], writes=[Buf("x")])
        for h in range(8):
            k.op(DVE, lambda: nc.vector.tensor_scalar(out=biasd[:, h, :], in0=Tt[:], scalar1=SLOPE_D[h],
                                                      scalar2=None, op0=ALU.mult), reads=[Bt], writes=[Buf("x")])
        for h in range(16):
            k.op(DVE, lambda: nc.vector.tensor_scalar(out=biass[:, h, :], in0=Tt[:, 0:2], scalar1=SLOPE_S[h],
                                                      scalar2=None, op0=ALU.mult), reads=[Bt], writes=[Buf("x")])
        B_l2, B_l3, B_l4, B_l5 = Buf("l2"), Buf("l3"), Buf("l4"), Buf("l5")
        k.op(DVE, lambda: nc.vector.tensor_tensor(out=lam4[:, 0, :], in0=lam4[:, 0, :], in1=lam4[:, 1, :], op=ALU.mult),
             reads=[Bl], writes=[B_l2], waw=False)
        k.op(DVE, lambda: nc.vector.tensor_tensor(out=lam4[:, 2, :], in0=lam4[:, 2, :], in1=lam4[:, 3, :], op=ALU.mult),
             reads=[Bl], writes=[B_l2], waw=False)
        k.op(DVE, lambda: nc.vector.tensor_reduce(out=lsm[:, 0:1], in_=lam4[:, 0, :], axis=AX.X, op=ALU.add),
             reads=[B_l2], writes=[B_l3], waw=False)
        k.op(DVE, lambda: nc.vector.tensor_reduce(out=lsm[:, 1:2], in_=lam4[:, 2, :], axis=AX.X, op=ALU.add),
             reads=[B_l2], writes=[B_l3], waw=False)
        k.op(ACT, lambda: nc.scalar.activation(out=lsm[:, 2:4], in_=lsm[:, 0:2], func=AF.Exp),
             reads=[B_l3], writes=[B_l4])
        k.op(DVE, lambda: nc.vector.tensor_tensor(out=lsm[:, 4:5], in0=lsm[:, 3:4], in1=lsm[:, 2:3], op=ALU.subtract),
             reads=[B_l4], writes=[B_l5])
        k.op(DVE, lambda: nc.vector.tensor_scalar(out=nlam[:], in0=lsm[:, 4:5], scalar1=-LAMBDA_INIT, scalar2=None,
                                                  op0=ALU.add), reads=[B_l5], writes=[Buf("x")])
        k.op(DVE, lambda: nc.vector.tensor_scalar(out=sublnbc[:], in0=sublnbc[:], scalar1=1.0 - LAMBDA_INIT,
                                                  scalar2=None, op0=ALU.mult), reads=[Bsu], writes=[Bsu])
        for h in range(16):
            k.op(ACT, lambda: nc.scalar.activation(out=sinkp[:, h:h + 1], in_=Tt[:, 0:1], func=AF.Exp,
                                                   scale=SLOPE_S[h], bias=sinkbc[:, h:h + 1]),
                 reads=[Bt, Bsk], writes=[Buf("x")])
        engs = [PE, ACT, DVE, POOL, SP]
        for E in engs:
            for Fe in engs:
                if Fe is not E and Fe.sem.cnt > 0:
                    k.wait(E, Fe.sem, Fe.sem.cnt)
            for ss_ in (S_set, S_set2, S_set3):
                k.wait(E, ss_, ss_.cnt)

    ring_state = {"i": 0, "g": 0, "pid": 0}
    NPIECE = 95
    wcache = nc.dram_tensor("wcache", [NPIECE, 128, 4096], BF16, kind="Internal").ap()
    B_wc = [Buf(f"wc{i}") for i in range(NPIECE)]
    S_ws = [SemC(nc, f"s_ws{i}") for i in range(3)]
    S_wrh = [SemC(nc, f"s_wrh{i}") for i in range(3)]

    def cache_group(pid):
        return 0 if pid % 3 == 0 else 1

    def ring_next():
        i = ring_state["i"] % 3
        ring_state["i"] += 1
        pid = ring_state["pid"]
        ring_state["pid"] += 1
        return i, pid

    def cache_store(i, pid, nel):
        k.op(SP, lambda: nc.sync.dma_start(out=wcache[pid, :, 0:nel], in_=wr[:, i, 0:nel]),
             reads=[B_wr[i]], writes=[B_wc[pid]], dsem=S_ws[i])

    def cache_load(i, pid, nel):
        k.op(SP, lambda: nc.sync.dma_start(out=wr[:, i, 0:nel], in_=wcache[pid, :, 0:nel]),
             reads=[B_wc[pid]], writes=[B_wr[i]], dsem=S_wrh[i])

    def load_piece(src_view, k0, k1, c0, c1):
        i, pid = ring_next()
        kk = k1 - k0
        n = c1 - c0
        assert kk * n <= 4096
        dst = wr[:, i, 0:kk * n].rearrange("p (k n) -> p k n", k=kk)
        gfill = cache_group(pid)
        if ring_state["g"] <= gfill:
            k.op(POOL, lambda: nc.gpsimd.dma_start(out=dst, in_=src_view[:, k0:k1, c0:c1]),
                 writes=[B_wr[i]], dsem=S_wr[i])
            if ring_state["g"] == gfill:
                cache_store(i, pid, kk * n)
        else:
            cache_load(i, pid, kk * n)
        return dst, B_wr[i]

    bank_state = {"i": 0}

    def next_bank():
        b = bank_state["i"] % 4
        bank_state["i"] += 1
        return b

    evac_state = {"i": 0}

    def evac_copy(out_ap, in_ap, reads, writes, waw=True, eng=None):
        evac_state["i"] += 1
        if (evac_state["i"] % 2 == 0 and eng is None) or eng == "act":
            return k.op(ACT, lambda: nc.scalar.activation(out=out_ap, in_=in_ap, func=AF.Copy),
                        reads=reads, writes=writes, waw=waw)
        return k.op(DVE, lambda: nc.vector.tensor_copy(out=out_ap, in_=in_ap), reads=reads, writes=writes, waw=waw)

    def load_wbc(vec):
        k.op(SP, lambda: nc.sync.dma_start(out=wbc[:], in_=vec.partition_broadcast(128)), writes=[B_wbc], dsem=S_wbc)

    B_statt = [Buf(f"stat{t}") for t in range(4)]
    B_vvt = [Buf(f"vv{t}") for t in range(4)]
    B_rstdt = [Buf(f"rstd{t}") for t in range(4)]
    B_xbh = [Buf("xbh0"), Buf("xbh1")]

    def stats_tile(t):
        k.op(ACT, lambda: nc.scalar.activation(out=xb[:], in_=xg[:, t, :], func=AF.Square, accum_out=ss[:, t:t + 1]),
             reads=[B_xg[t]], writes=B_xbh + [B_statt[t]])
        k.op(ACT, lambda: nc.scalar.activation(out=vv[:, t:t + 1], in_=ss[:, t:t + 1], func=AF.Sqrt, scale=1.0 / D,
                                               bias=epsc[:]), reads=[B_statt[t]], writes=[B_vvt[t]])
        k.op(DVE, lambda: nc.vector.reciprocal(out=rstd[:, t:t + 1], in_=vv[:, t:t + 1]),
             reads=[B_vvt[t]], writes=[B_rstdt[t]])

    def norm_to_hT():
        for t in range(4):
            stats_tile(t)
        for t in range(4):
            for r in range(2):
                cs = slice(r * 1024, (r + 1) * 1024)
                k.op(DVE, lambda: nc.vector.scalar_tensor_tensor(out=xb[:, cs], in0=xg[:, t, cs], scalar=rstd[:, t:t + 1],
                                                                 in1=wbc[:, cs], op0=ALU.mult, op1=ALU.mult),
                     reads=[B_xg[t], B_rstdt[t], B_wbc], writes=[B_xbh[r]])
                b = next_bank()
                tp = ps[:, b, :].bitcast(BF16).rearrange("p (c n) -> p c n", c=8)
                for c in range(8):
                    dc = r * 8 + c
                    k.op(PE, lambda: nc.tensor.transpose(out=tp[:, c, :], in_=xb[:, dc * 128:(dc + 1) * 128],
                                                         identity=ident[:]),
                         reads=[B_xbh[r]], writes=[B_ps[b]], signal=(c == 7))
                evac_copy(hT[:, r * 8:(r + 1) * 8, t * 128:(t + 1) * 128], tp, reads=[B_ps[b]], writes=[B_hT[t]], waw=False)

    ss2 = sb("ss2", [128, 4], F32)
    vv2 = sb("vv2", [128, 4], F32)
    rstd2 = sb("rstd2", [128, 4], F32)
    B_e1 = [Buf(f"e1_{t}") for t in range(4)]
    B_e2 = [Buf(f"e2_{t}") for t in range(4)]
    B_e3 = [Buf(f"e3_{t}") for t in range(4)]
    S_xe = SemC(nc, "s_xe")
    en_state = {"i": 0}

    def early_norm_dma(gn, t):
        sgx = sg[:].rearrange("p a n -> p (a n)")
        row0 = (gn * 4 + t) * 128
        k.op(ACT, lambda: nc.scalar.dma_start(out=sgx, in_=x[row0:row0 + 128, :]), writes=B_sg, dsem=S_xe)

    def early_norm_chain(gn, t):
        sgx = sg[:].rearrange("p a n -> p (a n)")
        junk16 = ost[:].rearrange("p a e -> p (a e)").bitcast(BF16)
        row0 = (gn * 4 + t) * 128
        k.op(ACT, lambda: nc.scalar.activation(out=junk16, in_=sgx, func=AF.Square, accum_out=ss2[:, t:t + 1]),
             reads=B_sg, writes=B_osth[0] + [B_e1[t]])
        k.op(ACT, lambda: nc.scalar.activation(out=vv2[:, t:t + 1], in_=ss2[:, t:t + 1], func=AF.Sqrt, scale=1.0 / D,
                                               bias=epsc[:]), reads=[B_e1[t]], writes=[B_e2[t]])
        k.op(DVE, lambda: nc.vector.reciprocal(out=rstd2[:, t:t + 1], in_=vv2[:, t:t + 1]),
             reads=[B_e2[t]], writes=[B_e3[t]])
        for r in range(2):
            cs = slice(r * 1024, (r + 1) * 1024)
            k.op(DVE, lambda: nc.vector.scalar_tensor_tensor(out=xb[:, cs], in0=sgx[:, cs], scalar=rstd2[:, t:t + 1],
                                                             in1=wbc[:, cs], op0=ALU.mult, op1=ALU.mult),
                 reads=B_sg + [B_e3[t], B_wbc], writes=[B_xbh[r]])

    def early_norm_transposes(t):
        for r in range(2):
            b = 4 + en_state["i"] % 2
            en_state["i"] += 1
            tp = ps[:, b, :].bitcast(BF16).rearrange("p (c n) -> p c n", c=8)
            for c in range(8):
                dc = r * 8 + c
                k.op(PE, lambda: nc.tensor.transpose(out=tp[:, c, :], in_=xb[:, dc * 128:(dc + 1) * 128], identity=ident[:]),
                     reads=[B_xbh[r]], writes=[B_bk[b]], signal=(c == 7))
            evac_copy(hT[:, r * 8:(r + 1) * 8, t * 128:(t + 1) * 128], tp, reads=[B_bk[b]], writes=[B_hT[t]], waw=False)

    def proj_feature_major(src_view, col0, nchunks, dest_fn, dest_bufs_fn):
        for s0 in range(0, nchunks, 4):
            nch = min(4, nchunks - s0)
            ncols = nch * 128
            kper = 4096 // ncols
            for k0 in range(0, 16, kper):
                piece, bw = load_piece(src_view, k0, k0 + kper, col0 + s0 * 128, col0 + s0 * 128 + ncols)
                for c in range(nch):
                    for kk in range(kper):
                        dc = k0 + kk
                        k.op(PE, lambda: nc.tensor.matmul(
                            out=ps[:, c, :], lhsT=piece[:, kk, c * 128:(c + 1) * 128], rhs=hT[:, dc, :],
                            start=(dc == 0), stop=(dc == 15)),
                             reads=[bw] + B_hT, writes=[B_ps[c]], signal=(kk == kper - 1))
            for c in range(nch):
                evac_copy(dest_fn(s0 + c), ps[:, c, :], reads=[B_ps[c]], writes=dest_bufs_fn(s0 + c))

    def proj_token_major(src_view, kchunks, col0, ncols, lhs_fn, lhs_bufs_fn, evac_fn):
        kper = 4096 // ncols
        pieces = [(a, min(a + kper, kchunks)) for a in range(0, kchunks, kper)]
        for (k0, k1) in pieces:
            piece, bw = load_piece(src_view, k0, k1, col0, col0 + ncols)
            for t in range(4):
                for kk in range(k0, k1):
                    last = (kk == kchunks - 1)
                    k.op(PE, lambda t=t, kk=kk, k0=k0, piece=piece: nc.tensor.matmul(
                        out=ps[:, t, 0:ncols], lhsT=lhs_fn(kk, t), rhs=piece[:, kk - k0, :],
                        start=(kk == 0), stop=(kk == kchunks - 1)),
                         reads=[bw] + lhs_bufs_fn(kk, t), writes=[B_ps[t]], signal=(kk == k1 - 1))
        for t in range(4):
            evac_fn(t)

    pending_io = []

    def group(g):
        T0 = g * 4
        ring_state["g"] = g
        ring_state["pid"] = 0
        def load_x():
            for t in range(4):
                k.op(SP, lambda t=t: nc.sync.dma_start(out=xg[:, t, :], in_=x[(T0 + t) * 128:(T0 + t + 1) * 128, :]),
                     writes=[B_xg[t]], dsem=S_x[t])

        if g == 0:
            norm_to_hT()
            load_wbc(ffn_norm_w)

        proj_feature_major(w_in_v, 0, 8, lambda c: big[:, c, :], lambda c: [B_big[c]])
        proj_feature_major(w_in_v, 1024, 8, lambda c: KTd[:, c, g * GT:(g + 1) * GT], lambda c: [B_KTd[c][g]])
        def evac_v(t, vb):
            tile_ = T0 + t
            pv = ps[:, t, :].rearrange("p (h e) -> p h e", h=4)
            if vb == 0:
                k.op(DVE, lambda: nc.vector.tensor_copy(out=Vd[:, tile_, 0:3, 0:128], in_=pv[:, 0:3, :]),
                     reads=[B_ps[t]], writes=[B_Vd[tile_]], waw=False)
                k.op(DVE, lambda: nc.vector.tensor_scalar(out=Vd[:, tile_, 3, 0:128], in0=pv[:, 3, :],
                                                          scalar1=vfac[:, 0, tile_:tile_ + 1], scalar2=None, op0=ALU.mult),
                     reads=[B_ps[t]], writes=[B_Vd[tile_]], waw=False)
            else:
                k.op(DVE, lambda: nc.vector.tensor_tensor(
                    out=Vd[:, tile_, 4:8, 0:128], in0=pv,
                    in1=vfac[:, 1:5, tile_].unsqueeze(2).broadcast_to([128, 4, 128]), op=ALU.mult),
                     reads=[B_ps[t]], writes=[B_Vd[tile_]], waw=False)

        for vb in range(2):
            proj_token_major(
                w_in_v, 16, 2048 + vb * 512, 512,
                lambda kk, t: hT[:, kk, t * 128:(t + 1) * 128], lambda kk, t: [B_hT[t]],
                lambda t, vb=vb: evac_v(t, vb))
        if DEBUG_STOP == "qkv":
            return
        if g > 0:
            for f_ in pending_io:
                f_()
            del pending_io[:]
            load_x()
            load_wbc(ffn_norm_w)
        diff_attention(g)
        if DEBUG_STOP == "dattn":
            return
        if g > 0:
            k.op(DVE, lambda: nc.vector.tensor_copy(out=KTs[:, :, 0:128], in_=KTs[:, :, 512:640]),
                 reads=[B_KTs[4]], writes=[B_KTs[0]])
            k.op(DVE, lambda: nc.vector.tensor_copy(out=Vs[:, 0, :, :], in_=Vs[:, 4, :, :]),
                 reads=[B_Vs[4]], writes=[B_Vs[0]])
        proj_feature_major(w_in_v, 3072, 8, lambda c: big[:, c, :], lambda c: [B_big[c]])
        for kh in range(2):
            i, pid = ring_next()
            dst = wr[:, i, :].rearrange("p (k h u e) -> p k h u e", k=8, h=4, u=2)
            srcv = w_in_v[:, kh * 8:kh * 8 + 8, 4096:4352].rearrange("p k (h e) -> p k h e", h=4)
            if g <= cache_group(pid):
                for u in range(2):
                    for hh in range(4):
                        k.op(POOL, lambda: nc.gpsimd.dma_start(out=dst[:, :, hh, u, :], in_=srcv[:, :, hh, :]),
                             writes=[B_wr[i]], dsem=S_wr[i], waw=(u == 0 and hh == 0))
                if g == cache_group(pid):
                    cache_store(i, pid, 4096)
            else:
                cache_load(i, pid, 4096)
            piece = wr[:, i, :].rearrange("p (k n) -> p k n", k=8)
            for kvh in range(4):
                for kk in range(8):
                    dc = kh * 8 + kk
                    k.op(PE, lambda: nc.tensor.matmul(out=ps[:, kvh, :], lhsT=piece[:, kk, kvh * 128:(kvh + 1) * 128],
                                                      rhs=hT[:, dc, :], start=(dc == 0), stop=(dc == 15)),
                         reads=[B_wr[i]] + B_hT, writes=[B_ps[kvh]], signal=(kk == 7))
        for kvh in range(4):
            evac_copy(KTs[:, kvh, 128:640], ps[:, kvh, :], reads=[B_ps[kvh]], writes=B_KTs[1:5], waw=False)
        proj_token_major(
            w_in_v, 16, 4352, 256,
            lambda kk, t: hT[:, kk, t * 128:(t + 1) * 128], lambda kk, t: [B_hT[t]],
            lambda t: evac_copy(Vs[:, 1 + t, :, 0:64], ps[:, t, 0:256].rearrange("p (h e) -> p h e", h=4),
                                reads=[B_ps[t]], writes=[B_Vs[1 + t]], waw=False))
        if DEBUG_STOP == "sproj":
            return
        swa_attention(g)
        if DEBUG_STOP == "attn":
            return
        for db in range(4):
            proj_token_major(
                w_out_v, 16, db * 512, 512,
                lambda kk, t: big[:, 8 + kk, t * 128:(t + 1) * 128], lambda kk, t: [B_big[8 + kk]],
                lambda t, db=db: k.op(DVE, lambda: nc.vector.tensor_tensor(
                    out=xg[:, t, db * 512:(db + 1) * 512], in0=ps[:, t, :], in1=xg[:, t, db * 512:(db + 1) * 512],
                    op=ALU.add), reads=[B_ps[t], B_xg[t]], writes=[B_xg[t]]))
        if DEBUG_STOP == "oproj":
            return
        norm_to_hT()
        for fh in range(2):
            f0 = fh * 22
            for s0 in range(0, 22, 4):
                nch = min(4, 22 - s0)
                ncols = nch * 128
                kper = 4096 // ncols
                c0 = (f0 + s0) * 128
                for (wv, bank0) in ((w_gate_v, 0), (w_up_v, 4)):
                    for k0 in range(0, 16, kper):
                        piece, bw = load_piece(wv, k0, k0 + kper, c0, c0 + ncols)
                        for c in range(nch):
                            for kk in range(kper):
                                dc = k0 + kk
                                k.op(PE, lambda: nc.tensor.matmul(
                                    out=ps[:, bank0 + c, :], lhsT=piece[:, kk, c * 128:(c + 1) * 128], rhs=hT[:, dc, :],
                                    start=(dc == 0), stop=(dc == 15)),
                                     reads=[bw] + B_hT, writes=[B_bk[bank0 + c]], signal=(kk == kper - 1))
                    if bank0 == 0:
                        for c in range(nch):
                            k.op(ACT, lambda: nc.scalar.activation(out=sg[:, c, :], in_=ps[:, c, :], func=AF.Silu),
                                 reads=[B_bk[c]], writes=[B_sg[c]])
                for c in range(nch):
                    k.op(DVE, lambda: nc.vector.tensor_tensor(out=big[:, s0 + c, :], in0=sg[:, c, :], in1=ps[:, 4 + c, :],
                                                              op=ALU.mult),
                         reads=[B_sg[c], B_bk[4 + c]], writes=[B_big[s0 + c]])
            early = (fh == 1 and g + 1 < NG)
            if early:
                load_wbc(attn_norm_w)
                early_norm_dma(g + 1, 0)
            for db in range(4):
                if early:
                    if db > 0:
                        early_norm_transposes(db - 1)
                    early_norm_chain(g + 1, db)
                    if db < 3:
                        early_norm_dma(g + 1, db + 1)
                proj_token_major(
                    w_down_v[:, f0:f0 + 22, :], 22, db * 512, 512,
                    lambda kk, t: big[:, kk, t * 128:(t + 1) * 128], lambda kk, t: [B_big[kk]],
                    lambda t, db=db: k.op(DVE, lambda: nc.vector.tensor_tensor(
                        out=xg[:, t, db * 512:(db + 1) * 512], in0=ps[:, t, :], in1=xg[:, t, db * 512:(db + 1) * 512],
                        op=ALU.add), reads=[B_ps[t], B_xg[t]], writes=[B_xg[t]]))
        if g + 1 < NG:
            early_norm_transposes(3)
        load_wbc(final_norm_w)
        for t in range(4):
            stats_tile(t)
        for t in range(4):
            k.op(DVE, lambda t=t: nc.vector.scalar_tensor_tensor(out=xg[:, t, :], in0=xg[:, t, :], scalar=rstd[:, t:t + 1],
                                                                 in1=wbc[:], op0=ALU.mult, op1=ALU.mult),
                 reads=[B_rstdt[t], B_wbc], writes=[B_xg[t]])
            pending_io.append(lambda t=t: k.op(
                SP, lambda: nc.sync.dma_start(out=out[(T0 + t) * 128:(T0 + t + 1) * 128, :], in_=xg[:, t, :]),
                reads=[B_xg[t]], dsem=S_o[t]))
        if g == NG - 1:
            for f_ in pending_io:
                f_()
            del pending_io[:]

    B_rr, B_rr2 = Buf("rr"), Buf("rr2")
    B_osth = [[Buf(f"ost{q}_{h}") for h in range(8)] for q in range(2)]
    B_ssq, B_vq, B_rq = Buf("ssq"), Buf("vq"), Buf("rq")
    tp_state = {"i": 0}

    def transpose_out(qt, chunk0, bank=None):
        if bank is None:
            b = 2 + tp_state["i"] % 2
            tp_state["i"] += 1
        else:
            b = bank
        tp = ps[:, b, :].bitcast(BF16).rearrange("p (c n) -> p c n", c=8)
        for c in range(8):
            k.op(PE, lambda: nc.tensor.transpose(out=tp[:, c, :], in_=obf[:, c * 128:(c + 1) * 128], identity=ident[:]),
                 reads=[B_obf], writes=[B_ps[b]], signal=(c == 7))
        evac_copy(big[:, chunk0:chunk0 + 8, qt * 128:(qt + 1) * 128], tp, reads=[B_ps[b]],
                  writes=B_big[chunk0:chunk0 + 8], waw=False, eng="dve")

    def diff_attention(g):
        T0 = g * 4
        items = []
        for qt in range(4):
            for h in range(8):
                T = T0 + qt
                kmin = 0
                while kmin < T and SLOPE_D[h] * (128 * (T - kmin - 1) + 1) > 124.0:
                    kmin += 1
                for k0 in range(kmin, T + 1, 4):
                    items.append((qt, h, k0, min(k0 + 4, T + 1), kmin))
        if DEBUG_MAXBLK is not None:
            items = items[:DEBUG_MAXBLK]
        st = {"p": 0, "u": -1}
        ostv = [ost[:], sg[:, 2:4, :].rearrange("p a (b e) -> p (a b) e", e=128)]
        for i_ in range(16):
            for pb_ in B_pTs:
                for s_, v_ in list(pb_.r.items()) + list(pb_.w.items()):
                    _merge(B_pT[i_].r, (s_, v_))
        for h_ in range(8):
            for sgb in B_sg[2:4]:
                for s_, v_ in list(sgb.r.items()) + list(sgb.w.items()):
                    _merge(B_osth[1][h_].r, (s_, v_))

        def emit_scores(idx):
            qt, h, k0, k1, kmin = items[idx]
            p = idx % 3
            for kt in range(k0, k1):
                for m in range(2):
                    bk = 2 + 2 * p + m
                    j = kt - k0
                    k.op(PE, lambda: nc.tensor.matmul(out=ps[:, bk, j * 128:(j + 1) * 128],
                                                      lhsT=KTd[m * 64:(m + 1) * 64, h, kt * 128:(kt + 1) * 128],
                                                      rhs=big[m * 64:(m + 1) * 64, h, qt * 128:(qt + 1) * 128],
                                                      start=True, stop=True),
                         reads=[B_KTd[h][kt // 4], B_big[h]], writes=[B_bk[bk]], signal=True)

        def finish_unit(qt, h, ob):
            o1 = ps[:, ob, 0:130]
            o2 = ps[:, ob, 130:260]
            ov = ostv[qt % 2][:, h, :]
            bo = B_osth[qt % 2][h]
            k.op(DVE, lambda: nc.vector.reciprocal(out=rr[:, 0:2], in_=ps[:, ob, 128:259:130]), reads=[B_bk[ob]],
                 writes=[B_rr])
            k.op(DVE, lambda: nc.vector.tensor_tensor(out=rr[:, 2:3], in0=rr[:, 1:2], in1=nlam[:], op=ALU.mult),
                 reads=[B_rr], writes=[B_rr2])
            k.op(DVE, lambda: nc.vector.tensor_scalar(out=ov, in0=o1[:, 0:128], scalar1=rr[:, 0:1], scalar2=None,
                                                      op0=ALU.mult), reads=[B_bk[ob], B_rr], writes=[bo])
            k.op(DVE, lambda: nc.vector.scalar_tensor_tensor(out=ov, in0=o2[:, 0:128], scalar=rr[:, 2:3],
                                                             in1=ov, op0=ALU.mult, op1=ALU.add),
                 reads=[B_bk[ob], B_rr2, bo], writes=[bo])

        def finish_qtile(qt):
            ov = ostv[qt % 2]
            bos = B_osth[qt % 2]
            sqv = sg[:, 0:2, :].rearrange("p a (b e) -> p (a b) e", e=128)
            k.op(DVE, lambda: nc.vector.tensor_tensor(out=sqv, in0=ov, in1=ov, op=ALU.mult),
                 reads=bos, writes=B_sg[0:2])
            k.op(DVE, lambda: nc.vector.tensor_reduce(out=ssq[:], in_=sqv, axis=AX.X, op=ALU.add),
                 reads=B_sg[0:2], writes=[B_ssq])
            k.op(DVE, lambda: nc.vector.tensor_scalar(out=vq[:], in0=ssq[:], scalar1=1.0 / 128, scalar2=EPS,
                                                      op0=ALU.mult, op1=ALU.add), reads=[B_ssq], writes=[B_vq])
            k.op(POOL, lambda: nc.gpsimd.tensor_tensor(out=rq[:], in0=vq[:], in1=nh[:], op=ALU.pow),
                 reads=[B_vq], writes=[B_rq])
            k.op(POOL, lambda: nc.gpsimd.tensor_tensor(out=ov, in0=ov,
                                                       in1=rq[:].unsqueeze(2).broadcast_to([128, 8, 128]), op=ALU.mult),
                 reads=[B_rq] + bos, writes=bos)
            k.op(POOL, lambda: nc.gpsimd.tensor_tensor(out=obf[:].rearrange("p (h e) -> p h e", h=8), in0=ov,
                                                       in1=sublnbc[:].unsqueeze(1).broadcast_to([128, 8, 128]),
                                                       op=ALU.mult),
                 reads=bos, writes=[B_obf])
            def tr_():
                st["u"] += 1
                transpose_out(qt, 8, bank=st["u"] % 2)
            deferred.append([6, tr_])

        def emit_pv(idx):
            qt, h, k0, k1, kmin = items[idx]
            T = T0 + qt
            p = idx % 3
            if k0 == kmin:
                st["u"] += 1
                st["ob"] = st["u"] % 2
            ob = st["ob"]
            wide = h >= 3
            wbase = {}
            if wide:
                nkt = k1 - k0
                for m in range(2):
                    bk = 2 + 2 * p + m
                    if st["p"] % 4:
                        st["p"] += 4 - st["p"] % 4
                    base = st["p"] % 16
                    st["p"] += 4
                    wbase[m] = base
                    k.op(ACT, lambda: nc.scalar.activation(
                        out=pT[:, base:base + nkt, :], in_=ps[:, bk, 0:nkt * 128].rearrange("p (j q) -> p j q", q=128),
                        func=AF.Exp, scale=0.125, bias=qbias[:, h - 3, T:T + 1]),
                         reads=[B_bk[bk]], writes=B_pT[base:base + nkt])
            for kt in range(k0, k1):
                for m in range(2):
                    bk = 2 + 2 * p + m
                    j = kt - k0
                    if wide:
                        pi = wbase[m] + j
                    else:
                        pi = st["p"] % 16
                        st["p"] += 1
                        k.op(ACT, lambda: nc.scalar.activation(out=pT[:, pi, :], in_=ps[:, bk, j * 128:(j + 1) * 128],
                                                               func=AF.Exp, scale=0.125,
                                                               bias=biasd[:, h, T - kt:T - kt + 1]),
                             reads=[B_bk[bk]], writes=[B_pT[pi]])
                    if kt == T:
                        k.op(DVE, lambda: nc.vector.tensor_tensor(out=pT[:, pi, :], in0=pT[:, pi, :], in1=maskC4[:, 0, :],
                                                                  op=ALU.mult), reads=[B_pT[pi]], writes=[B_pT[pi]])
                    k.op(PE, lambda: nc.tensor.matmul(out=ps[:, ob, m * 130:(m + 1) * 130], lhsT=pT[:, pi, :],
                                                      rhs=Vd[:, kt, h, :], start=(kt == kmin and m == 0), stop=(kt == T),
                                                      skip_group_check=True),
                         reads=[B_pT[pi], B_Vd[kt]], writes=[B_bk[ob]], signal=True)
            if k1 == T + 1:
                finish_unit(qt, h, ob)
                if h == 7:
                    finish_qtile(qt)

        n = len(items)
        deferred = []
        for i in range(min(2, n)):
            emit_scores(i)
        for i in range(n):
            if i + 2 < n:
                emit_scores(i + 2)
            emit_pv(i)
            for d_ in list(deferred):
                d_[0] -= 1
                if d_[0] <= 0:
                    deferred.remove(d_)
                    d_[1]()
        for d_ in deferred:
            d_[1]()
        for i_ in range(16):
            for s_, v_ in list(B_pT[i_].r.items()) + list(B_pT[i_].w.items()):
                for pb_ in B_pTs:
                    _merge(pb_.r, (s_, v_))
        for h_ in range(8):
            for s_, v_ in list(B_osth[1][h_].r.items()) + list(B_osth[1][h_].w.items()):
                for sgb in B_sg[2:4]:
                    _merge(sgb.r, (s_, v_))

    HMAP = [0, 2, 1, 3]

    def swa_attention(g):
        T0 = g * 4
        items = [(qt, kvh) for qt in range(4) for kvh in range(4)]

        def sreg(p, cur, half):
            return ps[:, 4 + 2 * p + half, cur * 256:(cur + 1) * 256]

        def emit_scores(idx):
            qt, kvh = items[idx]
            T = T0 + qt
            p = idx % 2
            for cur in ([0, 1] if T >= 1 else [1]):
                kcols = (qt + cur) * 128
                for half in range(2):
                    r0 = half * 64
                    k.op(PE, lambda: nc.tensor.matmul(
                        out=sreg(p, cur, half), lhsT=KTs[r0:r0 + 64, kvh, kcols:kcols + 128],
                        rhs=big[r0:r0 + 64, 2 * kvh:2 * kvh + 2, qt * 128:(qt + 1) * 128], start=True, stop=True),
                         reads=[B_KTs[qt + cur]] + B_big[2 * kvh:2 * kvh + 2], writes=[B_bk[4 + 2 * p + half]], signal=True)

        def emit_pv(idx):
            qt, kvh = items[idx]
            T = T0 + qt
            has_prev = T >= 1
            p = idx % 2
            pb = idx % 2
            ob = idx % 2
            osw = ps[:, ob, 0:264].rearrange("p (h e) -> p h e", h=4)
            for s_ in range(4):
                head = kvh * 4 + HMAP[s_]
                hf, jj = s_ // 2, s_ % 2
                for cur in ([0, 1] if has_prev else [1]):
                    k.op(ACT, lambda: nc.scalar.activation(
                        out=pTs[:, pb, cur, s_, :], in_=sreg(p, cur, hf)[:, jj * 128:(jj + 1) * 128], func=AF.Exp,
                        scale=0.125, bias=biass[:, head, 1 - cur:2 - cur]),
                         reads=[B_bk[4 + 2 * p + hf]], writes=[B_pTs[pb]], waw=False)
            if has_prev:
                k.op(DVE, lambda: nc.vector.tensor_tensor(out=pTs[:, pb, 0, :, :], in0=pTs[:, pb, 0, :, :], in1=maskP4[:],
                                                          op=ALU.mult), reads=[B_pTs[pb]], writes=[B_pTs[pb]])
            k.op(DVE, lambda: nc.vector.tensor_tensor(out=pTs[:, pb, 1, :, :], in0=pTs[:, pb, 1, :, :], in1=maskC4[:],
                                                      op=ALU.mult), reads=[B_pTs[pb]], writes=[B_pTs[pb]])
            first = True
            for s_ in range(4):
                i = HMAP[s_]
                if has_prev:
                    k.op(PE, lambda: nc.tensor.matmul(out=osw[:, i, :], lhsT=pTs[:, pb, 0, s_, :],
                                                      rhs=Vs[:, qt, kvh, :], start=first, stop=False,
                                                      skip_group_check=True),
                         reads=[B_pTs[pb], B_Vs[qt]], writes=[B_bk[ob]], signal=False)
                    first = False
                k.op(PE, lambda: nc.tensor.matmul(out=osw[:, i, :], lhsT=pTs[:, pb, 1, s_, :],
                                                  rhs=Vs[:, qt + 1, kvh, :], start=first, stop=True,
                                                  skip_group_check=True),
                     reads=[B_pTs[pb], B_Vs[qt + 1]], writes=[B_bk[ob]], signal=(s_ == 3))
                first = False
            dd = den[:, ob, :]
            rd = rden[:, ob, :]

            def fin():
                k.op(DVE, lambda: nc.vector.tensor_tensor(out=dd, in0=osw[:, :, 64], in1=sinkp[:, kvh * 4:kvh * 4 + 4],
                                                          op=ALU.add), reads=[B_bk[ob]], writes=[B_den[ob]])
                k.op(DVE, lambda: nc.vector.reciprocal(out=rd, in_=dd), reads=[B_den[ob]], writes=[B_rden[ob]])
                k.op(DVE, lambda: nc.vector.tensor_tensor(
                    out=obf[:, kvh * 256:(kvh + 1) * 256].rearrange("p (h e) -> p h e", h=4), in0=osw[:, :, 0:64],
                    in1=rd.unsqueeze(2).broadcast_to([128, 4, 64]), op=ALU.mult),
                     reads=[B_bk[ob], B_rden[ob]], writes=[B_obf], waw=(kvh == 0))
                if kvh == 3:
                    transpose_out(qt, 16)
            return fin

        n = len(items)
        pending_fin = None
        emit_scores(0)
        for i in range(n):
            if i + 1 < n:
                emit_scores(i + 1)
            fin_ = emit_pv(i)
            if pending_fin is not None:
                pending_fin()
            pending_fin = fin_
        if pending_fin is not None:
            pending_fin()

    for t in range(4):
        k.op(SP, lambda t=t: nc.sync.dma_start(out=xg[:, t, :], in_=x[t * 128:(t + 1) * 128, :]),
             writes=[B_xg[t]], dsem=S_x[t])
    load_wbc(attn_norm_w)
    setup()
    for g in range(NG):
        group(g)
    for t in range(4):
        k.wait(SP, S_o[t], S_o[t].cnt)
    if DEBUG_DUMP:
        tens = {"hT": hT, "big": big, "KTd": KTd, "Vd": Vd, "KTs": KTs, "Vs": Vs, "xg": xg, "biasd": biasd,
                "sinkp": sinkp, "nlam": nlam, "maskC4": maskC4, "maskP4": maskP4, "ident": ident, "rstd": rstd,
                "xb": xb, "ost": ost, "obf": obf}
        for E in [PE, ACT, DVE, POOL]:
            k.wait(SP, E.sem, E.sem.cnt)
        for i_ in range(3):
            k.wait(SP, S_wr[i_], S_wr[i_].cnt)
        S_d = SemC(nc, "s_dbg")
        for name in DEBUG_DUMP:
            tt = tens[name]
            shp = list(tt.shape)
            flat = 1
            for d_ in shp[1:]:
                flat *= d_
            dd_ = nc.dram_tensor("dbg_" + name, [128, flat], tt.dtype, kind="ExternalOutput").ap()
            letters = "abcde"[:len(shp) - 1]
            src = tt[:] if len(shp) == 2 else tt[:].rearrange("p " + " ".join(letters) + " -> p (" + " ".join(letters) + ")")
            k.op(SP, lambda: nc.sync.dma_start(out=dd_, in_=src), dsem=S_d)
        k.wait(SP, S_d, S_d.cnt)
    return nc


_NC_CACHE = {}


def kernel(x, attn_norm_w, w_in, lambda_q1, lambda_k1, lambda_q2, lambda_k2, subln_w, sinks, w_out,
           ffn_norm_w, w_gate, w_up, w_down, final_norm_w):
    f = lambda a: np.ascontiguousarray(np.asarray(a, dtype=np.float32))
    x = f(x)
    shared = {
        "w_in": f(w_in)[0], "w_out": f(w_out)[0], "w_gate": f(w_gate)[0], "w_up": f(w_up)[0], "w_down": f(w_down)[0],
        "attn_norm_w": f(attn_norm_w)[0], "ffn_norm_w": f(ffn_norm_w)[0], "final_norm_w": f(final_norm_w),
        "lambda_q1": f(lambda_q1)[0], "lambda_k1": f(lambda_k1)[0], "lambda_q2": f(lambda_q2)[0],
        "lambda_k2": f(lambda_k2)[0], "subln_w": f(subln_w)[0], "sinks": f(sinks)[0],
    }
    if "nc" not in _NC_CACHE:
        _NC_CACHE["nc"] = build_nc()
    nc = _NC_CACHE["nc"]
    in_maps = [dict(shared, x=x[b]) for b in range(8)]
    res = run_bass_kernel_spmd(nc, in_maps, core_ids=list(range(8)))
    return np.stack([np.asarray(res.results[b]["out"], dtype=np.float32) for b in range(8)], axis=0)
```
